# Optimizing a Trainium2 kernel written in Bass

```python
import jax, jax.numpy as jnp
from jax import lax
import numpy as np

D_MODEL = 1024
BATCH = 16
SEQ = 4096
DEPTH = 1
DEC_BATCH = 32
DEC_SEQ = 16
PAST_LEN = 4096

CHUNK = 64
Q_BLOCK = 128
N_HEADS = D_MODEL // 128
QK_NOPE = 64
QK_ROPE = 32
V_DIM = 64
KV_LORA = D_MODEL // 4
Q_LORA = 3 * KV_LORA
CONV_CH = D_MODEL // 2
CONV_W = 31
D_FF = 4 * D_MODEL
PLE_DIM = 256
ROPE_THETA = 10000.0
RMS_EPS = 1e-6
LN_EPS = 1e-5
SM_SCALE = (QK_NOPE + QK_ROPE) ** -0.5
NEG_INF = -1e30

OFF_KV = Q_LORA
OFF_KR = OFF_KV + KV_LORA
OFF_CONV = OFF_KR + QK_ROPE
OFF_GATE = OFF_CONV + 2 * CONV_CH
IN_W = OFF_GATE + 2 * D_MODEL

kernel_name = 'mla_conformer_parallel_stream_encoder_step'


def rmsnorm(x, g):
    xf = x.astype(jnp.float32)
    y = xf * lax.rsqrt(jnp.mean(xf * xf, axis=-1, keepdims=True) + RMS_EPS)
    return (y * g.astype(jnp.float32)).astype(x.dtype)


def layernorm(x, g, b):
    xf = x.astype(jnp.float32)
    mu = jnp.mean(xf, axis=-1, keepdims=True)
    xc = xf - mu
    y = xc * lax.rsqrt(jnp.mean(xc * xc, axis=-1, keepdims=True) + LN_EPS)
    return (y * g.astype(jnp.float32) + b.astype(jnp.float32)).astype(x.dtype)


def rope(x, pos):
    half = x.shape[-1] // 2
    inv = ROPE_THETA ** (-jnp.arange(half, dtype=jnp.float32) / half)
    ang = pos.astype(jnp.float32)[:, None] * inv[None, :]
    cos = jnp.cos(ang)[:, None, :]
    sin = jnp.sin(ang)[:, None, :]
    x1 = x[..., :half].astype(jnp.float32)
    x2 = x[..., half:].astype(jnp.float32)
    out = jnp.concatenate([x1 * cos - x2 * sin, x2 * cos + x1 * sin], axis=-1)
    return out.astype(x.dtype)


def attend(q_lat, q_rope, lat, kr, q_pos, k_pos):
    s = (jnp.einsum('bqhc,bkc->bhqk', q_lat, lat)
         + jnp.einsum('bqhr,bkr->bhqk', q_rope, kr)).astype(jnp.float32) * SM_SCALE
    mask = (k_pos // CHUNK)[None, :] <= (q_pos // CHUNK)[:, None]
    s = jnp.where(mask, s, NEG_INF)
    p = jax.nn.softmax(s, axis=-1).astype(lat.dtype)
    return jnp.einsum('bhqk,bkc->bqhc', p, lat)


def blocked_attend(q_lat, q_rope, lat, kr, q_pos, k_pos):
    B, T, H, C = q_lat.shape
    if T <= Q_BLOCK or T % Q_BLOCK != 0:
        return attend(q_lat, q_rope, lat, kr, q_pos, k_pos)
    nb = T // Q_BLOCK
    ql = q_lat.reshape(B, nb, Q_BLOCK, H, C).transpose(1, 0, 2, 3, 4)
    qr = q_rope.reshape(B, nb, Q_BLOCK, H, QK_ROPE).transpose(1, 0, 2, 3, 4)
    qp = q_pos.reshape(nb, Q_BLOCK)
    out = lax.map(lambda a: attend(a[0], a[1], lat, kr, a[2], k_pos), (ql, qr, qp))
    return out.transpose(1, 0, 2, 3, 4).reshape(B, T, H, C)


def encoder_layer(x, p_emb, q_pos, cache_lat, cache_kr, cache_cv,
                  norm_mix_g, w_in, q_norm_g, w_uq, kv_norm_g, w_uk, w_uv, w_attn_out,
                  conv_w, conv_b, conv_ln_g, conv_ln_b, w_conv_out, w_out,
                  norm_ffn_g, w_ff_up, w_ff_down, ple_norm_g, w_ple_gate, w_ple_proj):
    B, T, _ = x.shape
    h = rmsnorm(x, norm_mix_g)
    z = h @ w_in
    c_q = z[..., :OFF_KV]
    c_kv = z[..., OFF_KV:OFF_KR]
    k_r = z[..., OFF_KR:OFF_CONV]
    u2 = z[..., OFF_CONV:OFF_GATE]
    g = z[..., OFF_GATE:]

    q = jnp.einsum('bsc,chd->bshd', rmsnorm(c_q, q_norm_g), w_uq)
    q_nope = q[..., :QK_NOPE]
    q_rope = rope(q[..., QK_NOPE:], q_pos)
    q_lat = jnp.einsum('bshd,chd->bshc', q_nope, w_uk)
    latent = rmsnorm(c_kv, kv_norm_g)
    k_rope = rope(k_r[:, :, None, :], q_pos)[:, :, 0, :]
    if cache_lat is None:
        lat_all, kr_all, k_pos = latent, k_rope, q_pos
        u_hist = jnp.zeros((B, CONV_W - 1, CONV_CH), x.dtype)
    else:
        lat_all = jnp.concatenate([cache_lat, latent], axis=1)
        kr_all = jnp.concatenate([cache_kr, k_rope], axis=1)
        k_pos = jnp.arange(cache_lat.shape[1] + T)
        u_hist = cache_cv
    o_lat = blocked_attend(q_lat, q_rope, lat_all, kr_all, q_pos, k_pos)
    o = jnp.einsum('bshc,chd->bshd', o_lat, w_uv).reshape(B, T, N_HEADS * V_DIM)
    attn_branch = o @ w_attn_out

    u = u2[..., :CONV_CH] * jax.nn.sigmoid(u2[..., CONV_CH:])
    u_ext = jnp.concatenate([u_hist, u], axis=1)
    dw = lax.conv_general_dilated(u_ext, conv_w[:, None, :], (1,), 'VALID',
                                  dimension_numbers=('NWC', 'WIO', 'NWC'),
                                  feature_group_count=CONV_CH) + conv_b
    conv_branch = jax.nn.silu(layernorm(dw, conv_ln_g, conv_ln_b)) @ w_conv_out

    gate_a = jax.nn.sigmoid(g[..., :D_MODEL])
    gate_c = jax.nn.sigmoid(g[..., D_MODEL:])
    x = x + (gate_a * attn_branch + gate_c * conv_branch) @ w_out

    x = x + jnp.square(jax.nn.relu(rmsnorm(x, norm_ffn_g) @ w_ff_up)) @ w_ff_down

    x = x + jax.nn.sigmoid(rmsnorm(x, ple_norm_g) @ w_ple_gate) * (p_emb @ w_ple_proj)
    return x, latent, k_rope, u_ext[:, -(CONV_W - 1):]


def setup_inputs(seed: int = 0) -> dict:
    key = jax.random.key(seed)
    ks = jax.random.split(key, 32)
    f32 = jnp.float32
    L = DEPTH

    def nrm(k, shape, scale):
        return jax.random.normal(k, shape, f32) * scale

    def gain(k, n):
        return 1.0 + 0.1 * jax.random.normal(k, (L, n), f32)

    return {
        'x_prompt': nrm(ks[0], (BATCH, SEQ, D_MODEL), 1.0),
        'x_sample': nrm(ks[1], (DEC_BATCH, DEC_SEQ, D_MODEL), 1.0),
        'p_prompt': nrm(ks[2], (L, BATCH, SEQ, PLE_DIM), 1.0),
        'p_sample': nrm(ks[3], (L, DEC_BATCH, DEC_SEQ, PLE_DIM), 1.0),
        'cache_kv_latent': nrm(ks[4], (L, DEC_BATCH, PAST_LEN, KV_LORA), 1.0),
        'cache_k_rope': nrm(ks[5], (L, DEC_BATCH, PAST_LEN, QK_ROPE), 1.0),
        'cache_conv': nrm(ks[6], (L, DEC_BATCH, CONV_W - 1, CONV_CH), 0.5),
        'norm_mix_g': gain(ks[7], D_MODEL),
        'w_in': nrm(ks[8], (L, D_MODEL, IN_W), D_MODEL ** -0.5),
        'q_norm_g': gain(ks[9], Q_LORA),
        'w_uq': nrm(ks[10], (L, Q_LORA, N_HEADS, QK_NOPE + QK_ROPE), Q_LORA ** -0.5),
        'kv_norm_g': gain(ks[11], KV_LORA),
        'w_uk': nrm(ks[12], (L, KV_LORA, N_HEADS, QK_NOPE), KV_LORA ** -0.5),
        'w_uv': nrm(ks[13], (L, KV_LORA, N_HEADS, V_DIM), KV_LORA ** -0.5),
        'w_attn_out': nrm(ks[14], (L, N_HEADS * V_DIM, D_MODEL), (N_HEADS * V_DIM) ** -0.5),
        'conv_w': nrm(ks[15], (L, CONV_W, CONV_CH), CONV_W ** -0.5),
        'conv_b': nrm(ks[16], (L, CONV_CH), 0.01),
        'conv_ln_g': gain(ks[17], CONV_CH),
        'conv_ln_b': nrm(ks[18], (L, CONV_CH), 0.01),
        'w_conv_out': nrm(ks[19], (L, CONV_CH, D_MODEL), CONV_CH ** -0.5),
        'w_out': nrm(ks[20], (L, D_MODEL, D_MODEL), D_MODEL ** -0.5),
        'norm_ffn_g': gain(ks[21], D_MODEL),
        'w_ff_up': nrm(ks[22], (L, D_MODEL, D_FF), D_MODEL ** -0.5),
        'w_ff_down': nrm(ks[23], (L, D_FF, D_MODEL), D_FF ** -0.5),
        'ple_norm_g': gain(ks[24], D_MODEL),
        'w_ple_gate': nrm(ks[25], (L, D_MODEL, D_MODEL), D_MODEL ** -0.5),
        'w_ple_proj': nrm(ks[26], (L, PLE_DIM, D_MODEL), PLE_DIM ** -0.5),
        'final_norm_g': 1.0 + 0.1 * jax.random.normal(ks[27], (D_MODEL,), f32),
    }


def reference(x_prompt, x_sample, p_prompt, p_sample, cache_kv_latent, cache_k_rope, cache_conv,
              norm_mix_g, w_in, q_norm_g, w_uq, kv_norm_g, w_uk, w_uv, w_attn_out,
              conv_w, conv_b, conv_ln_g, conv_ln_b, w_conv_out, w_out,
              norm_ffn_g, w_ff_up, w_ff_down, ple_norm_g, w_ple_gate, w_ple_proj, final_norm_g):
    t_prompt = x_prompt.shape[1]
    t_sample = x_sample.shape[1]
    past = cache_kv_latent.shape[2]
    pos_p = jnp.arange(t_prompt)
    pos_s = past + jnp.arange(t_sample)
    xp, xs = x_prompt, x_sample
    lat_p, kr_p, cv_p, lat_s, kr_s, cv_s = [], [], [], [], [], []
    for i in range(DEPTH):
        lw = (norm_mix_g[i], w_in[i], q_norm_g[i], w_uq[i], kv_norm_g[i], w_uk[i], w_uv[i],
              w_attn_out[i], conv_w[i], conv_b[i], conv_ln_g[i], conv_ln_b[i], w_conv_out[i],
              w_out[i], norm_ffn_g[i], w_ff_up[i], w_ff_down[i], ple_norm_g[i], w_ple_gate[i],
              w_ple_proj[i])
        xp, a, b, c = encoder_layer(xp, p_prompt[i], pos_p, None, None, None, *lw)
        lat_p.append(a)
        kr_p.append(b)
        cv_p.append(c)
        xs, a, b, c = encoder_layer(xs, p_sample[i], pos_s, cache_kv_latent[i], cache_k_rope[i],
                                    cache_conv[i], *lw)
        lat_s.append(a)
        kr_s.append(b)
        cv_s.append(c)
    y_prompt = rmsnorm(xp, final_norm_g)
    y_sample = rmsnorm(xs, final_norm_g)
    return (y_prompt, y_sample, jnp.stack(lat_p), jnp.stack(kr_p), jnp.stack(cv_p),
            jnp.stack(lat_s), jnp.stack(kr_s), jnp.stack(cv_s))
```

```python
import numpy as np
import concourse.bass as bass
import concourse.mybir as mybir
from concourse.bass_utils import run_bass_kernel_spmd

F32 = mybir.dt.float32
BF16 = mybir.dt.bfloat16
AF = mybir.ActivationFunctionType
ALU = mybir.AluOpType

D = 1024
NH = 8
KVL = 256
QL = 768
CC = 512
CW = 31
HIST = CW - 1
DFF = 4096
PLE = 256
OFF_KV = 768
OFF_CONV = 1056
OFF_GATE = 2080
IN_W = 4128
SM_SCALE = 96.0 ** -0.5
NCORES = 8


class Tok:
    __slots__ = ("last_w", "readers", "sem")

    def __init__(self):
        self.last_w = None
        self.readers = []
        self.sem = None


class Prog:
    ENGS = ("pe", "act", "dve", "pool", "sp")

    def __init__(self, nc):
        self.nc = nc
        self.ops = {e: [] for e in self.ENGS}
        self.dma_cnt = {}
        self.nsem = 0

    def _tok_sem(self, t):
        if t.sem is None:
            t.sem = self.nc.alloc_semaphore(f"d{self.nsem}")
            self.nsem += 1
            self.dma_cnt[t.sem] = 0
        return t.sem

    def op(self, eng, fn, r=(), w=(), dma=None):
        idx = len(self.ops[eng])
        waits = {}

        def need(dep):
            if dep is not None and waits.get(dep[0], 0) < dep[1]:
                waits[dep[0]] = dep[1]

        for t in r:
            need(t.last_w)
        for t in w:
            need(t.last_w)
            for d in t.readers:
                need(d)
        if dma is not None:
            s = self._tok_sem(dma)
            self.dma_cnt[s] += 16
            mydep = (s, self.dma_cnt[s])
        else:
            mydep = (eng, idx + 1)
        for t in r:
            if len(t.readers) > 64:
                m = {}
                for k, v in t.readers:
                    if m.get(k, 0) < v:
                        m[k] = v
                t.readers = list(m.items())
            t.readers.append(mydep)
        for t in w:
            t.last_w = mydep
            t.readers = []
        self.ops[eng].append((fn, waits, self._tok_sem(dma) if dma is not None else None))

    def emit(self):
        nc = self.nc
        signaled = {e: set() for e in self.ENGS}
        plan = {}
        for e in self.ENGS:
            known = {}
            lst = []
            for (fn, waits, dsem) in self.ops[e]:
                ws = []
                for k, v in waits.items():
                    if k == e and e in ("pe", "sp"):
                        continue
                    if known.get(k, 0) >= v:
                        continue
                    known[k] = v
                    ws.append((k, v))
                    if isinstance(k, str):
                        signaled[k].add(v)
                lst.append(ws)
            plan[e] = lst
        semval = {}
        for e in self.ENGS:
            m = {}
            for c, v in enumerate(sorted(signaled[e])):
                m[v] = c + 1
            semval[e] = m
        esem = {e: nc.alloc_semaphore(f"e_{e}") for e in self.ENGS}
        final_waits = list(self.dma_cnt.items())
        engmap = {"pe": "tensor", "act": "scalar", "dve": "vector", "pool": "gpsimd", "sp": "sync"}
        with nc.Block() as block:
            for e in self.ENGS:
                def body(eng, e=e):
                    for i, (fn, waits, dsem) in enumerate(self.ops[e]):
                        for k, v in plan[e][i]:
                            if isinstance(k, str):
                                eng.wait_ge(esem[k], semval[k][v])
                            else:
                                eng.wait_ge(k, v)
                        ins = fn(eng)
                        if dsem is not None:
                            ins.then_inc(dsem, 16)
                        elif (i + 1) in semval[e]:
                            ins.then_inc(esem[e], 1)
                    if e == "sp":
                        for s, c in final_waits:
                            eng.wait_ge(s, c)
                getattr(block, engmap[e])(body)


class B:
    def __init__(self, t):
        self.t = t
        self.k = Tok()


def build_program(T, NSEQ, NB, PAST, DEC):
    W = 256
    NT = T // W
    NSAMP = NB * DEC
    KMAX = max(T, PAST + DEC)
    NBLK = (KMAX + 127) // 128
    nc = bass.Bass("TRN2", target_bir_lowering=False)
    P = Prog(nc)

    def din(name, shape, dt=F32):
        return nc.dram_tensor(name, list(shape), dt, kind="ExternalInput").ap()

    def dout(name, shape):
        return nc.dram_tensor(name, list(shape), F32, kind="ExternalOutput").ap()

    def dscr(name, shape):
        return nc.dram_tensor(name, list(shape), BF16, kind="Internal").ap()

    xp = din("xp", [NSEQ * T, D])
    xs = din("xs", [NSAMP, D])
    pp = din("pp", [NSEQ * T, PLE])
    psd = din("psd", [NSAMP, PLE])
    clat = din("clat", [NB * PAST, KVL])
    ckr = din("ckr", [NB * PAST, 32])
    ccv = din("ccv", [NB, HIST, CC])
    wdefs = dict(w_in=[D, IN_W], wq_n=[QL, 1024], wq_r=[QL, 512], w_uk=[KVL, 512], w_uv=[KVL, 512],
                 w_ao=[512, D], cdiag=[4 * 128, CW * 128], w_co=[CC, D], w_out=[D, D], w_up=[D, DFF],
                 w_dn=[DFF, D], w_pg=[D, D], w_pp=[PLE, D])
    wf = {k: din(k, v) for k, v in wdefs.items()}
    wb = {k: dscr(k + "_b", v) for k, v in wdefs.items()}
    wtok = {k: Tok() for k in wdefs}
    clat_b = dscr("clat_b", [max(1, NB * PAST), KVL]); ckr_b = dscr("ckr_b", [max(1, NB * PAST), 32])
    ctok = Tok()
    g_mix = din("g_mix", [D]); g_ffn = din("g_ffn", [D]); g_ple = din("g_ple", [D]); g_fin = din("g_fin", [D])
    g_kv = din("g_kv", [KVL])
    vecs = din("vecs", [128, 24])
    cosk_d = din("cosk", [128, (T // 128) * 16]); sink_d = din("sink", [128, (T // 128) * 16])
    cosks_d = din("cosks", [NSAMP, 16]); sinks_d = din("sinks", [NSAMP, 16])
    cos4_d = din("cos4", [128, T]); sin4_d = din("sin4", [128, T])
    cos4s_d = din("cos4s", [128, NSAMP]); sin4s_d = din("sin4s", [128, NSAMP])

    y_p = dout("y_p", [NSEQ * T, D]); y_s = dout("y_s", [NSAMP, D])
    lat_p = dout("lat_p", [NSEQ * T, KVL]); kr_p = dout("kr_p", [NSEQ * T, 32]); cv_p = dout("cv_p", [NSEQ, HIST, CC])
    lat_s = dout("lat_s", [NSAMP, KVL]); kr_s = dout("kr_s", [NSAMP, 32]); cv_s = dout("cv_s", [NB, HIST, CC])

    def sb(name, shape, dt):
        return B(nc.alloc_sbuf_tensor("s_" + name, list(shape), dt))

    Kc = sb("Kc", [128, 4, NBLK * 128], BF16)
    krT = sb("krT", [128, NBLK * 128], BF16)
    Vc = sb("Vc", [128, NBLK, NH, 65], BF16)
    ident = sb("ident", [128, 128], BF16)
    identf = sb("identf", [128, 128], F32)
    onesb = sb("onesb", [128, 128], BF16)
    ones512 = sb("ones512", [128, 128], BF16)
    onesf = sb("onesf", [128, 64], F32)
    eps6 = sb("eps6", [128, 1], F32)
    eps5 = sb("eps5", [128, 1], F32)
    vec = sb("vec", [128, 24], F32)
    cosk = sb("cosk", [128, (T // 128) * 16], F32); sink = sb("sink", [128, (T // 128) * 16], F32)
    cosks = sb("cosks", [128, 16], F32); sinks = sb("sinks", [128, 16], F32)
    gkv = sb("gkv", [128, KVL], F32)
    wuk = sb("wuk", [128, 2, 512], BF16); wuv = sb("wuv", [128, 2, 512], BF16)
    CONST = Tok()
    xsets = [[sb(f"xres{a}_{s}", [128, D], F32) for s in range(2)] for a in range(2)]
    grep = sb("grep", [128, D], F32)
    hbfs = [sb(f"hbf{i}", [128, D], BF16) for i in range(2)]
    HTs = [sb(f"HT{i}", [128, 8, W], BF16) for i in range(2)]

    class _Cur:
        b = None
        t = property(lambda self: self.b.t)
        k = property(lambda self: self.b.k)
    HT = _Cur()
    cqT = sb("cqT", [128, 6, W], BF16)
    sqT = [sb(f"sqT{i}", [128, W], BF16) for i in range(2)]
    rq = sb("rq", [128, W], F32)
    cos4 = sb("cos4t", [128, W], F32); sin4 = sb("sin4t", [128, W], F32)
    latst = [sb(f"latst{i}", [128, 288], F32) for i in range(2)]
    latbf = [sb(f"latbf{i}", [128, 416], BF16) for i in range(2)]
    ropet = sb("ropet", [128, 64], F32)
    latT = sb("latT", [128, 2, W], BF16)
    krTs = sb("krTs", [128, 64], BF16)
    QT = sb("QT", [128, NH, W], BF16)
    QrT = sb("QrT", [128, NH, W], BF16)
    PT = [sb(f"PT{i}", [128, 2 * W], BF16) for i in range(3)]
    oT = sb("oT", [128, NH, W], BF16)
    rdn = sb("rdn", [128, W], F32)
    rrep = sb("rrep", [128, W], F32)
    USEG = HIST + 16
    ubuf = sb("ubuf", [128, 4, HIST + W], BF16)
    dwbf = sb("dwbf", [128, 4, W], BF16)
    dsq = sb("dsq", [128, 4, W], BF16)
    mean = sb("mean", [128, W], F32); var = sb("var", [128, W], F32); lrs = sb("lrs", [128, W], F32)
    tmp = [sb(f"tmp{i}", [128, 512], F32) for i in range(4)]
    hidT = [sb(f"hidT{i}", [128, 8, W], BF16) for i in range(2)]
    sT = cqT
    mrgT = hidT[0]
    pbf = sb("pbf", [128, 2, PLE], BF16)
    ppT = sb("ppT", [128, 2, W], BF16)
    ss = sb("ss", [128, 32], F32)
    rs = sb("rs", [128, 32], F32)
    NSLOT = 4
    wslot = [sb(f"wslot{i}", [128, 4096], BF16) for i in range(NSLOT)]
    pTb = [B(nc.alloc_psum_tensor(f"pT{i}", [128, 8, 128], BF16)) for i in range(2)]
    pS = [B(nc.alloc_psum_tensor(f"pS{i}", [128, 512], F32)) for i in range(3)]
    pO = [B(nc.alloc_psum_tensor(f"pO{i}", [128, 512], F32)) for i in range(2)]
    pG = [B(nc.alloc_psum_tensor(f"pG{i}", [128, 512], F32)) for i in range(1)]
    gen_pool = [pG[0], pS[0], pS[1], pS[2], pO[0], pO[1]]
    st = dict(g=0, t=0, tmp=0, ssc=0, pt=0, lat=0, hb=0, ht=0, pending_HT=None, finish_prev=None, stores=[])
    held = []

    def gbank():
        while True:
            b = gen_pool[st["g"] % len(gen_pool)]
            st["g"] += 1
            if b not in held:
                return b

    def tbank():
        b = pTb[st["t"] % 2]
        st["t"] += 1
        return b

    held_tmp = []

    def gtmp():
        while True:
            b = tmp[st["tmp"] % 4]
            st["tmp"] += 1
            if b not in held_tmp:
                return b

    def sscol():
        c = 2 * (st["ssc"] % 16)
        st["ssc"] += 1
        return c

    def mm(out, lhsT, rhs, start, stop, r, w, **kw):
        P.op("pe", lambda e: e.matmul(out, lhsT=lhsT, rhs=rhs, start=start, stop=stop, **kw), r=r, w=w)

    def tr(out, in_, r, w):
        P.op("pe", lambda e: e.transpose(out=out, in_=in_, identity=ident.t[:in_.shape[0], :in_.shape[0]]), r=list(r) + [CONST], w=w)

    def act(out, in_, func, r, w, **kw):
        P.op("act", lambda e: e.activation(out=out, in_=in_, func=func, **kw), r=r, w=w)

    def tt(eng, out, in0, in1, op, r, w):
        P.op(eng, lambda e: e.tensor_tensor(out=out, in0=in0, in1=in1, op=op), r=r, w=w)

    def stt(eng, out, in0, scalar, in1, op0, op1, r, w):
        P.op(eng, lambda e: e.scalar_tensor_tensor(out=out, in0=in0, scalar=scalar, in1=in1, op0=op0, op1=op1), r=r, w=w)

    def ts(eng, out, in0, s1, op0, r, w):
        P.op(eng, lambda e: e.tensor_scalar(out=out, in0=in0, scalar1=s1, scalar2=None, op0=op0), r=r, w=w)

    def cp(eng, out, in_, r, w):
        if eng == "act":
            P.op("act", lambda e: e.activation(out=out, in_=in_, func=AF.Copy), r=r, w=w)
        else:
            P.op(eng, lambda e: e.tensor_copy(out=out, in_=in_), r=r, w=w)

    def recip(out, in_, r, w):
        P.op("dve", lambda e: e.reciprocal(out=out, in_=in_), r=r, w=w)

    def memset(eng, ap, val, w):
        P.op(eng, lambda e: e.memset(ap, val), w=w)

    def dma(q, out, in_, r, w, tok):
        P.op(q, lambda e: e.dma_start(out=out, in_=in_), r=r, w=w, dma=tok)

    for name in ("w_in", "wq_n", "wq_r", "w_uk", "w_uv", "cdiag", "w_ao", "w_co", "w_out", "w_up", "w_dn", "w_pg", "w_pp"):
        rows = wdefs[name][0]
        for r0 in range(0, rows, 128):
            r1 = min(rows, r0 + 128)
            dma("pool", wb[name][r0:r1, :], wf[name][r0:r1, :], [], [wtok[name]], wtok[name])
    for r0 in range(0, NB * PAST, 1024):
        r1 = min(NB * PAST, r0 + 1024)
        dma("pool", clat_b[r0:r1, :], clat[r0:r1, :], [], [ctok], ctok)
    for r0 in range(0, NB * PAST, 4096):
        r1 = min(NB * PAST, r0 + 4096)
        dma("pool", ckr_b[r0:r1, :], ckr[r0:r1, :], [], [ctok], ctok)
    dma("sp", vec.t[:, :], vecs[:, :], [], [CONST], CONST)
    dma("sp", cosk.t[:, :], cosk_d[:, :], [], [CONST], CONST)
    dma("sp", sink.t[:, :], sink_d[:, :], [], [CONST], CONST)
    if NSAMP:
        dma("sp", cosks.t[:NSAMP, :], cosks_d[:, :], [], [CONST], CONST)
        dma("sp", sinks.t[:NSAMP, :], sinks_d[:, :], [], [CONST], CONST)
    dma("sp", gkv.t[:, :], g_kv.partition_broadcast(128), [], [CONST], CONST)
    dma("sp", wuk.t[:, :, :], wb["w_uk"].rearrange("(kc p) n -> p kc n", p=128), [wtok["w_uk"]], [CONST], CONST)
    dma("sp", wuv.t[:, :, :], wb["w_uv"].rearrange("(kc p) n -> p kc n", p=128), [wtok["w_uv"]], [CONST], CONST)
    memset("pool", identf.t[:, :], 0.0, [CONST])
    P.op("pool", lambda e: e.affine_select(out=identf.t[:, :], in_=identf.t[:, :], pattern=[[-1, 128]], compare_op=ALU.not_equal,
                                           fill=1.0, base=0, channel_multiplier=1), w=[CONST])
    cp("pool", ident.t[:, :], identf.t[:, :], [], [CONST])
    memset("pool", onesb.t[:, :], 1.0, [CONST])
    memset("pool", ones512.t[:, :], 1.0 / 512.0, [CONST])
    memset("pool", onesf.t[:, :], 1.0, [CONST])
    memset("pool", eps6.t[:, :], 1e-6, [CONST])
    memset("pool", eps5.t[:, :], 1e-5, [CONST])
    memset("pool", Vc.t[:, :, :, 64:65], 1.0, [Vc.k])
    memset("pool", ss.t[:, :], 0.0, [ss.k])
    GQ, CB, LG, LB, MK = 0, 6, 10, 14, 18

    wq = []
    wstate = dict(issued=0)
    released = set()

    def slot_view(j):
        name, src, shp = wq[j]
        slot = wslot[j % NSLOT]
        n = 1
        for d in shp[1:]:
            n *= d
        v = slot.t[:shp[0], 0:n]
        if len(shp) == 3:
            v = v.rearrange("p (a b) -> p a b", a=shp[1])
        elif len(shp) == 4:
            v = v.rearrange("p (a b c) -> p a b c", a=shp[1], b=shp[2])
        return v, slot

    def pump():
        while wstate["issued"] < len(wq) and (wstate["issued"] < NSLOT or (wstate["issued"] - NSLOT) in released):
            j = wstate["issued"]
            v, slot = slot_view(j)
            if len(wq[j][2]) == 4:
                for g_ in range(wq[j][2][2]):
                    dma("sp", v[:, :, g_, :], wq[j][1][:, :, g_, :], [wtok[wq[j][0]]], [slot.k], slot.k)
            else:
                dma("sp", v, wq[j][1], [wtok[wq[j][0]]], [slot.k], slot.k)
            wstate["issued"] += 1

    def wget(name, src, shp):
        wq.append((name, src, shp))
        return len(wq) - 1

    def wuse(j):
        pump()
        assert wstate["issued"] > j, "weight slot pipeline stuck"
        v, slot = slot_view(j)
        return v, slot.k

    def wdone(*js):
        for j in js:
            released.add(j)
        pump()

    def kmaj(name, k0, k1, c0, c1):
        return wb[name][k0:k1, c0:c1].rearrange("(kc p) n -> p kc n", p=128), [128, (k1 - k0) // 128, c1 - c0]

    def rstd_from(ssap, sw, epsb):
        c = sscol()
        o = rs.t[:sw, c:c + 1]
        act(o, ssap, AF.Sqrt, [ss.k, CONST], [rs.k], bias=epsb.t[:sw, 0:1], scale=1.0)
        recip(o, o, [rs.k], [rs.k])
        return o

    def norm_to_HT(xres, next_gd, NS, SW):
        HTn = HTs[st["ht"] % 2]
        st["ht"] += 1
        c = 2 * (st["ssc"] % 16)
        st["ssc"] += 1
        memset("pool", ss.t[:SW, c:c + NS], 0.0, [ss.k])
        hb_ = []
        for s in range(NS):
            hbf = hbfs[st["hb"] % 2]
            st["hb"] += 1
            hb_.append(hbf)
            act(hbf.t[:SW, :], xres[s].t[:SW, :], AF.Square, [xres[s].k, ss.k], [hbf.k, ss.k], scale=1.0 / 32.0, accum_out=ss.t[:SW, c + s:c + s + 1])
        act(rs.t[:SW, c:c + NS], ss.t[:SW, c:c + NS], AF.Sqrt, [ss.k, CONST], [rs.k], bias=eps6.t[:SW, 0:1], scale=1.0)
        recip(rs.t[:SW, c:c + NS], rs.t[:SW, c:c + NS], [], [rs.k])
        for s in range(NS):
            hbf = hb_[s]
            stt("dve", hbf.t[:SW, :], xres[s].t[:SW, :], rs.t[:SW, c + s:c + s + 1], grep.t[:SW, :], ALU.mult, ALU.mult, [xres[s].k, rs.k, grep.k], [hbf.k])
        for s in range(NS):
            hbf = hb_[s]
            tb = tbank()
            for kc in range(8):
                tr(tb.t[:, kc, 0:SW], hbf.t[:SW, kc * 128:(kc + 1) * 128], [hbf.k], [tb.k])
            cp("dve" if s % 2 == 0 else "act", HTn.t[:, :, s * SW:(s + 1) * SW], tb.t[:, :, 0:SW], [], [tb.k, HTn.k])
        if next_gd is not None:
            dma("sp", grep.t[:, :], next_gd.partition_broadcast(128), [], [grep.k], grep.k)
        return HTn

    def kv_build(lb, SW, latT_dst, lat_tok, kr_dst, kr_tok):
        tb = tbank()
        tr(tb.t[:, 0, 0:SW], lb.t[:SW, 0:128], [lb.k], [tb.k])
        tr(tb.t[:, 1, 0:SW], lb.t[:SW, 128:256], [lb.k], [tb.k])
        tr(tb.t[:, 2, 0:SW], lb.t[:SW, 288:416], [lb.k], [tb.k])
        cp("dve", latT_dst, tb.t[:, 0:2, 0:SW], [], [tb.k, lat_tok])
        cp("act", kr_dst, tb.t[:, 2, 0:SW], [], [tb.k, kr_tok])

    def kv_project(wcols, kcol0, vblk0, vrows_list):
        for pr in range(4):
            g = gbank()
            for kc in range(2):
                mm(g.t[:, 0:wcols], wuk.t[:, kc, pr * 128:(pr + 1) * 128], latT.t[:, kc, 0:wcols], kc == 0, kc == 1, [CONST, latT.k], [g.k])
            cp("act" if pr % 2 else "dve", Kc.t[:, pr, kcol0:kcol0 + wcols], g.t[:, 0:wcols], [], [g.k, Kc.k])
        for i, (c0, n) in enumerate(vrows_list):
            g = gbank()
            for kc in range(2):
                mm(g.t[:n, :], latT.t[:, kc, c0:c0 + n], wuv.t[:, kc, :], kc == 0, kc == 1, [CONST, latT.k], [g.k])
            cp("dve" if i % 2 else "act", Vc.t[:n, vblk0 + i, :, 0:64], g.t[:n, :].rearrange("p (h d) -> p h d", h=NH), [], [g.k, Vc.k])

    def attention(h, qc0, qn, blocks, hook=None):
        pr = h // 2
        ob = pO[h % 2]
        nb = len(blocks)
        groups = [blocks[i:i + 2] for i in range(0, nb, 2)]
        sbank = {}

        def S(gi):
            sbk = pS[gi % 3]
            sbank[gi] = sbk
            for j, (kcol, nk, c0, dg) in enumerate(groups[gi]):
                o = sbk.t[:nk, j * W + c0:j * W + qn]
                mm(o, Kc.t[:, pr, kcol:kcol + nk], QT.t[:, h, qc0 + c0:qc0 + qn], True, False, [Kc.k, QT.k], [sbk.k])
                mm(o, krT.t[:, kcol:kcol + nk], QrT.t[:, h, qc0 + c0:qc0 + qn], False, True, [krT.k, QrT.k], [sbk.k])

        S(0)
        if len(groups) > 1:
            S(1)
        for gi in range(len(groups)):
            if gi + 2 < len(groups):
                S(gi + 2)
            if gi == min(2, len(groups) - 1) and hook is not None:
                hook()
            sbk = sbank[gi]
            pt = PT[st["pt"] % 3]
            st["pt"] += 1
            grp = groups[gi]
            full = all(c0 == 0 and nk == 128 and not dg for (_, nk, c0, dg) in grp) and len(grp) == 2 and qn == W
            if full:
                act(pt.t[:, 0:2 * W], sbk.t[:, 0:2 * W], AF.Exp, [], [sbk.k, pt.k], scale=SM_SCALE)
            else:
                for j, (kcol, nk, c0, dg) in enumerate(grp):
                    act(pt.t[:nk, j * W + c0:j * W + qn], sbk.t[:nk, j * W + c0:j * W + qn], AF.Exp, [], [sbk.k, pt.k], scale=SM_SCALE)
                    if dg:
                        memset("pool", pt.t[64:128, j * W + c0:j * W + c0 + 64], 0.0, [pt.k])
            for j, (kcol, nk, c0, dg) in enumerate(grp):
                bi = gi * 2 + j
                mm(ob.t[0:65, c0:qn], Vc.t[:nk, kcol // 128, h, 0:65], pt.t[:nk, j * W + c0:j * W + qn], bi == 0, bi == nb - 1, [Vc.k, pt.k], [ob.k])
        recip(rdn.t[64:65, 0:qn], ob.t[64:65, 0:qn], [], [ob.k, rdn.k])

        def norm():
            g = pG[0]
            mm(g.t[0:64, 0:qn], onesf.t[64:65, 0:64], rdn.t[64:65, 0:qn], True, True, [CONST, rdn.k], [g.k])
            cp("act", rrep.t[0:64, 0:qn], g.t[0:64, 0:qn], [], [g.k, rrep.k])
            tt("dve", oT.t[0:64, h, qc0:qc0 + qn], ob.t[0:64, 0:qn], rrep.t[0:64, 0:qn], ALU.mult, [rrep.k], [ob.k, oT.k])
        return norm

    FFN_ORDER = [("u", 0), ("u", 1), ("d", 0), ("u", 2), ("d", 1), ("u", 3), ("d", 2), ("d", 3)]

    def decl_chunks():
        ch = {}
        ch["cq"] = [wget("w_in", *kmaj("w_in", 0, D, 256 * c, 256 * c + 256)) for c in range(3)]
        ch["ckv"] = wget("w_in", *kmaj("w_in", 0, D, OFF_KV, OFF_KV + 288))
        ch["u"] = [wget("w_in", *kmaj("w_in", 0, D, OFF_CONV + 256 * c, OFF_CONV + 256 * c + 256)) for c in (0, 2, 1, 3)]
        ch["cd"] = [wget("cdiag", wb["cdiag"][128 * c:128 * c + 128, :], [128, CW * 128]) for c in range(4)]
        ch["qn"] = [wget("wq_n", *kmaj("wq_n", 0, QL, 512 * c, 512 * c + 512)) for c in range(2)]
        ch["qr"] = wget("wq_r", *kmaj("wq_r", 0, QL, 0, 512))
        ch["mrg"] = []
        for mp in range(4):
            c0 = 256 * mp
            a = wget("w_ao", wb["w_ao"][:, c0:c0 + 256].rearrange("(h d) n -> d h n", d=64), [64, NH, 256])
            b_ = wget("w_co", *kmaj("w_co", 0, CC, c0, c0 + 256))
            gg = wget("w_in", wb["w_in"][:, OFF_GATE:OFF_GATE + 2 * D].rearrange("(kc p) (g n) -> p kc g n", p=128, g=2)[:, :, :, c0:c0 + 256],
                      [128, 8, 2, 256])
            ch["mrg"].append((a, b_, gg))
        ch["wo"] = [wget("w_out", *kmaj("w_out", 0, D, 512 * n, 512 * n + 512)) for n in range(2)]
        ch["up"], ch["dn"] = {}, {}
        for kind_, q in FFN_ORDER:
            if kind_ == "u":
                ch["up"][q] = [wget("w_up", *kmaj("w_up", 0, D, q * 1024 + 256 * c, q * 1024 + 256 * c + 256)) for c in range(4)]
            else:
                ch["dn"][q] = [wget("w_dn", *kmaj("w_dn", q * 1024, (q + 1) * 1024, 512 * n, 512 * n + 512)) for n in range(2)]
        ch["ple"] = [(wget("w_pg", *kmaj("w_pg", 0, D, 512 * n, 512 * n + 512)),
                      wget("w_pp", *kmaj("w_pp", 0, PLE, 512 * n, 512 * n + 512))) for n in range(2)]
        return ch

    tiles = [("p", seq, ti) for seq in range(NSEQ) for ti in range(NT)] + ([("s", 0, 0)] if NSAMP else [])

    def tile_geom(idx):
        kind, seq, ti = tiles[idx]
        if kind == "p":
            return dict(kind=kind, seq=seq, ti=ti, NS=2, SW=128, Wt=W, t0=ti * W, row0=seq * T + ti * W, xin=xp, pin=pp)
        return dict(kind=kind, seq=0, ti=0, NS=1, SW=NSAMP, Wt=NSAMP, t0=0, row0=0, xin=xs, pin=psd)

    def load_x(idx):
        gm = tile_geom(idx)
        xr = xsets[idx % 2]
        for s in range(gm["NS"]):
            r0 = gm["row0"] + s * gm["SW"]
            dma("sp", xr[s].t[:gm["SW"], :], gm["xin"][r0:r0 + gm["SW"], :], [], [xr[s].k], xr[s].k)

    def flush_stores():
        for dst, src, tok in st["stores"]:
            dma("sp", dst, src, [tok], [], tok)
        st["stores"] = []

    def run_tile(idx):
        gm = tile_geom(idx)
        kind, seq, ti, NS, SW, Wt, t0, row0 = (gm[k] for k in ("kind", "seq", "ti", "NS", "SW", "Wt", "t0", "row0"))
        pin = gm["pin"]
        if kind == "p":
            yout, latout, krout = y_p, lat_p, kr_p
            last = (ti == NT - 1)
        else:
            yout, latout, krout = y_s, lat_s, kr_s
            last = True
        xres = xsets[idx % 2]
        ch = chs.pop(0)

        for s in range(NS):
            dma("pool", pbf.t[:SW, s, :], pin[row0 + s * SW:row0 + (s + 1) * SW, :], [], [pbf.k], pbf.k)
        if kind == "p":
            dma("sp", cos4.t[:, 0:Wt], cos4_d[:, t0:t0 + Wt], [], [cos4.k], cos4.k)
            dma("sp", sin4.t[:, 0:Wt], sin4_d[:, t0:t0 + Wt], [], [sin4.k], sin4.k)
        else:
            dma("sp", cos4.t[:, 0:Wt], cos4s_d[:, :], [], [cos4.k], cos4.k)
            dma("sp", sin4.t[:, 0:Wt], sin4s_d[:, :], [], [sin4.k], sin4.k)

        if st["pending_HT"] is not None:
            HT.b = st["pending_HT"]
            st["pending_HT"] = None
        else:
            HT.b = norm_to_HT(xres, None, NS, SW)

        sbk = gbank()
        held.append(sbk)
        prev = None
        for m in range(6):
            wv, wk = wuse(ch["cq"][m // 2])
            g = gbank()
            for kc in range(8):
                mm(g.t[:, 0:Wt], wv[:, kc, (m % 2) * 128:(m % 2) * 128 + 128], HT.t[:, kc, 0:Wt], kc == 0, kc == 7, [wk, HT.k], [g.k])
            sq = sqT[m % 2]
            act(sq.t[:, 0:Wt], g.t[:, 0:Wt], AF.Square, [], [g.k, sq.k])
            ts("dve", cqT.t[:, m, 0:Wt], g.t[:, 0:Wt], vec.t[:, GQ + m:GQ + m + 1], ALU.mult, [CONST], [g.k, cqT.k])
            if prev is not None:
                mm(sbk.t[:, 0:Wt], onesb.t[:, :], prev[0].t[:, 0:Wt], prev[1] == 0, False, [CONST, prev[0].k], [sbk.k])
            prev = (sq, m)
            if m % 2 == 1:
                wdone(ch["cq"][m // 2])

        dma("sp", grep.t[:, :], g_fin.partition_broadcast(128), [], [grep.k], grep.k)

        wv, wk = wuse(ch["ckv"])
        lbs = []
        for s in range(NS):
            g = gbank()
            for kc in range(8):
                mm(g.t[:SW, 0:288], HT.t[:, kc, s * SW:(s + 1) * SW], wv[:, kc, :], kc == 0, kc == 7, [wk, HT.k], [g.k])
            if s == 0:
                mm(sbk.t[:, 0:Wt], onesb.t[:, :], prev[0].t[:, 0:Wt], False, True, [CONST, prev[0].k], [sbk.k])
                act(rq.t[:, 0:Wt], sbk.t[:, 0:Wt], AF.Sqrt, [CONST], [sbk.k, rq.k], bias=eps6.t[:, 0:1], scale=1.0 / QL)
                held.remove(sbk)
                recip(rq.t[:, 0:Wt], rq.t[:, 0:Wt], [], [rq.k])
                tt("dve", cos4.t[:, 0:Wt], cos4.t[:, 0:Wt], rq.t[:, 0:Wt], ALU.mult, [rq.k], [cos4.k])
                tt("pool", sin4.t[:, 0:Wt], sin4.t[:, 0:Wt], rq.t[:, 0:Wt], ALU.mult, [rq.k], [sin4.k])
            lst = latst[st["lat"] % 2]
            lb = latbf[st["lat"] % 2]
            st["lat"] += 1
            cp("act", lst.t[:SW, 0:288], g.t[:SW, 0:288], [], [g.k, lst.k])
            lbs.append((s, lb, lst))
        wdone(ch["ckv"])

        if kind == "p" and ti == 0:
            memset("pool", ubuf.t[:, :, 0:HIST], 0.0, [ubuf.k])
        if last:
            ba, bb = gbank(), gbank()
            held.extend([ba, bb])
        for half in range(2):
            wa, ka = wuse(ch["u"][2 * half])
            wb_, kb = wuse(ch["u"][2 * half + 1])
            for blk in range(2):
                cc = 2 * half + blk
                g1, g2 = gbank(), gbank()
                for kc in range(8):
                    mm(g1.t[:, 0:Wt], wa[:, kc, blk * 128:blk * 128 + 128], HT.t[:, kc, 0:Wt], kc == 0, kc == 7, [ka, HT.k], [g1.k])
                for kc in range(8):
                    mm(g2.t[:, 0:Wt], wb_[:, kc, blk * 128:blk * 128 + 128], HT.t[:, kc, 0:Wt], kc == 0, kc == 7, [kb, HT.k], [g2.k])
                tm = gtmp()
                act(tm.t[:, 0:Wt], g2.t[:, 0:Wt], AF.Sigmoid, [], [g2.k, tm.k])
                if kind == "p":
                    tt("dve", ubuf.t[:, cc, HIST:HIST + Wt], g1.t[:, 0:Wt], tm.t[:, 0:Wt], ALU.mult, [tm.k], [g1.k, ubuf.k])
                else:
                    for b in range(NB):
                        tt("dve", ubuf.t[:, cc, b * USEG + HIST:b * USEG + HIST + DEC], g1.t[:, b * DEC:(b + 1) * DEC],
                           tm.t[:, b * DEC:(b + 1) * DEC], ALU.mult, [tm.k], [g1.k, ubuf.k])
            if last:
                sl = slice((NS - 1) * SW, NS * SW)
                for kc in range(8):
                    mm(ba.t[:SW, half * 256:half * 256 + 256], HT.t[:, kc, sl], wa[:, kc, :], kc == 0, kc == 7, [ka, HT.k], [ba.k])
                for kc in range(8):
                    mm(bb.t[:SW, half * 256:half * 256 + 256], HT.t[:, kc, sl], wb_[:, kc, :], kc == 0, kc == 7, [kb, HT.k], [bb.k])
            wdone(ch["u"][2 * half], ch["u"][2 * half + 1])
        if last:
            tm, tm2 = gtmp(), gtmp()
            act(tm.t[:SW, :], bb.t[:SW, :], AF.Sigmoid, [], [bb.k, tm.k])
            tt("dve", tm2.t[:SW, :], ba.t[:SW, :], tm.t[:SW, :], ALU.mult, [tm.k], [ba.k, tm2.k])
            held.remove(ba)
            held.remove(bb)
            if kind == "p":
                dma("sp", cv_p[seq, :, :], tm2.t[SW - HIST:SW, :], [tm2.k], [], tm2.k)
            else:
                for b in range(NB):
                    dma("sp", cv_s[b, HIST - DEC:HIST, :], tm2.t[b * DEC:(b + 1) * DEC, :], [tm2.k], [], tm2.k)
                    dma("sp", cv_s[b, 0:HIST - DEC, :], ccv[b, DEC:HIST, :], [], [], tm2.k)

        ckc = 2 * (st["ssc"] % 16)
        st["ssc"] += 1

        def ckv_a():
            memset("pool", ss.t[:SW, ckc:ckc + NS], 0.0, [ss.k])
            for s, lb, lst in lbs:
                act(lb.t[:SW, 0:256], lst.t[:SW, 0:256], AF.Square, [ss.k, lst.k], [lb.k, ss.k], scale=1.0 / 16.0, accum_out=ss.t[:SW, ckc + s:ckc + s + 1])
            act(rs.t[:SW, ckc:ckc + NS], ss.t[:SW, ckc:ckc + NS], AF.Sqrt, [ss.k, CONST], [rs.k], bias=eps6.t[:SW, 0:1], scale=1.0)

        def ckv_b():
            recip(rs.t[:SW, ckc:ckc + NS], rs.t[:SW, ckc:ckc + NS], [], [rs.k])
            for s, lb, lst in lbs:
                stt("dve", lst.t[:SW, 0:256], lst.t[:SW, 0:256], rs.t[:SW, ckc + s:ckc + s + 1], gkv.t[:SW, :], ALU.mult, ALU.mult, [rs.k, CONST], [lst.k])
                if kind == "p":
                    blk = (t0 + s * SW) // 128
                    cs_, sn_ = cosk.t[:SW, blk * 16:blk * 16 + 16], sink.t[:SW, blk * 16:blk * 16 + 16]
                else:
                    cs_, sn_ = cosks.t[:SW, :], sinks.t[:SW, :]
                x1, x2 = lst.t[:SW, 256:272], lst.t[:SW, 272:288]
                rt = ropet
                tt("dve", rt.t[:SW, 0:16], x1, cs_, ALU.mult, [CONST, lst.k], [rt.k])
                tt("dve", rt.t[:SW, 16:32], x2, sn_, ALU.mult, [CONST, lst.k], [rt.k])
                tt("dve", rt.t[:SW, 32:48], x2, cs_, ALU.mult, [CONST, lst.k], [rt.k])
                tt("dve", rt.t[:SW, 48:64], x1, sn_, ALU.mult, [CONST, lst.k], [rt.k])
                tt("dve", lst.t[:SW, 256:272], rt.t[:SW, 0:16], rt.t[:SW, 16:32], ALU.subtract, [rt.k], [lst.k])
                tt("dve", lst.t[:SW, 272:288], rt.t[:SW, 32:48], rt.t[:SW, 48:64], ALU.add, [rt.k], [lst.k])

        def ckv_c():
            for s, lb, lst in lbs:
                r0 = row0 + s * SW
                st["stores"].append((latout[r0:r0 + SW, :], lst.t[:SW, 0:256], lst.k))
                st["stores"].append((krout[r0:r0 + SW, :], lst.t[:SW, 256:288], lst.k))
                cp("pool", lb.t[:SW, 0:288], lst.t[:SW, 0:288], [lst.k], [lb.k])
                for rr in range(4):
                    cp("pool", lb.t[:SW, 288 + 32 * rr:320 + 32 * rr], lst.t[:SW, 256:288], [lst.k], [lb.k])

        fin_prev = st["finish_prev"]
        st["finish_prev"] = None
        ckv_a()
        if fin_prev is not None:
            fin_prev[0]()

        if kind == "s":
            ccs = hidT[0]
            for b in range(NB):
                dma("pool", ccs.t[:HIST, 2 * b:2 * b + 2, :], ccv[b, :, :].rearrange("r (a c) -> r a c", a=2), [], [ccs.k], ccs.k)
            for b in range(NB):
                tb = tbank()
                for cc in range(4):
                    tr(tb.t[:, cc, 0:HIST], ccs.t[:HIST, 2 * b + cc // 2, (cc % 2) * 128:(cc % 2) * 128 + 128], [ccs.k], [tb.k])
                cp("dve", ubuf.t[:, :, b * USEG:b * USEG + HIST], tb.t[:, 0:4, 0:HIST], [], [tb.k, ubuf.k])
        dwf = [gtmp(), gtmp()]
        held_tmp.extend(dwf)
        for cc in range(4):
            wv, wk = wuse(ch["cd"][cc])
            db = gbank()
            o0 = 0
            df = dwf[cc // 2].t[:, (cc % 2) * W:(cc % 2) * W + Wt]
            if kind == "p":
                for j in range(CW):
                    mm(db.t[:, o0:o0 + Wt], wv[:, j * 128:(j + 1) * 128], ubuf.t[:, cc, j:j + Wt], j == 0, j == CW - 1, [wk, ubuf.k], [db.k])
            else:
                for b in range(NB):
                    for j in range(CW):
                        mm(db.t[:, o0 + b * DEC:o0 + (b + 1) * DEC], wv[:, j * 128:(j + 1) * 128], ubuf.t[:, cc, b * USEG + j:b * USEG + j + DEC],
                           j == 0, j == CW - 1, [wk, ubuf.k], [db.k])
            wdone(ch["cd"][cc])
            act(df, db.t[:, o0:o0 + Wt], AF.Identity, [CONST], [db.k, dwf[cc // 2].k], bias=vec.t[:, CB + cc:CB + cc + 1], scale=1.0)
            cp("pool", dwbf.t[:, cc, 0:Wt], df, [dwf[cc // 2].k], [dwbf.k])
            tt("pool", dsq.t[:, cc, 0:Wt], df, df, ALU.mult, [dwf[cc // 2].k], [dsq.k])
        if kind == "p":
            cp("pool", ubuf.t[:, :, 0:HIST], ubuf.t[:, :, Wt:Wt + HIST], [], [ubuf.k])

        ckv_b()
        if fin_prev is not None:
            fin_prev[1]()

        for h in range(NH):
            wv, wk = wuse(ch["qn"][h // 4])
            g = gbank()
            for kc in range(6):
                mm(g.t[:, 0:Wt], wv[:, kc, (h % 4) * 128:(h % 4) * 128 + 128], cqT.t[:, kc, 0:Wt], kc == 0, kc == 5, [wk, cqT.k], [g.k])
            tt("dve", QT.t[:, h, 0:Wt], g.t[:, 0:Wt], rq.t[:, 0:Wt], ALU.mult, [rq.k], [g.k, QT.k])
            if h % 4 == 3:
                wdone(ch["qn"][h // 4])

        ckv_c()

        mb = gbank()
        held.append(mb)
        for cc in range(4):
            mm(mb.t[:, 0:Wt], ones512.t[:, :], dwbf.t[:, cc, 0:Wt], cc == 0, cc == 3, [CONST, dwbf.k], [mb.k])
        for cc in range(4):
            mm(mb.t[:, W:W + Wt], ones512.t[:, :], dsq.t[:, cc, 0:Wt], cc == 0, cc == 3, [CONST, dsq.k], [mb.k])

        for s_, lb, _ in lbs:
            if kind == "p":
                kv_build(lb, SW, latT.t[:, 0:2, s_ * SW:(s_ + 1) * SW], latT.k, krT.t[:, t0 + s_ * SW:t0 + (s_ + 1) * SW], krT.k)
            else:
                kv_build(lb, SW, latTs.t[:, 0:2, 0:SW], latTs.k, krTs.t[:, 0:SW], krTs.k)
        if kind == "p":
            kv_project(Wt, t0, t0 // 128, [(0, 128), (128, 128)])

        wv, wk = wuse(ch["qr"])
        for g4 in range(2):
            g1, g2 = gbank(), gbank()
            for kc in range(6):
                mm(g1.t[:, 0:Wt], wv[:, kc, g4 * 128:g4 * 128 + 128], cqT.t[:, kc, 0:Wt], kc == 0, kc == 5, [wk, cqT.k], [g1.k])
            for kc in range(6):
                mm(g2.t[:, 0:Wt], wv[:, kc, 256 + g4 * 128:256 + g4 * 128 + 128], cqT.t[:, kc, 0:Wt], kc == 0, kc == 5, [wk, cqT.k], [g2.k])
            t1, t2 = gtmp(), gtmp()
            tt("dve", t1.t[:, 0:Wt], g1.t[:, 0:Wt], cos4.t[:, 0:Wt], ALU.mult, [cos4.k], [g1.k, t1.k])
            tt("dve", t2.t[:, 0:Wt], g2.t[:, 0:Wt], sin4.t[:, 0:Wt], ALU.mult, [sin4.k], [g2.k, t2.k])
            tt("pool", t1.t[:, 0:Wt], t1.t[:, 0:Wt], t2.t[:, 0:Wt], ALU.add, [t2.k], [t1.k])
            for j in range(4):
                if j % 2:
                    ts("dve", QrT.t[:, 4 * g4 + j, 0:Wt], t1.t[:, 0:Wt], vec.t[:, MK + j:MK + j + 1], ALU.mult, [CONST, t1.k], [QrT.k])
                else:
                    act(QrT.t[:, 4 * g4 + j, 0:Wt], t1.t[:, 0:Wt], AF.Copy, [CONST, t1.k], [QrT.k], scale=vec.t[:, MK + j:MK + j + 1])
        wdone(ch["qr"])

        cp("act", mean.t[:, 0:Wt], mb.t[:, 0:Wt], [], [mb.k, mean.k])
        tt("pool", var.t[:, 0:Wt], mean.t[:, 0:Wt], mean.t[:, 0:Wt], ALU.mult, [mean.k], [var.k])
        tt("dve", var.t[:, 0:Wt], mb.t[:, W:W + Wt], var.t[:, 0:Wt], ALU.subtract, [], [mb.k, var.k])
        held.remove(mb)
        act(lrs.t[:, 0:Wt], var.t[:, 0:Wt], AF.Sqrt, [var.k, CONST], [lrs.k], bias=eps5.t[:, 0:1], scale=1.0)
        recip(lrs.t[:, 0:Wt], lrs.t[:, 0:Wt], [], [lrs.k])

        def ln_apply_a():
            for cc in range(4):
                df = dwf[cc // 2].t[:, (cc % 2) * W:(cc % 2) * W + Wt]
                tt("dve", df, df, mean.t[:, 0:Wt], ALU.subtract, [mean.k], [dwf[cc // 2].k])
                tt("pool", df, df, lrs.t[:, 0:Wt], ALU.mult, [lrs.k], [dwf[cc // 2].k])

        def ln_apply_b():
            for cc in range(4):
                df = dwf[cc // 2].t[:, (cc % 2) * W:(cc % 2) * W + Wt]
                act(sT.t[:, cc, 0:Wt], df, AF.Silu, [dwf[cc // 2].k, CONST], [sT.k], scale=vec.t[:, LG + cc:LG + cc + 1], bias=vec.t[:, LB + cc:LB + cc + 1])
            held_tmp.remove(dwf[0])
            held_tmp.remove(dwf[1])

        if kind == "p":
            nkb = (t0 + Wt) // 128
            blocks = []
            for kb in range(nkb):
                c0 = max(0, kb * 128 - t0)
                blocks.append((kb * 128, 128, c0, kb * 128 >= t0))
            pend = None
            for h in range(NH):
                if h == 2:
                    pend = (lambda p: (lambda: (p(), ln_apply_a())))(pend)
                if h == 5:
                    pend = (lambda p: (lambda: (p(), ln_apply_b())))(pend)
                pend = attention(h, 0, Wt, blocks, hook=pend)
            pend()
        else:
            ln_apply_a()
            ln_apply_b()
            for b in range(NB):
                for blk in range(PAST // 128):
                    lb = latbf[st["lat"] % 2]
                    st["lat"] += 1
                    r0 = b * PAST + blk * 128
                    dma("sp", lb.t[:, 0:256], clat_b[r0:r0 + 128, :], [ctok], [lb.k], lb.k)
                    dma("sp", lb.t[:, 256:288], ckr_b[r0:r0 + 128, :], [ctok], [lb.k], lb.k)
                    for rr in range(4):
                        cp("pool", lb.t[:, 288 + 32 * rr:320 + 32 * rr], lb.t[:, 256:288], [], [lb.k])
                    half = blk % 2
                    kv_build(lb, 128, latT.t[:, 0:2, half * 128:(half + 1) * 128], latT.k, krT.t[:, blk * 128:(blk + 1) * 128], krT.k)
                    if half == 1:
                        kv_project(256, (blk - 1) * 128, blk - 1, [(0, 128), (128, 128)])
                cp("pool", latT.t[:, 0:2, 0:DEC], latTs.t[:, 0:2, b * DEC:(b + 1) * DEC], [latTs.k], [latT.k])
                cp("pool", krT.t[:, PAST:PAST + DEC], krTs.t[:, b * DEC:(b + 1) * DEC], [krTs.k], [krT.k])
                kv_project(DEC, PAST, PAST // 128, [(0, DEC)])
                blocks = [(kb * 128, 128, 0, False) for kb in range(PAST // 128)] + [(PAST, DEC, 0, False)]
                pend = None
                for h in range(NH):
                    pend = attention(h, b * DEC, DEC, blocks, hook=pend)
                pend()

        dma("sp", grep.t[:, :], g_ffn.partition_broadcast(128), [], [grep.k], grep.k)

        for mp in range(4):
            gA = [gbank(), gbank()]
            held.extend(gA)
            gG = [gbank(), gbank()]
            held.extend(gG)
            wa, ka = wuse(ch["mrg"][mp][0])
            for blk in range(2):
                bs = slice(blk * 128, blk * 128 + 128)
                for h in range(NH):
                    mm(gA[blk].t[:, 0:Wt], wa[:, h, bs], oT.t[0:64, h, 0:Wt], h == 0, h == NH - 1, [ka, oT.k], [gA[blk].k])
            wdone(ch["mrg"][mp][0])
            wc, kc_ = wuse(ch["mrg"][mp][1])
            for blk in range(2):
                bs = slice(blk * 128, blk * 128 + 128)
                for cc in range(4):
                    mm(gA[blk].t[:, W:W + Wt], wc[:, cc, bs], sT.t[:, cc, 0:Wt], cc == 0, cc == 3, [kc_, sT.k], [gA[blk].k])
            wdone(ch["mrg"][mp][1])
            wgg, kgg = wuse(ch["mrg"][mp][2])
            for blk in range(2):
                bs = slice(blk * 128, blk * 128 + 128)
                for kc in range(8):
                    mm(gG[blk].t[:, 0:Wt], wgg[:, kc, 0, bs], HT.t[:, kc, 0:Wt], kc == 0, kc == 7, [kgg, HT.k], [gG[blk].k])
                for kc in range(8):
                    mm(gG[blk].t[:, W:W + Wt], wgg[:, kc, 1, bs], HT.t[:, kc, 0:Wt], kc == 0, kc == 7, [kgg, HT.k], [gG[blk].k])
            wdone(ch["mrg"][mp][2])
            for blk in range(2):
                m = 2 * mp + blk
                t1, t2 = gtmp(), gtmp()
                for o0 in (0, W):
                    act(t1.t[:, o0:o0 + Wt], gG[blk].t[:, o0:o0 + Wt], AF.Sigmoid, [], [gG[blk].k, t1.k])
                    tt("dve", t2.t[:, o0:o0 + Wt], gA[blk].t[:, o0:o0 + Wt], t1.t[:, o0:o0 + Wt], ALU.mult, [t1.k], [gA[blk].k, t2.k])
                tt("pool", mrgT.t[:, m, 0:Wt], t2.t[:, 0:Wt], t2.t[:, W:W + Wt], ALU.add, [t2.k], [mrgT.k])
            for b_ in gA + gG:
                held.remove(b_)

        wo = [wuse(c_) for c_ in ch["wo"]]
        for s in range(NS):
            for n in range(2):
                wv, wk = wo[n]
                g = gbank()
                for kc in range(8):
                    mm(g.t[:SW, :], mrgT.t[:, kc, s * SW:(s + 1) * SW], wv[:, kc, :], kc == 0, kc == 7, [wk, mrgT.k], [g.k])
                xs_ = xres[s].t[:SW, n * 512:(n + 1) * 512]
                tt("dve", xs_, xs_, g.t[:SW, :], ALU.add, [], [g.k, xres[s].k])
        wdone(*ch["wo"])

        flush_stores()
        if idx + 1 < len(tiles):
            load_x(idx + 1)

        HT.b = norm_to_HT(xres, None, NS, SW)

        def ffn_up(q):
            hb = hidT[(q + 1) % 2]
            for c in range(4):
                wv, wk = wuse(ch["up"][q][c])
                for blk in range(2):
                    hc = 2 * c + blk
                    g = gbank()
                    for kc in range(8):
                        mm(g.t[:, 0:Wt], wv[:, kc, blk * 128:blk * 128 + 128], HT.t[:, kc, 0:Wt], kc == 0, kc == 7, [wk, HT.k], [g.k])
                    t1 = gtmp()
                    act(t1.t[:, 0:Wt], g.t[:, 0:Wt], AF.Relu, [], [g.k, t1.k])
                    tt("pool", hb.t[:, hc, 0:Wt], t1.t[:, 0:Wt], t1.t[:, 0:Wt], ALU.mult, [t1.k], [hb.k])
                wdone(ch["up"][q][c])

        def ffn_dn(q):
            hb = hidT[(q + 1) % 2]
            dns = ch["dn"][q]
            if q < 3:
                for n in range(2):
                    wv, wk = wuse(dns[n])
                    for s in range(NS):
                        g = gbank()
                        for kc in range(8):
                            mm(g.t[:SW, :], hb.t[:, kc, s * SW:(s + 1) * SW], wv[:, kc, :], kc == 0, kc == 7, [wk, hb.k], [g.k])
                        xs_ = xres[s].t[:SW, n * 512:(n + 1) * 512]
                        tt("dve", xs_, xs_, g.t[:SW, :], ALU.add, [], [g.k, xres[s].k])
                    wdone(dns[n])
            else:
                dn = [wuse(c_) for c_ in dns]
                for s in range(NS):
                    for n in range(2):
                        wv, wk = dn[n]
                        g = gbank()
                        for kc in range(8):
                            mm(g.t[:SW, :], hb.t[:, kc, s * SW:(s + 1) * SW], wv[:, kc, :], kc == 0, kc == 7, [wk, hb.k], [g.k])
                        xs_ = xres[s].t[:SW, n * 512:(n + 1) * 512]
                        tt("dve", xs_, xs_, g.t[:SW, :], ALU.add, [], [g.k, xres[s].k])
                wdone(*dns)

        for i_, (kind_, q) in enumerate(FFN_ORDER):
            (ffn_up if kind_ == "u" else ffn_dn)(q)
            if i_ == 1:
                dma("sp", grep.t[:, :], g_ple.partition_broadcast(128), [], [grep.k], grep.k)

        for s in range(NS):
            tb = tbank()
            for kc in range(2):
                tr(tb.t[:, kc, 0:SW], pbf.t[:SW, s, kc * 128:(kc + 1) * 128], [pbf.k], [tb.k])
            cp("dve", ppT.t[:, 0:2, s * SW:(s + 1) * SW], tb.t[:, 0:2, 0:SW], [], [tb.k, ppT.k])
        nxt = idx + 1 < len(tiles)
        HT.b = norm_to_HT(xres, g_mix if nxt else g_fin, NS, SW)
        if nxt:
            gn = tile_geom(idx + 1)
            st["pending_HT"] = norm_to_HT(xsets[(idx + 1) % 2], None, gn["NS"], gn["SW"])
        for n in range(2):
            wg_, kg = wuse(ch["ple"][n][0])
            wp_, kp = wuse(ch["ple"][n][1])
            for s in range(NS):
                g1, g2 = gbank(), gbank()
                for kc in range(8):
                    mm(g1.t[:SW, :], HT.t[:, kc, s * SW:(s + 1) * SW], wg_[:, kc, :], kc == 0, kc == 7, [kg, HT.k], [g1.k])
                for kc in range(2):
                    mm(g2.t[:SW, :], ppT.t[:, kc, s * SW:(s + 1) * SW], wp_[:, kc, :], kc == 0, kc == 1, [kp, ppT.k], [g2.k])
                t1 = gtmp()
                act(t1.t[:SW, :], g1.t[:SW, :], AF.Sigmoid, [], [g1.k, t1.k])
                tt("dve", t1.t[:SW, :], t1.t[:SW, :], g2.t[:SW, :], ALU.mult, [], [g2.k, t1.k])
                xs_ = xres[s].t[:SW, n * 512:(n + 1) * 512]
                tt("pool", xs_, xs_, t1.t[:SW, :], ALU.add, [t1.k], [xres[s].k])
            wdone(*ch["ple"][n])

        fc = 2 * (st["ssc"] % 16)
        st["ssc"] += 1

        def finish_a():
            memset("pool", ss.t[:SW, fc:fc + NS], 0.0, [ss.k])
            for s in range(NS):
                hbf = hbfs[st["hb"] % 2]
                st["hb"] += 1
                act(hbf.t[:SW, :], xres[s].t[:SW, :], AF.Square, [xres[s].k, ss.k], [hbf.k, ss.k], scale=1.0 / 32.0, accum_out=ss.t[:SW, fc + s:fc + s + 1])
            act(rs.t[:SW, fc:fc + NS], ss.t[:SW, fc:fc + NS], AF.Sqrt, [ss.k, CONST], [rs.k], bias=eps6.t[:SW, 0:1], scale=1.0)

        def finish_b():
            recip(rs.t[:SW, fc:fc + NS], rs.t[:SW, fc:fc + NS], [], [rs.k])
            for s in range(NS):
                stt("dve", xres[s].t[:SW, :], xres[s].t[:SW, :], rs.t[:SW, fc + s:fc + s + 1], grep.t[:SW, :], ALU.mult, ALU.mult, [rs.k, grep.k], [xres[s].k])
                r0 = row0 + s * SW
                st["stores"].append((yout[r0:r0 + SW, :], xres[s].t[:SW, :], xres[s].k))
        st["finish_prev"] = (finish_a, finish_b)

    latTs = sb("latTs", [128, 2, 64], BF16)

    chs = [decl_chunks() for _ in tiles]
    dma("sp", grep.t[:, :], g_mix.partition_broadcast(128), [], [grep.k], grep.k)
    load_x(0)
    for idx in range(len(tiles)):
        run_tile(idx)
    st["finish_prev"][0]()
    st["finish_prev"][1]()
    flush_stores()
    P.emit()
    return nc


def _rope_tables(T, past, dec, nb):
    half = 16
    inv = 10000.0 ** (-np.arange(half, dtype=np.float64) / half)
    pos = np.arange(T, dtype=np.float64)
    ang = pos[:, None] * inv[None, :]
    cos_t, sin_t = np.cos(ang), np.sin(ang)
    nblk = T // 128
    cosk = cos_t.reshape(nblk, 128, 16).transpose(1, 0, 2).reshape(128, nblk * 16)
    sink = sin_t.reshape(nblk, 128, 16).transpose(1, 0, 2).reshape(128, nblk * 16)
    c4 = np.tile(np.concatenate([cos_t.T, cos_t.T], 0), (4, 1))
    s4 = np.tile(np.concatenate([-sin_t.T, sin_t.T], 0), (4, 1))
    poss = past + np.arange(dec, dtype=np.float64)
    angs = poss[:, None] * inv[None, :]
    cs, sn = np.cos(angs), np.sin(angs)
    cosks = np.tile(cs, (nb, 1)); sinks = np.tile(sn, (nb, 1))
    c4s = np.tile(np.concatenate([cosks.T, cosks.T], 0), (4, 1))
    s4s = np.tile(np.concatenate([-sinks.T, sinks.T], 0), (4, 1))
    f = lambda a: np.ascontiguousarray(a, dtype=np.float32)
    return dict(cosk=f(cosk), sink=f(sink), cos4=f(c4), sin4=f(s4), cosks=f(cosks), sinks=f(sinks), cos4s=f(c4s), sin4s=f(s4s))


def _layout_weights(inp):
    f = lambda a: np.ascontiguousarray(a, dtype=np.float32)
    w_uq = np.asarray(inp["w_uq"])[0]
    wq_n = np.zeros((QL, NH, 128), np.float32)
    for h in range(NH):
        o = 0 if h % 2 == 0 else 64
        wq_n[:, h, o:o + 64] = w_uq[:, h, 0:64]
    wq_r = np.zeros((QL, 2, NH, 32), np.float32)
    wq_r[:, 0] = w_uq[:, :, 64:96]
    wq_r[:, 1, :, 0:16] = w_uq[:, :, 80:96]
    wq_r[:, 1, :, 16:32] = w_uq[:, :, 64:80]
    conv_w = np.asarray(inp["conv_w"])[0]
    cdiag = np.zeros((4, 128, CW, 128), np.float32)
    idx = np.arange(128)
    for cc in range(4):
        for j in range(CW):
            cdiag[cc, idx, j, idx] = conv_w[j, cc * 128:(cc + 1) * 128]
    vecs = np.zeros((128, 24), np.float32)
    vecs[:, 0:6] = np.asarray(inp["q_norm_g"])[0].reshape(6, 128).T
    vecs[:, 6:10] = np.asarray(inp["conv_b"])[0].reshape(4, 128).T
    vecs[:, 10:14] = np.asarray(inp["conv_ln_g"])[0].reshape(4, 128).T
    vecs[:, 14:18] = np.asarray(inp["conv_ln_b"])[0].reshape(4, 128).T
    for j in range(4):
        vecs[32 * j:32 * j + 32, 18 + j] = 1.0
    return dict(
        w_in=f(np.asarray(inp["w_in"])[0]), wq_n=f(wq_n.reshape(QL, 1024)), wq_r=f(wq_r.reshape(QL, 512)),
        w_uk=f(np.asarray(inp["w_uk"])[0].reshape(KVL, 512)), w_uv=f(np.asarray(inp["w_uv"])[0].reshape(KVL, 512)),
        w_ao=f(np.asarray(inp["w_attn_out"])[0]), cdiag=f(cdiag.reshape(4 * 128, CW * 128)), w_co=f(np.asarray(inp["w_conv_out"])[0]),
        w_out=f(np.asarray(inp["w_out"])[0]), w_up=f(np.asarray(inp["w_ff_up"])[0]), w_dn=f(np.asarray(inp["w_ff_down"])[0]),
        w_pg=f(np.asarray(inp["w_ple_gate"])[0]), w_pp=f(np.asarray(inp["w_ple_proj"])[0]),
        g_mix=f(np.asarray(inp["norm_mix_g"])[0]), g_ffn=f(np.asarray(inp["norm_ffn_g"])[0]), g_ple=f(np.asarray(inp["ple_norm_g"])[0]),
        g_fin=f(np.asarray(inp["final_norm_g"])), g_kv=f(np.asarray(inp["kv_norm_g"])[0]), vecs=vecs)


_PROG_CACHE = {}


def run_cores(inp, n_cores):
    x_prompt = np.asarray(inp["x_prompt"]); x_sample = np.asarray(inp["x_sample"])
    BATCH, T, _ = x_prompt.shape
    DECB, DEC, _ = x_sample.shape
    PAST = np.asarray(inp["cache_kv_latent"]).shape[2]
    NSEQ = BATCH // n_cores
    NB = DECB // n_cores
    key = (T, NSEQ, NB, PAST, DEC)
    if key not in _PROG_CACHE:
        _PROG_CACHE[key] = build_program(*key)
    nc = _PROG_CACHE[key]
    shared = _layout_weights(inp)
    shared.update(_rope_tables(T, PAST, DEC, NB))
    f = lambda a: np.ascontiguousarray(a, dtype=np.float32)
    p_prompt = np.asarray(inp["p_prompt"])[0]; p_sample = np.asarray(inp["p_sample"])[0]
    clat = np.asarray(inp["cache_kv_latent"])[0]; ckr = np.asarray(inp["cache_k_rope"])[0]; ccv = np.asarray(inp["cache_conv"])[0]
    in_maps = []
    for c in range(n_cores):
        sq = slice(c * NSEQ, (c + 1) * NSEQ)
        bq = slice(c * NB, (c + 1) * NB)
        m = dict(shared)
        m.update(xp=f(x_prompt[sq].reshape(NSEQ * T, D)), xs=f(x_sample[bq].reshape(NB * DEC, D)),
                 pp=f(p_prompt[sq].reshape(NSEQ * T, PLE)), psd=f(p_sample[bq].reshape(NB * DEC, PLE)),
                 clat=f(clat[bq].reshape(NB * PAST, KVL)), ckr=f(ckr[bq].reshape(NB * PAST, 32)), ccv=f(ccv[bq]))
        in_maps.append(m)
    res = run_bass_kernel_spmd(nc, in_maps, core_ids=list(range(n_cores)))
    R = res.results
    cat = lambda k, shp: np.concatenate([np.asarray(r[k], dtype=np.float32).reshape(shp) for r in R], axis=0)
    y_p = cat("y_p", (NSEQ, T, D)); y_s = cat("y_s", (NB, DEC, D))
    lat_p = cat("lat_p", (NSEQ, T, KVL))[None]; kr_p = cat("kr_p", (NSEQ, T, 32))[None]; cv_p = cat("cv_p", (NSEQ, HIST, CC))[None]
    lat_s = cat("lat_s", (NB, DEC, KVL))[None]; kr_s = cat("kr_s", (NB, DEC, 32))[None]; cv_s = cat("cv_s", (NB, HIST, CC))[None]
    return (y_p, y_s, lat_p, kr_p, cv_p, lat_s, kr_s, cv_s)


def kernel(**inputs):
    return run_cores(inputs, NCORES)
```

```python
import numpy as np
import concourse.bass as bass
import concourse.mybir as mybir
from concourse.bass_utils import run_bass_kernel_spmd

F32 = mybir.dt.float32
BF16 = mybir.dt.bfloat16
AF = mybir.ActivationFunctionType
ALU = mybir.AluOpType

D = 1024
NH = 8
KVL = 256
QL = 768
CC = 512
CW = 31
HIST = CW - 1
DFF = 4096
PLE = 256
OFF_KV = 768
OFF_CONV = 1056
OFF_GATE = 2080
IN_W = 4128
SM_SCALE = 96.0 ** -0.5
NCORES = 8


class Tok:
    __slots__ = ("last_w", "readers", "sem")

    def __init__(self):
        self.last_w = None
        self.readers = []
        self.sem = None


class Prog:
    ENGS = ("pe", "act", "dve", "pool", "sp")

    def __init__(self, nc):
        self.nc = nc
        self.ops = {e: [] for e in self.ENGS}
        self.dma_cnt = {}
        self.nsem = 0

    def _tok_sem(self, t):
        if t.sem is None:
            t.sem = self.nc.alloc_semaphore(f"d{self.nsem}")
            self.nsem += 1
            self.dma_cnt[t.sem] = 0
        return t.sem

    def op(self, eng, fn, r=(), w=(), dma=None):
        idx = len(self.ops[eng])
        waits = {}

        def need(dep):
            if dep is not None and waits.get(dep[0], 0) < dep[1]:
                waits[dep[0]] = dep[1]

        for t in r:
            need(t.last_w)
        for t in w:
            need(t.last_w)
            for d in t.readers:
                need(d)
        if dma is not None:
            s = self._tok_sem(dma)
            self.dma_cnt[s] += 16
            mydep = (s, self.dma_cnt[s])
        else:
            mydep = (eng, idx + 1)
        for t in r:
            if len(t.readers) > 64:
                m = {}
                for k, v in t.readers:
                    if m.get(k, 0) < v:
                        m[k] = v
                t.readers = list(m.items())
            t.readers.append(mydep)
        for t in w:
            t.last_w = mydep
            t.readers = []
        self.ops[eng].append((fn, waits, self._tok_sem(dma) if dma is not None else None))

    def emit(self):
        nc = self.nc
        signaled = {e: set() for e in self.ENGS}
        plan = {}
        for e in self.ENGS:
            known = {}
            lst = []
            for (fn, waits, dsem) in self.ops[e]:
                ws = []
                for k, v in waits.items():
                    if k == e and e in ("pe", "sp"):
                        continue
                    if known.get(k, 0) >= v:
                        continue
                    known[k] = v
                    ws.append((k, v))
                    if isinstance(k, str):
                        signaled[k].add(v)
                lst.append(ws)
            plan[e] = lst
        semval = {}
        for e in self.ENGS:
            m = {}
            for c, v in enumerate(sorted(signaled[e])):
                m[v] = c + 1
            semval[e] = m
        esem = {e: nc.alloc_semaphore(f"e_{e}") for e in self.ENGS}
        final_waits = list(self.dma_cnt.items())
        engmap = {"pe": "tensor", "act": "scalar", "dve": "vector", "pool": "gpsimd", "sp": "sync"}
        with nc.Block() as block:
            for e in self.ENGS:
                def body(eng, e=e):
                    for i, (fn, waits, dsem) in enumerate(self.ops[e]):
                        for k, v in plan[e][i]:
                            if isinstance(k, str):
                                eng.wait_ge(esem[k], semval[k][v])
                            else:
                                eng.wait_ge(k, v)
                        ins = fn(eng)
                        if dsem is not None:
                            ins.then_inc(dsem, 16)
                        elif (i + 1) in semval[e]:
                            ins.then_inc(esem[e], 1)
                    if e == "sp":
                        for s, c in final_waits:
                            eng.wait_ge(s, c)
                getattr(block, engmap[e])(body)


class B:
    def __init__(self, t):
        self.t = t
        self.k = Tok()


def build_program(T, NSEQ, NB, PAST, DEC):
    W = 256
    NT = T // W
    NSAMP = NB * DEC
    KMAX = max(T, PAST + DEC)
    NBLK = (KMAX + 127) // 128
    nc = bass.Bass("TRN2", target_bir_lowering=False)
    P = Prog(nc)

    def din(name, shape, dt=F32):
        return nc.dram_tensor(name, list(shape), dt, kind="ExternalInput").ap()

    def dout(name, shape):
        return nc.dram_tensor(name, list(shape), F32, kind="ExternalOutput").ap()

    def dscr(name, shape):
        return nc.dram_tensor(name, list(shape), BF16, kind="Internal").ap()

    xp = din("xp", [NSEQ * T, D])
    xs = din("xs", [NSAMP, D])
    pp = din("pp", [NSEQ * T, PLE])
    psd = din("psd", [NSAMP, PLE])
    clat = din("clat", [NB * PAST, KVL])
    ckr = din("ckr", [NB * PAST, 32])
    ccv = din("ccv", [NB, HIST, CC])
    wdefs = dict(w_in=[D, IN_W], wq_n=[QL, 1024], wq_r=[QL, 512], w_uk=[KVL, 512], w_uv=[KVL, 512],
                 w_ao=[512, D], cdiag=[4 * 128, CW * 128], w_co=[CC, D], w_out=[D, D], w_up=[D, DFF],
                 w_dn=[DFF, D], w_pg=[D, D], w_pp=[PLE, D])
    wf = {k: din(k, v) for k, v in wdefs.items()}
    wb = {k: dscr(k + "_b", v) for k, v in wdefs.items()}
    wtok = {k: Tok() for k in wdefs}
    clat_b = dscr("clat_b", [max(1, NB * PAST), KVL]); ckr_b = dscr("ckr_b", [max(1, NB * PAST), 32])
    ctok = Tok()
    g_mix = din("g_mix", [D]); g_ffn = din("g_ffn", [D]); g_ple = din("g_ple", [D]); g_fin = din("g_fin", [D])
    g_kv = din("g_kv", [KVL])
    vecs = din("vecs", [128, 24])
    cosk_d = din("cosk", [128, (T // 128) * 16]); sink_d = din("sink", [128, (T // 128) * 16])
    cosks_d = din("cosks", [NSAMP, 16]); sinks_d = din("sinks", [NSAMP, 16])
    cos4_d = din("cos4", [128, T]); sin4_d = din("sin4", [128, T])
    cos4s_d = din("cos4s", [128, NSAMP]); sin4s_d = din("sin4s", [128, NSAMP])

    y_p = dout("y_p", [NSEQ * T, D]); y_s = dout("y_s", [NSAMP, D])
    lat_p = dout("lat_p", [NSEQ * T, KVL]); kr_p = dout("kr_p", [NSEQ * T, 32]); cv_p = dout("cv_p", [NSEQ, HIST, CC])
    lat_s = dout("lat_s", [NSAMP, KVL]); kr_s = dout("kr_s", [NSAMP, 32]); cv_s = dout("cv_s", [NB, HIST, CC])

    def sb(name, shape, dt):
        return B(nc.alloc_sbuf_tensor("s_" + name, list(shape), dt))

    Kc = sb("Kc", [128, 4, NBLK * 128], BF16)
    krT = sb("krT", [128, NBLK * 128], BF16)
    Vc = sb("Vc", [128, NBLK, NH, 65], BF16)
    ident = sb("ident", [128, 128], BF16)
    identf = sb("identf", [128, 128], F32)
    onesb = sb("onesb", [128, 128], BF16)
    ones512 = sb("ones512", [128, 128], BF16)
    onesf = sb("onesf", [128, 64], F32)
    eps6 = sb("eps6", [128, 1], F32)
    eps5 = sb("eps5", [128, 1], F32)
    vec = sb("vec", [128, 24], F32)
    cosk = sb("cosk", [128, (T // 128) * 16], F32); sink = sb("sink", [128, (T // 128) * 16], F32)
    cosks = sb("cosks", [128, 16], F32); sinks = sb("sinks", [128, 16], F32)
    gkv = sb("gkv", [128, KVL], F32)
    wuk = sb("wuk", [128, 2, 512], BF16); wuv = sb("wuv", [128, 2, 512], BF16)
    CONST = Tok()
    xsets = [[sb(f"xres{a}_{s}", [128, D], F32) for s in range(2)] for a in range(2)]
    grep = sb("grep", [128, D], F32)
    hbfs = [sb(f"hbf{i}", [128, D], BF16) for i in range(2)]
    HTs = [sb(f"HT{i}", [128, 8, W], BF16) for i in range(2)]

    class _Cur:
        b = None
        t = property(lambda self: self.b.t)
        k = property(lambda self: self.b.k)
    HT = _Cur()
    cqT = sb("cqT", [128, 6, W], BF16)
    sqT = [sb(f"sqT{i}", [128, W], BF16) for i in range(2)]
    rq = sb("rq", [128, W], F32)
    cos4 = sb("cos4t", [128, W], F32); sin4 = sb("sin4t", [128, W], F32)
    latst = [sb(f"latst{i}", [128, 288], F32) for i in range(2)]
    latbf = [sb(f"latbf{i}", [128, 416], BF16) for i in range(2)]
    ropet = sb("ropet", [128, 64], F32)
    latT = sb("latT", [128, 2, W], BF16)
    krTs = sb("krTs", [128, 64], BF16)
    QT = sb("QT", [128, NH, W], BF16)
    QrT = sb("QrT", [128, NH, W], BF16)
    PT = [sb(f"PT{i}", [128, 2 * W], BF16) for i in range(3)]
    oT = sb("oT", [128, NH, W], BF16)
    rdn = sb("rdn", [128, W], F32)
    rrep = sb("rrep", [128, W], F32)
    USEG = HIST + 16
    ubuf = sb("ubuf", [128, 4, HIST + W], BF16)
    dwbf = sb("dwbf", [128, 4, W], BF16)
    dsq = sb("dsq", [128, 4, W], BF16)
    mean = sb("mean", [128, W], F32); var = sb("var", [128, W], F32); lrs = sb("lrs", [128, W], F32)
    tmp = [sb(f"tmp{i}", [128, 512], F32) for i in range(4)]
    hidT = [sb(f"hidT{i}", [128, 8, W], BF16) for i in range(2)]
    sT = cqT
    mrgT = hidT[0]
    pbf = sb("pbf", [128, 2, PLE], BF16)
    ppT = sb("ppT", [128, 2, W], BF16)
    ss = sb("ss", [128, 32], F32)
    rs = sb("rs", [128, 32], F32)
    NSLOT = 4
    wslot = [sb(f"wslot{i}", [128, 4096], BF16) for i in range(NSLOT)]
    pTb = [B(nc.alloc_psum_tensor(f"pT{i}", [128, 8, 128], BF16)) for i in range(2)]
    pS = [B(nc.alloc_psum_tensor(f"pS{i}", [128, 512], F32)) for i in range(3)]
    pO = [B(nc.alloc_psum_tensor(f"pO{i}", [128, 512], F32)) for i in range(2)]
    pG = [B(nc.alloc_psum_tensor(f"pG{i}", [128, 512], F32)) for i in range(1)]
    gen_pool = [pG[0], pS[0], pS[1], pS[2], pO[0], pO[1]]
    st = dict(g=0, t=0, tmp=0, ssc=0, pt=0, lat=0, hb=0, ht=0, pending_HT=None, finish_prev=None, stores=[])
    held = []

    def gbank():
        while True:
            b = gen_pool[st["g"] % len(gen_pool)]
            st["g"] += 1
            if b not in held:
                return b

    def tbank():
        b = pTb[st["t"] % 2]
        st["t"] += 1
        return b

    held_tmp = []

    def gtmp():
        while True:
            b = tmp[st["tmp"] % 4]
            st["tmp"] += 1
            if b not in held_tmp:
                return b

    def sscol():
        c = 2 * (st["ssc"] % 16)
        st["ssc"] += 1
        return c

    def mm(out, lhsT, rhs, start, stop, r, w, **kw):
        P.op("pe", lambda e: e.matmul(out, lhsT=lhsT, rhs=rhs, start=start, stop=stop, **kw), r=r, w=w)

    def tr(out, in_, r, w):
        P.op("pe", lambda e: e.transpose(out=out, in_=in_, identity=ident.t[:in_.shape[0], :in_.shape[0]]), r=list(r) + [CONST], w=w)

    def act(out, in_, func, r, w, **kw):
        P.op("act", lambda e: e.activation(out=out, in_=in_, func=func, **kw), r=r, w=w)

    def tt(eng, out, in0, in1, op, r, w):
        P.op(eng, lambda e: e.tensor_tensor(out=out, in0=in0, in1=in1, op=op), r=r, w=w)

    def stt(eng, out, in0, scalar, in1, op0, op1, r, w):
        P.op(eng, lambda e: e.scalar_tensor_tensor(out=out, in0=in0, scalar=scalar, in1=in1, op0=op0, op1=op1), r=r, w=w)

    def ts(eng, out, in0, s1, op0, r, w):
        P.op(eng, lambda e: e.tensor_scalar(out=out, in0=in0, scalar1=s1, scalar2=None, op0=op0), r=r, w=w)

    def cp(eng, out, in_, r, w):
        if eng == "act":
            P.op("act", lambda e: e.activation(out=out, in_=in_, func=AF.Copy), r=r, w=w)
        else:
            P.op(eng, lambda e: e.tensor_copy(out=out, in_=in_), r=r, w=w)

    def recip(out, in_, r, w):
        P.op("dve", lambda e: e.reciprocal(out=out, in_=in_), r=r, w=w)

    def memset(eng, ap, val, w):
        P.op(eng, lambda e: e.memset(ap, val), w=w)

    def dma(q, out, in_, r, w, tok):
        P.op(q, lambda e: e.dma_start(out=out, in_=in_), r=r, w=w, dma=tok)

    for name in ("w_in", "wq_n", "wq_r", "w_uk", "w_uv", "cdiag", "w_ao", "w_co", "w_out", "w_up", "w_dn", "w_pg", "w_pp"):
        rows = wdefs[name][0]
        for r0 in range(0, rows, 128):
            r1 = min(rows, r0 + 128)
            dma("pool", wb[name][r0:r1, :], wf[name][r0:r1, :], [], [wtok[name]], wtok[name])
    for r0 in range(0, NB * PAST, 1024):
        r1 = min(NB * PAST, r0 + 1024)
        dma("pool", clat_b[r0:r1, :], clat[r0:r1, :], [], [ctok], ctok)
    for r0 in range(0, NB * PAST, 4096):
        r1 = min(NB * PAST, r0 + 4096)
        dma("pool", ckr_b[r0:r1, :], ckr[r0:r1, :], [], [ctok], ctok)
    dma("sp", vec.t[:, :], vecs[:, :], [], [CONST], CONST)
    dma("sp", cosk.t[:, :], cosk_d[:, :], [], [CONST], CONST)
    dma("sp", sink.t[:, :], sink_d[:, :], [], [CONST], CONST)
    if NSAMP:
        dma("sp", cosks.t[:NSAMP, :], cosks_d[:, :], [], [CONST], CONST)
        dma("sp", sinks.t[:NSAMP, :], sinks_d[:, :], [], [CONST], CONST)
    dma("sp", gkv.t[:, :], g_kv.partition_broadcast(128), [], [CONST], CONST)
    dma("sp", wuk.t[:, :, :], wb["w_uk"].rearrange("(kc p) n -> p kc n", p=128), [wtok["w_uk"]], [CONST], CONST)
    dma("sp", wuv.t[:, :, :], wb["w_uv"].rearrange("(kc p) n -> p kc n", p=128), [wtok["w_uv"]], [CONST], CONST)
    memset("pool", identf.t[:, :], 0.0, [CONST])
    P.op("pool", lambda e: e.affine_select(out=identf.t[:, :], in_=identf.t[:, :], pattern=[[-1, 128]], compare_op=ALU.not_equal,
                                           fill=1.0, base=0, channel_multiplier=1), w=[CONST])
    cp("pool", ident.t[:, :], identf.t[:, :], [], [CONST])
    memset("pool", onesb.t[:, :], 1.0, [CONST])
    memset("pool", ones512.t[:, :], 1.0 / 512.0, [CONST])
    memset("pool", onesf.t[:, :], 1.0, [CONST])
    memset("pool", eps6.t[:, :], 1e-6, [CONST])
    memset("pool", eps5.t[:, :], 1e-5, [CONST])
    memset("pool", Vc.t[:, :, :, 64:65], 1.0, [Vc.k])
    memset("pool", ss.t[:, :], 0.0, [ss.k])
    GQ, CB, LG, LB, MK = 0, 6, 10, 14, 18

    wq = []
    wstate = dict(issued=0)
    released = set()

    def slot_view(j):
        name, src, shp = wq[j]
        slot = wslot[j % NSLOT]
        n = 1
        for d in shp[1:]:
            n *= d
        v = slot.t[:shp[0], 0:n]
        if len(shp) == 3:
            v = v.rearrange("p (a b) -> p a b", a=shp[1])
        elif len(shp) == 4:
            v = v.rearrange("p (a b c) -> p a b c", a=shp[1], b=shp[2])
        return v, slot

    def pump():
        while wstate["issued"] < len(wq) and (wstate["issued"] < NSLOT or (wstate["issued"] - NSLOT) in released):
            j = wstate["issued"]
            v, slot = slot_view(j)
            if len(wq[j][2]) == 4:
                for g_ in range(wq[j][2][2]):
                    dma("sp", v[:, :, g_, :], wq[j][1][:, :, g_, :], [wtok[wq[j][0]]], [slot.k], slot.k)
            else:
                dma("sp", v, wq[j][1], [wtok[wq[j][0]]], [slot.k], slot.k)
            wstate["issued"] += 1

    def wget(name, src, shp):
        wq.append((name, src, shp))
        return len(wq) - 1

    def wuse(j):
        pump()
        assert wstate["issued"] > j, "weight slot pipeline stuck"
        v, slot = slot_view(j)
        return v, slot.k

    def wdone(*js):
        for j in js:
            released.add(j)
        pump()

    def kmaj(name, k0, k1, c0, c1):
        return wb[name][k0:k1, c0:c1].rearrange("(kc p) n -> p kc n", p=128), [128, (k1 - k0) // 128, c1 - c0]

    def rstd_from(ssap, sw, epsb):
        c = sscol()
        o = rs.t[:sw, c:c + 1]
        act(o, ssap, AF.Sqrt, [ss.k, CONST], [rs.k], bias=epsb.t[:sw, 0:1], scale=1.0)
        recip(o, o, [rs.k], [rs.k])
        return o

    def norm_to_HT(xres, next_gd, NS, SW, batched=True):
        HTn = HTs[st["ht"] % 2]
        st["ht"] += 1
        c = 2 * (st["ssc"] % 16)
        st["ssc"] += 1
        memset("pool", ss.t[:SW, c:c + NS], 0.0, [ss.k])
        hb_ = []
        for s in range(NS):
            hbf = hbfs[st["hb"] % 2]
            st["hb"] += 1
            hb_.append(hbf)
            act(hbf.t[:SW, :], xres[s].t[:SW, :], AF.Square, [xres[s].k, ss.k], [hbf.k, ss.k], scale=1.0 / 32.0, accum_out=ss.t[:SW, c + s:c + s + 1])
            if not batched:
                act(rs.t[:SW, c + s:c + s + 1], ss.t[:SW, c + s:c + s + 1], AF.Sqrt, [ss.k, CONST], [rs.k], bias=eps6.t[:SW, 0:1], scale=1.0)
                recip(rs.t[:SW, c + s:c + s + 1], rs.t[:SW, c + s:c + s + 1], [], [rs.k])
                stt("dve", hbf.t[:SW, :], xres[s].t[:SW, :], rs.t[:SW, c + s:c + s + 1], grep.t[:SW, :], ALU.mult, ALU.mult, [xres[s].k, rs.k, grep.k], [hbf.k])
        if batched:
            act(rs.t[:SW, c:c + NS], ss.t[:SW, c:c + NS], AF.Sqrt, [ss.k, CONST], [rs.k], bias=eps6.t[:SW, 0:1], scale=1.0)
            recip(rs.t[:SW, c:c + NS], rs.t[:SW, c:c + NS], [], [rs.k])
            for s in range(NS):
                hbf = hb_[s]
                stt("dve", hbf.t[:SW, :], xres[s].t[:SW, :], rs.t[:SW, c + s:c + s + 1], grep.t[:SW, :], ALU.mult, ALU.mult, [xres[s].k, rs.k, grep.k], [hbf.k])
        for s in range(NS):
            hbf = hb_[s]
            tb = tbank()
            for kc in range(8):
                tr(tb.t[:, kc, 0:SW], hbf.t[:SW, kc * 128:(kc + 1) * 128], [hbf.k], [tb.k])
            cp("dve" if s % 2 == 0 else "act", HTn.t[:, :, s * SW:(s + 1) * SW], tb.t[:, :, 0:SW], [], [tb.k, HTn.k])
        if next_gd is not None:
            dma("sp", grep.t[:, :], next_gd.partition_broadcast(128), [], [grep.k], grep.k)
        return HTn

    def kv_build(lb, SW, latT_dst, lat_tok, kr_dst, kr_tok):
        tb = tbank()
        tr(tb.t[:, 0, 0:SW], lb.t[:SW, 0:128], [lb.k], [tb.k])
        tr(tb.t[:, 1, 0:SW], lb.t[:SW, 128:256], [lb.k], [tb.k])
        tr(tb.t[:, 2, 0:SW], lb.t[:SW, 288:416], [lb.k], [tb.k])
        cp("dve", latT_dst, tb.t[:, 0:2, 0:SW], [], [tb.k, lat_tok])
        cp("act", kr_dst, tb.t[:, 2, 0:SW], [], [tb.k, kr_tok])

    def kv_project(wcols, kcol0, vblk0, vrows_list):
        for pr in range(4):
            g = gbank()
            for kc in range(2):
                mm(g.t[:, 0:wcols], wuk.t[:, kc, pr * 128:(pr + 1) * 128], latT.t[:, kc, 0:wcols], kc == 0, kc == 1, [CONST, latT.k], [g.k])
            cp("act" if pr % 2 else "dve", Kc.t[:, pr, kcol0:kcol0 + wcols], g.t[:, 0:wcols], [], [g.k, Kc.k])
        for i, (c0, n) in enumerate(vrows_list):
            g = gbank()
            for kc in range(2):
                mm(g.t[:n, :], latT.t[:, kc, c0:c0 + n], wuv.t[:, kc, :], kc == 0, kc == 1, [CONST, latT.k], [g.k])
            cp("dve" if i % 2 else "act", Vc.t[:n, vblk0 + i, :, 0:64], g.t[:n, :].rearrange("p (h d) -> p h d", h=NH), [], [g.k, Vc.k])

    def attention(h, qc0, qn, blocks, hook=None):
        pr = h // 2
        ob = pO[h % 2]
        nb = len(blocks)
        groups = [blocks[i:i + 2] for i in range(0, nb, 2)]
        sbank = {}

        def S(gi):
            sbk = pS[gi % 3]
            sbank[gi] = sbk
            for j, (kcol, nk, c0, dg) in enumerate(groups[gi]):
                o = sbk.t[:nk, j * W + c0:j * W + qn]
                mm(o, Kc.t[:, pr, kcol:kcol + nk], QT.t[:, h, qc0 + c0:qc0 + qn], True, False, [Kc.k, QT.k], [sbk.k])
                mm(o, krT.t[:, kcol:kcol + nk], QrT.t[:, h, qc0 + c0:qc0 + qn], False, True, [krT.k, QrT.k], [sbk.k])

        S(0)
        if len(groups) > 1:
            S(1)
        for gi in range(len(groups)):
            if gi + 2 < len(groups):
                S(gi + 2)
            if gi == min(2, len(groups) - 1) and hook is not None:
                hook()
            sbk = sbank[gi]
            pt = PT[st["pt"] % 3]
            st["pt"] += 1
            grp = groups[gi]
            full = all(c0 == 0 and nk == 128 and not dg for (_, nk, c0, dg) in grp) and len(grp) == 2 and qn == W
            if full:
                act(pt.t[:, 0:2 * W], sbk.t[:, 0:2 * W], AF.Exp, [], [sbk.k, pt.k], scale=SM_SCALE)
            else:
                for j, (kcol, nk, c0, dg) in enumerate(grp):
                    act(pt.t[:nk, j * W + c0:j * W + qn], sbk.t[:nk, j * W + c0:j * W + qn], AF.Exp, [], [sbk.k, pt.k], scale=SM_SCALE)
                    if dg:
                        memset("pool", pt.t[64:128, j * W + c0:j * W + c0 + 64], 0.0, [pt.k])
            for j, (kcol, nk, c0, dg) in enumerate(grp):
                bi = gi * 2 + j
                mm(ob.t[0:65, c0:qn], Vc.t[:nk, kcol // 128, h, 0:65], pt.t[:nk, j * W + c0:j * W + qn], bi == 0, bi == nb - 1, [Vc.k, pt.k], [ob.k])
        recip(rdn.t[64:65, 0:qn], ob.t[64:65, 0:qn], [], [ob.k, rdn.k])

        def norm():
            g = pG[0]
            mm(g.t[0:64, 0:qn], onesf.t[64:65, 0:64], rdn.t[64:65, 0:qn], True, True, [CONST, rdn.k], [g.k])
            cp("act", rrep.t[0:64, 0:qn], g.t[0:64, 0:qn], [], [g.k, rrep.k])
            tt("dve", oT.t[0:64, h, qc0:qc0 + qn], ob.t[0:64, 0:qn], rrep.t[0:64, 0:qn], ALU.mult, [rrep.k], [ob.k, oT.k])
        return norm

    FFN_ORDER = [("u", 0), ("u", 1), ("d", 0), ("u", 2), ("d", 1), ("u", 3), ("d", 2), ("d", 3)]

    def decl_chunks():
        ch = {}
        ch["cq"] = [wget("w_in", *kmaj("w_in", 0, D, 256 * c, 256 * c + 256)) for c in range(3)]
        ch["ckv"] = wget("w_in", *kmaj("w_in", 0, D, OFF_KV, OFF_KV + 288))
        ch["u"] = [wget("w_in", *kmaj("w_in", 0, D, OFF_CONV + 256 * c, OFF_CONV + 256 * c + 256)) for c in (0, 2, 1, 3)]
        ch["cd"] = [wget("cdiag", wb["cdiag"][128 * c:128 * c + 128, :], [128, CW * 128]) for c in range(4)]
        ch["qn"] = [wget("wq_n", *kmaj("wq_n", 0, QL, 512 * c, 512 * c + 512)) for c in range(2)]
        ch["qr"] = wget("wq_r", *kmaj("wq_r", 0, QL, 0, 512))
        ch["mrg"] = []
        for mp in range(4):
            c0 = 256 * mp
            a = wget("w_ao", wb["w_ao"][:, c0:c0 + 256].rearrange("(h d) n -> d h n", d=64), [64, NH, 256])
            b_ = wget("w_co", *kmaj("w_co", 0, CC, c0, c0 + 256))
            gg = wget("w_in", wb["w_in"][:, OFF_GATE:OFF_GATE + 2 * D].rearrange("(kc p) (g n) -> p kc g n", p=128, g=2)[:, :, :, c0:c0 + 256],
                      [128, 8, 2, 256])
            ch["mrg"].append((a, b_, gg))
        ch["wo"] = [wget("w_out", *kmaj("w_out", 0, D, 512 * n, 512 * n + 512)) for n in range(2)]
        ch["up"], ch["dn"] = {}, {}
        for kind_, q in FFN_ORDER:
            if kind_ == "u":
                ch["up"][q] = [wget("w_up", *kmaj("w_up", 0, D, q * 1024 + 256 * c, q * 1024 + 256 * c + 256)) for c in range(4)]
            else:
                ch["dn"][q] = [wget("w_dn", *kmaj("w_dn", q * 1024, (q + 1) * 1024, 512 * n, 512 * n + 512)) for n in range(2)]
        ch["ple"] = [(wget("w_pg", *kmaj("w_pg", 0, D, 512 * n, 512 * n + 512)),
                      wget("w_pp", *kmaj("w_pp", 0, PLE, 512 * n, 512 * n + 512))) for n in range(2)]
        return ch

    tiles = [("p", seq, ti) for seq in range(NSEQ) for ti in range(NT)] + ([("s", 0, 0)] if NSAMP else [])

    def tile_geom(idx):
        kind, seq, ti = tiles[idx]
        if kind == "p":
            return dict(kind=kind, seq=seq, ti=ti, NS=2, SW=128, Wt=W, t0=ti * W, row0=seq * T + ti * W, xin=xp, pin=pp)
        return dict(kind=kind, seq=0, ti=0, NS=1, SW=NSAMP, Wt=NSAMP, t0=0, row0=0, xin=xs, pin=psd)

    def load_x(idx):
        gm = tile_geom(idx)
        xr = xsets[idx % 2]
        for s in range(gm["NS"]):
            r0 = gm["row0"] + s * gm["SW"]
            dma("sp", xr[s].t[:gm["SW"], :], gm["xin"][r0:r0 + gm["SW"], :], [], [xr[s].k], xr[s].k)

    def flush_stores():
        for dst, src, tok in st["stores"]:
            dma("sp", dst, src, [tok], [], tok)
        st["stores"] = []

    def run_tile(idx):
        gm = tile_geom(idx)
        kind, seq, ti, NS, SW, Wt, t0, row0 = (gm[k] for k in ("kind", "seq", "ti", "NS", "SW", "Wt", "t0", "row0"))
        pin = gm["pin"]
        if kind == "p":
            yout, latout, krout = y_p, lat_p, kr_p
            last = (ti == NT - 1)
        else:
            yout, latout, krout = y_s, lat_s, kr_s
            last = True
        xres = xsets[idx % 2]
        ch = chs.pop(0)

        for s in range(NS):
            dma("pool", pbf.t[:SW, s, :], pin[row0 + s * SW:row0 + (s + 1) * SW, :], [], [pbf.k], pbf.k)
        if kind == "p":
            dma("sp", cos4.t[:, 0:Wt], cos4_d[:, t0:t0 + Wt], [], [cos4.k], cos4.k)
            dma("sp", sin4.t[:, 0:Wt], sin4_d[:, t0:t0 + Wt], [], [sin4.k], sin4.k)
        else:
            dma("sp", cos4.t[:, 0:Wt], cos4s_d[:, :], [], [cos4.k], cos4.k)
            dma("sp", sin4.t[:, 0:Wt], sin4s_d[:, :], [], [sin4.k], sin4.k)

        if st["pending_HT"] is not None:
            HT.b = st["pending_HT"]
            st["pending_HT"] = None
        else:
            HT.b = norm_to_HT(xres, None, NS, SW)

        sbk = gbank()
        held.append(sbk)
        prev = None
        for m in range(6):
            wv, wk = wuse(ch["cq"][m // 2])
            g = gbank()
            for kc in range(8):
                mm(g.t[:, 0:Wt], wv[:, kc, (m % 2) * 128:(m % 2) * 128 + 128], HT.t[:, kc, 0:Wt], kc == 0, kc == 7, [wk, HT.k], [g.k])
            sq = sqT[m % 2]
            act(sq.t[:, 0:Wt], g.t[:, 0:Wt], AF.Square, [], [g.k, sq.k])
            ts("dve", cqT.t[:, m, 0:Wt], g.t[:, 0:Wt], vec.t[:, GQ + m:GQ + m + 1], ALU.mult, [CONST], [g.k, cqT.k])
            if prev is not None:
                mm(sbk.t[:, 0:Wt], onesb.t[:, :], prev[0].t[:, 0:Wt], prev[1] == 0, False, [CONST, prev[0].k], [sbk.k])
            prev = (sq, m)
            if m % 2 == 1:
                wdone(ch["cq"][m // 2])

        dma("sp", grep.t[:, :], g_fin.partition_broadcast(128), [], [grep.k], grep.k)

        wv, wk = wuse(ch["ckv"])
        lbs = []
        for s in range(NS):
            g = gbank()
            for kc in range(8):
                mm(g.t[:SW, 0:288], HT.t[:, kc, s * SW:(s + 1) * SW], wv[:, kc, :], kc == 0, kc == 7, [wk, HT.k], [g.k])
            if s == 0:
                mm(sbk.t[:, 0:Wt], onesb.t[:, :], prev[0].t[:, 0:Wt], False, True, [CONST, prev[0].k], [sbk.k])
                act(rq.t[:, 0:Wt], sbk.t[:, 0:Wt], AF.Sqrt, [CONST], [sbk.k, rq.k], bias=eps6.t[:, 0:1], scale=1.0 / QL)
                held.remove(sbk)
                recip(rq.t[:, 0:Wt], rq.t[:, 0:Wt], [], [rq.k])
                tt("dve", cos4.t[:, 0:Wt], cos4.t[:, 0:Wt], rq.t[:, 0:Wt], ALU.mult, [rq.k], [cos4.k])
                tt("pool", sin4.t[:, 0:Wt], sin4.t[:, 0:Wt], rq.t[:, 0:Wt], ALU.mult, [rq.k], [sin4.k])
            lst = latst[st["lat"] % 2]
            lb = latbf[st["lat"] % 2]
            st["lat"] += 1
            cp("act", lst.t[:SW, 0:288], g.t[:SW, 0:288], [], [g.k, lst.k])
            lbs.append((s, lb, lst))
        wdone(ch["ckv"])

        if kind == "p" and ti == 0:
            memset("pool", ubuf.t[:, :, 0:HIST], 0.0, [ubuf.k])
        if last:
            ba, bb = gbank(), gbank()
            held.extend([ba, bb])
        for half in range(2):
            wa, ka = wuse(ch["u"][2 * half])
            wb_, kb = wuse(ch["u"][2 * half + 1])
            for blk in range(2):
                cc = 2 * half + blk
                g1, g2 = gbank(), gbank()
                for kc in range(8):
                    mm(g1.t[:, 0:Wt], wa[:, kc, blk * 128:blk * 128 + 128], HT.t[:, kc, 0:Wt], kc == 0, kc == 7, [ka, HT.k], [g1.k])
                for kc in range(8):
                    mm(g2.t[:, 0:Wt], wb_[:, kc, blk * 128:blk * 128 + 128], HT.t[:, kc, 0:Wt], kc == 0, kc == 7, [kb, HT.k], [g2.k])
                tm = gtmp()
                act(tm.t[:, 0:Wt], g2.t[:, 0:Wt], AF.Sigmoid, [], [g2.k, tm.k])
                if kind == "p":
                    tt("dve", ubuf.t[:, cc, HIST:HIST + Wt], g1.t[:, 0:Wt], tm.t[:, 0:Wt], ALU.mult, [tm.k], [g1.k, ubuf.k])
                else:
                    for b in range(NB):
                        tt("dve", ubuf.t[:, cc, b * USEG + HIST:b * USEG + HIST + DEC], g1.t[:, b * DEC:(b + 1) * DEC],
                           tm.t[:, b * DEC:(b + 1) * DEC], ALU.mult, [tm.k], [g1.k, ubuf.k])
            if last:
                sl = slice((NS - 1) * SW, NS * SW)
                for kc in range(8):
                    mm(ba.t[:SW, half * 256:half * 256 + 256], HT.t[:, kc, sl], wa[:, kc, :], kc == 0, kc == 7, [ka, HT.k], [ba.k])
                for kc in range(8):
                    mm(bb.t[:SW, half * 256:half * 256 + 256], HT.t[:, kc, sl], wb_[:, kc, :], kc == 0, kc == 7, [kb, HT.k], [bb.k])
            wdone(ch["u"][2 * half], ch["u"][2 * half + 1])
        if last:
            tm, tm2 = gtmp(), gtmp()
            act(tm.t[:SW, :], bb.t[:SW, :], AF.Sigmoid, [], [bb.k, tm.k])
            tt("dve", tm2.t[:SW, :], ba.t[:SW, :], tm.t[:SW, :], ALU.mult, [tm.k], [ba.k, tm2.k])
            held.remove(ba)
            held.remove(bb)
            if kind == "p":
                dma("sp", cv_p[seq, :, :], tm2.t[SW - HIST:SW, :], [tm2.k], [], tm2.k)
            else:
                for b in range(NB):
                    dma("sp", cv_s[b, HIST - DEC:HIST, :], tm2.t[b * DEC:(b + 1) * DEC, :], [tm2.k], [], tm2.k)
                    dma("sp", cv_s[b, 0:HIST - DEC, :], ccv[b, DEC:HIST, :], [], [], tm2.k)

        ckc = 2 * (st["ssc"] % 16)
        st["ssc"] += 1

        def ckv_a():
            memset("pool", ss.t[:SW, ckc:ckc + NS], 0.0, [ss.k])
            for s, lb, lst in lbs:
                act(lb.t[:SW, 0:256], lst.t[:SW, 0:256], AF.Square, [ss.k, lst.k], [lb.k, ss.k], scale=1.0 / 16.0, accum_out=ss.t[:SW, ckc + s:ckc + s + 1])
            act(rs.t[:SW, ckc:ckc + NS], ss.t[:SW, ckc:ckc + NS], AF.Sqrt, [ss.k, CONST], [rs.k], bias=eps6.t[:SW, 0:1], scale=1.0)

        def ckv_b():
            recip(rs.t[:SW, ckc:ckc + NS], rs.t[:SW, ckc:ckc + NS], [], [rs.k])
            for s, lb, lst in lbs:
                stt("dve", lst.t[:SW, 0:256], lst.t[:SW, 0:256], rs.t[:SW, ckc + s:ckc + s + 1], gkv.t[:SW, :], ALU.mult, ALU.mult, [rs.k, CONST], [lst.k])
                if kind == "p":
                    blk = (t0 + s * SW) // 128
                    cs_, sn_ = cosk.t[:SW, blk * 16:blk * 16 + 16], sink.t[:SW, blk * 16:blk * 16 + 16]
                else:
                    cs_, sn_ = cosks.t[:SW, :], sinks.t[:SW, :]
                x1, x2 = lst.t[:SW, 256:272], lst.t[:SW, 272:288]
                rt = ropet
                tt("dve", rt.t[:SW, 0:16], x1, cs_, ALU.mult, [CONST, lst.k], [rt.k])
                tt("dve", rt.t[:SW, 16:32], x2, sn_, ALU.mult, [CONST, lst.k], [rt.k])
                tt("dve", rt.t[:SW, 32:48], x2, cs_, ALU.mult, [CONST, lst.k], [rt.k])
                tt("dve", rt.t[:SW, 48:64], x1, sn_, ALU.mult, [CONST, lst.k], [rt.k])
                tt("dve", lst.t[:SW, 256:272], rt.t[:SW, 0:16], rt.t[:SW, 16:32], ALU.subtract, [rt.k], [lst.k])
                tt("dve", lst.t[:SW, 272:288], rt.t[:SW, 32:48], rt.t[:SW, 48:64], ALU.add, [rt.k], [lst.k])

        def ckv_c():
            for s, lb, lst in lbs:
                r0 = row0 + s * SW
                st["stores"].append((latout[r0:r0 + SW, :], lst.t[:SW, 0:256], lst.k))
                st["stores"].append((krout[r0:r0 + SW, :], lst.t[:SW, 256:288], lst.k))
                cp("pool", lb.t[:SW, 0:288], lst.t[:SW, 0:288], [lst.k], [lb.k])
                for rr in range(4):
                    cp("pool", lb.t[:SW, 288 + 32 * rr:320 + 32 * rr], lst.t[:SW, 256:288], [lst.k], [lb.k])

        fin_prev = st["finish_prev"]
        st["finish_prev"] = None
        ckv_a()
        if fin_prev is not None:
            fin_prev[0]()

        if kind == "s":
            ccs = hidT[0]
            for b in range(NB):
                dma("pool", ccs.t[:HIST, 2 * b:2 * b + 2, :], ccv[b, :, :].rearrange("r (a c) -> r a c", a=2), [], [ccs.k], ccs.k)
            for b in range(NB):
                tb = tbank()
                for cc in range(4):
                    tr(tb.t[:, cc, 0:HIST], ccs.t[:HIST, 2 * b + cc // 2, (cc % 2) * 128:(cc % 2) * 128 + 128], [ccs.k], [tb.k])
                cp("dve", ubuf.t[:, :, b * USEG:b * USEG + HIST], tb.t[:, 0:4, 0:HIST], [], [tb.k, ubuf.k])
        dwf = [gtmp(), gtmp()]
        held_tmp.extend(dwf)
        for cc in range(4):
            wv, wk = wuse(ch["cd"][cc])
            db = gbank()
            o0 = 0
            df = dwf[cc // 2].t[:, (cc % 2) * W:(cc % 2) * W + Wt]
            if kind == "p":
                for j in range(CW):
                    mm(db.t[:, o0:o0 + Wt], wv[:, j * 128:(j + 1) * 128], ubuf.t[:, cc, j:j + Wt], j == 0, j == CW - 1, [wk, ubuf.k], [db.k])
            else:
                for b in range(NB):
                    for j in range(CW):
                        mm(db.t[:, o0 + b * DEC:o0 + (b + 1) * DEC], wv[:, j * 128:(j + 1) * 128], ubuf.t[:, cc, b * USEG + j:b * USEG + j + DEC],
                           j == 0, j == CW - 1, [wk, ubuf.k], [db.k])
            wdone(ch["cd"][cc])
            act(df, db.t[:, o0:o0 + Wt], AF.Identity, [CONST], [db.k, dwf[cc // 2].k], bias=vec.t[:, CB + cc:CB + cc + 1], scale=1.0)
            cp("pool", dwbf.t[:, cc, 0:Wt], df, [dwf[cc // 2].k], [dwbf.k])
            tt("pool", dsq.t[:, cc, 0:Wt], df, df, ALU.mult, [dwf[cc // 2].k], [dsq.k])
        if kind == "p":
            cp("pool", ubuf.t[:, :, 0:HIST], ubuf.t[:, :, Wt:Wt + HIST], [], [ubuf.k])

        for h in range(NH):
            wv, wk = wuse(ch["qn"][h // 4])
            g = gbank()
            for kc in range(6):
                mm(g.t[:, 0:Wt], wv[:, kc, (h % 4) * 128:(h % 4) * 128 + 128], cqT.t[:, kc, 0:Wt], kc == 0, kc == 5, [wk, cqT.k], [g.k])
            tt("dve", QT.t[:, h, 0:Wt], g.t[:, 0:Wt], rq.t[:, 0:Wt], ALU.mult, [rq.k], [g.k, QT.k])
            if h % 4 == 3:
                wdone(ch["qn"][h // 4])

        ckv_b()
        if fin_prev is not None:
            fin_prev[1]()

        mb = gbank()
        held.append(mb)
        for cc in range(4):
            mm(mb.t[:, 0:Wt], ones512.t[:, :], dwbf.t[:, cc, 0:Wt], cc == 0, cc == 3, [CONST, dwbf.k], [mb.k])
        for cc in range(4):
            mm(mb.t[:, W:W + Wt], ones512.t[:, :], dsq.t[:, cc, 0:Wt], cc == 0, cc == 3, [CONST, dsq.k], [mb.k])

        wv, wk = wuse(ch["qr"])
        for g4 in range(2):
            g1, g2 = gbank(), gbank()
            for kc in range(6):
                mm(g1.t[:, 0:Wt], wv[:, kc, g4 * 128:g4 * 128 + 128], cqT.t[:, kc, 0:Wt], kc == 0, kc == 5, [wk, cqT.k], [g1.k])
            for kc in range(6):
                mm(g2.t[:, 0:Wt], wv[:, kc, 256 + g4 * 128:256 + g4 * 128 + 128], cqT.t[:, kc, 0:Wt], kc == 0, kc == 5, [wk, cqT.k], [g2.k])
            t1, t2 = gtmp(), gtmp()
            tt("dve", t1.t[:, 0:Wt], g1.t[:, 0:Wt], cos4.t[:, 0:Wt], ALU.mult, [cos4.k], [g1.k, t1.k])
            tt("dve", t2.t[:, 0:Wt], g2.t[:, 0:Wt], sin4.t[:, 0:Wt], ALU.mult, [sin4.k], [g2.k, t2.k])
            tt("pool", t1.t[:, 0:Wt], t1.t[:, 0:Wt], t2.t[:, 0:Wt], ALU.add, [t2.k], [t1.k])
            for j in range(4):
                if j % 2:
                    ts("dve", QrT.t[:, 4 * g4 + j, 0:Wt], t1.t[:, 0:Wt], vec.t[:, MK + j:MK + j + 1], ALU.mult, [CONST, t1.k], [QrT.k])
                else:
                    act(QrT.t[:, 4 * g4 + j, 0:Wt], t1.t[:, 0:Wt], AF.Copy, [CONST, t1.k], [QrT.k], scale=vec.t[:, MK + j:MK + j + 1])
        wdone(ch["qr"])

        ckv_c()

        for s_, lb, _ in lbs:
            if kind == "p":
                kv_build(lb, SW, latT.t[:, 0:2, s_ * SW:(s_ + 1) * SW], latT.k, krT.t[:, t0 + s_ * SW:t0 + (s_ + 1) * SW], krT.k)
            else:
                kv_build(lb, SW, latTs.t[:, 0:2, 0:SW], latTs.k, krTs.t[:, 0:SW], krTs.k)
        if kind == "p":
            kv_project(Wt, t0, t0 // 128, [(0, 128), (128, 128)])

        cp("act", mean.t[:, 0:Wt], mb.t[:, 0:Wt], [], [mb.k, mean.k])
        tt("pool", var.t[:, 0:Wt], mean.t[:, 0:Wt], mean.t[:, 0:Wt], ALU.mult, [mean.k], [var.k])
        tt("dve", var.t[:, 0:Wt], mb.t[:, W:W + Wt], var.t[:, 0:Wt], ALU.subtract, [], [mb.k, var.k])
        held.remove(mb)
        act(lrs.t[:, 0:Wt], var.t[:, 0:Wt], AF.Sqrt, [var.k, CONST], [lrs.k], bias=eps5.t[:, 0:1], scale=1.0)
        recip(lrs.t[:, 0:Wt], lrs.t[:, 0:Wt], [], [lrs.k])

        def ln_apply_a():
            for cc in range(4):
                df = dwf[cc // 2].t[:, (cc % 2) * W:(cc % 2) * W + Wt]
                tt("dve", df, df, mean.t[:, 0:Wt], ALU.subtract, [mean.k], [dwf[cc // 2].k])
                tt("pool", df, df, lrs.t[:, 0:Wt], ALU.mult, [lrs.k], [dwf[cc // 2].k])

        def ln_apply_b():
            for cc in range(4):
                df = dwf[cc // 2].t[:, (cc % 2) * W:(cc % 2) * W + Wt]
                act(sT.t[:, cc, 0:Wt], df, AF.Silu, [dwf[cc // 2].k, CONST], [sT.k], scale=vec.t[:, LG + cc:LG + cc + 1], bias=vec.t[:, LB + cc:LB + cc + 1])
            held_tmp.remove(dwf[0])
            held_tmp.remove(dwf[1])

        if kind == "p":
            nkb = (t0 + Wt) // 128
            blocks = []
            for kb in range(nkb):
                c0 = max(0, kb * 128 - t0)
                blocks.append((kb * 128, 128, c0, kb * 128 >= t0))
            pend = None
            for h in range(NH):
                if h == 2:
                    pend = (lambda p: (lambda: (p(), ln_apply_a())))(pend)
                pend = attention(h, 0, Wt, blocks, hook=pend)
            pend()
            ln_apply_b()
        else:
            ln_apply_a()
            ln_apply_b()
            for b in range(NB):
                for blk in range(PAST // 128):
                    lb = latbf[st["lat"] % 2]
                    st["lat"] += 1
                    r0 = b * PAST + blk * 128
                    dma("sp", lb.t[:, 0:256], clat_b[r0:r0 + 128, :], [ctok], [lb.k], lb.k)
                    dma("sp", lb.t[:, 256:288], ckr_b[r0:r0 + 128, :], [ctok], [lb.k], lb.k)
                    for rr in range(4):
                        cp("pool", lb.t[:, 288 + 32 * rr:320 + 32 * rr], lb.t[:, 256:288], [], [lb.k])
                    half = blk % 2
                    kv_build(lb, 128, latT.t[:, 0:2, half * 128:(half + 1) * 128], latT.k, krT.t[:, blk * 128:(blk + 1) * 128], krT.k)
                    if half == 1:
                        kv_project(256, (blk - 1) * 128, blk - 1, [(0, 128), (128, 128)])
                cp("pool", latT.t[:, 0:2, 0:DEC], latTs.t[:, 0:2, b * DEC:(b + 1) * DEC], [latTs.k], [latT.k])
                cp("pool", krT.t[:, PAST:PAST + DEC], krTs.t[:, b * DEC:(b + 1) * DEC], [krTs.k], [krT.k])
                kv_project(DEC, PAST, PAST // 128, [(0, DEC)])
                blocks = [(kb * 128, 128, 0, False) for kb in range(PAST // 128)] + [(PAST, DEC, 0, False)]
                pend = None
                for h in range(NH):
                    pend = attention(h, b * DEC, DEC, blocks, hook=pend)
                pend()

        dma("sp", grep.t[:, :], g_ffn.partition_broadcast(128), [], [grep.k], grep.k)

        for mp in range(4):
            gA = [gbank(), gbank()]
            held.extend(gA)
            gG = [gbank(), gbank()]
            held.extend(gG)
            wa, ka = wuse(ch["mrg"][mp][0])
            for blk in range(2):
                bs = slice(blk * 128, blk * 128 + 128)
                for h in range(NH):
                    mm(gA[blk].t[:, 0:Wt], wa[:, h, bs], oT.t[0:64, h, 0:Wt], h == 0, h == NH - 1, [ka, oT.k], [gA[blk].k])
            wdone(ch["mrg"][mp][0])
            wc, kc_ = wuse(ch["mrg"][mp][1])
            for blk in range(2):
                bs = slice(blk * 128, blk * 128 + 128)
                for cc in range(4):
                    mm(gA[blk].t[:, W:W + Wt], wc[:, cc, bs], sT.t[:, cc, 0:Wt], cc == 0, cc == 3, [kc_, sT.k], [gA[blk].k])
            wdone(ch["mrg"][mp][1])
            wgg, kgg = wuse(ch["mrg"][mp][2])
            for blk in range(2):
                bs = slice(blk * 128, blk * 128 + 128)
                for kc in range(8):
                    mm(gG[blk].t[:, 0:Wt], wgg[:, kc, 0, bs], HT.t[:, kc, 0:Wt], kc == 0, kc == 7, [kgg, HT.k], [gG[blk].k])
                for kc in range(8):
                    mm(gG[blk].t[:, W:W + Wt], wgg[:, kc, 1, bs], HT.t[:, kc, 0:Wt], kc == 0, kc == 7, [kgg, HT.k], [gG[blk].k])
            wdone(ch["mrg"][mp][2])
            for blk in range(2):
                m = 2 * mp + blk
                t1, t2 = gtmp(), gtmp()
                for o0 in (0, W):
                    act(t1.t[:, o0:o0 + Wt], gG[blk].t[:, o0:o0 + Wt], AF.Sigmoid, [], [gG[blk].k, t1.k])
                    tt("dve", t2.t[:, o0:o0 + Wt], gA[blk].t[:, o0:o0 + Wt], t1.t[:, o0:o0 + Wt], ALU.mult, [t1.k], [gA[blk].k, t2.k])
                tt("pool", mrgT.t[:, m, 0:Wt], t2.t[:, 0:Wt], t2.t[:, W:W + Wt], ALU.add, [t2.k], [mrgT.k])
            for b_ in gA + gG:
                held.remove(b_)

        wo = [wuse(c_) for c_ in ch["wo"]]
        for s in range(NS):
            for n in range(2):
                wv, wk = wo[n]
                g = gbank()
                for kc in range(8):
                    mm(g.t[:SW, :], mrgT.t[:, kc, s * SW:(s + 1) * SW], wv[:, kc, :], kc == 0, kc == 7, [wk, mrgT.k], [g.k])
                xs_ = xres[s].t[:SW, n * 512:(n + 1) * 512]
                tt("dve", xs_, xs_, g.t[:SW, :], ALU.add, [], [g.k, xres[s].k])
        wdone(*ch["wo"])

        flush_stores()
        if idx + 1 < len(tiles):
            load_x(idx + 1)

        HT.b = norm_to_HT(xres, None, NS, SW, batched=False)

        def ffn_up(q):
            hb = hidT[(q + 1) % 2]
            for c in range(4):
                wv, wk = wuse(ch["up"][q][c])
                for blk in range(2):
                    hc = 2 * c + blk
                    g = gbank()
                    for kc in range(8):
                        mm(g.t[:, 0:Wt], wv[:, kc, blk * 128:blk * 128 + 128], HT.t[:, kc, 0:Wt], kc == 0, kc == 7, [wk, HT.k], [g.k])
                    t1 = gtmp()
                    act(t1.t[:, 0:Wt], g.t[:, 0:Wt], AF.Relu, [], [g.k, t1.k])
                    tt("pool", hb.t[:, hc, 0:Wt], t1.t[:, 0:Wt], t1.t[:, 0:Wt], ALU.mult, [t1.k], [hb.k])
                wdone(ch["up"][q][c])

        def ffn_dn(q):
            hb = hidT[(q + 1) % 2]
            dns = ch["dn"][q]
            if q < 3:
                for n in range(2):
                    wv, wk = wuse(dns[n])
                    for s in range(NS):
                        g = gbank()
                        for kc in range(8):
                            mm(g.t[:SW, :], hb.t[:, kc, s * SW:(s + 1) * SW], wv[:, kc, :], kc == 0, kc == 7, [wk, hb.k], [g.k])
                        xs_ = xres[s].t[:SW, n * 512:(n + 1) * 512]
                        tt("dve", xs_, xs_, g.t[:SW, :], ALU.add, [], [g.k, xres[s].k])
                    wdone(dns[n])
            else:
                dn = [wuse(c_) for c_ in dns]
                for s in range(NS):
                    for n in range(2):
                        wv, wk = dn[n]
                        g = gbank()
                        for kc in range(8):
                            mm(g.t[:SW, :], hb.t[:, kc, s * SW:(s + 1) * SW], wv[:, kc, :], kc == 0, kc == 7, [wk, hb.k], [g.k])
                        xs_ = xres[s].t[:SW, n * 512:(n + 1) * 512]
                        tt("dve", xs_, xs_, g.t[:SW, :], ALU.add, [], [g.k, xres[s].k])
                wdone(*dns)

        nxt = idx + 1 < len(tiles)
        for i_, (kind_, q) in enumerate(FFN_ORDER):
            (ffn_up if kind_ == "u" else ffn_dn)(q)
            if i_ == 1:
                dma("sp", grep.t[:, :], (g_mix if nxt else g_ple).partition_broadcast(128), [], [grep.k], grep.k)
            if i_ == 5 and nxt:
                gn = tile_geom(idx + 1)
                st["pending_HT"] = norm_to_HT(xsets[(idx + 1) % 2], g_ple, gn["NS"], gn["SW"])

        for s in range(NS):
            tb = tbank()
            for kc in range(2):
                tr(tb.t[:, kc, 0:SW], pbf.t[:SW, s, kc * 128:(kc + 1) * 128], [pbf.k], [tb.k])
            cp("dve", ppT.t[:, 0:2, s * SW:(s + 1) * SW], tb.t[:, 0:2, 0:SW], [], [tb.k, ppT.k])
        HT.b = norm_to_HT(xres, None if nxt else g_fin, NS, SW, batched=False)
        for n in range(2):
            wg_, kg = wuse(ch["ple"][n][0])
            wp_, kp = wuse(ch["ple"][n][1])
            for s in range(NS):
                g1, g2 = gbank(), gbank()
                for kc in range(8):
                    mm(g1.t[:SW, :], HT.t[:, kc, s * SW:(s + 1) * SW], wg_[:, kc, :], kc == 0, kc == 7, [kg, HT.k], [g1.k])
                for kc in range(2):
                    mm(g2.t[:SW, :], ppT.t[:, kc, s * SW:(s + 1) * SW], wp_[:, kc, :], kc == 0, kc == 1, [kp, ppT.k], [g2.k])
                t1 = gtmp()
                act(t1.t[:SW, :], g1.t[:SW, :], AF.Sigmoid, [], [g1.k, t1.k])
                tt("dve", t1.t[:SW, :], t1.t[:SW, :], g2.t[:SW, :], ALU.mult, [], [g2.k, t1.k])
                xs_ = xres[s].t[:SW, n * 512:(n + 1) * 512]
                tt("pool", xs_, xs_, t1.t[:SW, :], ALU.add, [t1.k], [xres[s].k])
            wdone(*ch["ple"][n])

        fc = 2 * (st["ssc"] % 16)
        st["ssc"] += 1

        def finish_a():
            memset("pool", ss.t[:SW, fc:fc + NS], 0.0, [ss.k])
            for s in range(NS):
                hbf = hbfs[st["hb"] % 2]
                st["hb"] += 1
                act(hbf.t[:SW, :], xres[s].t[:SW, :], AF.Square, [xres[s].k, ss.k], [hbf.k, ss.k], scale=1.0 / 32.0, accum_out=ss.t[:SW, fc + s:fc + s + 1])
            act(rs.t[:SW, fc:fc + NS], ss.t[:SW, fc:fc + NS], AF.Sqrt, [ss.k, CONST], [rs.k], bias=eps6.t[:SW, 0:1], scale=1.0)

        def finish_b():
            recip(rs.t[:SW, fc:fc + NS], rs.t[:SW, fc:fc + NS], [], [rs.k])
            for s in range(NS):
                stt("dve", xres[s].t[:SW, :], xres[s].t[:SW, :], rs.t[:SW, fc + s:fc + s + 1], grep.t[:SW, :], ALU.mult, ALU.mult, [rs.k, grep.k], [xres[s].k])
                r0 = row0 + s * SW
                st["stores"].append((yout[r0:r0 + SW, :], xres[s].t[:SW, :], xres[s].k))
        st["finish_prev"] = (finish_a, finish_b)

    latTs = sb("latTs", [128, 2, 64], BF16)

    chs = [decl_chunks() for _ in tiles]
    dma("sp", grep.t[:, :], g_mix.partition_broadcast(128), [], [grep.k], grep.k)
    load_x(0)
    for idx in range(len(tiles)):
        run_tile(idx)
    st["finish_prev"][0]()
    st["finish_prev"][1]()
    flush_stores()
    P.emit()
    return nc


def _rope_tables(T, past, dec, nb):
    half = 16
    inv = 10000.0 ** (-np.arange(half, dtype=np.float64) / half)
    pos = np.arange(T, dtype=np.float64)
    ang = pos[:, None] * inv[None, :]
    cos_t, sin_t = np.cos(ang), np.sin(ang)
    nblk = T // 128
    cosk = cos_t.reshape(nblk, 128, 16).transpose(1, 0, 2).reshape(128, nblk * 16)
    sink = sin_t.reshape(nblk, 128, 16).transpose(1, 0, 2).reshape(128, nblk * 16)
    c4 = np.tile(np.concatenate([cos_t.T, cos_t.T], 0), (4, 1))
    s4 = np.tile(np.concatenate([-sin_t.T, sin_t.T], 0), (4, 1))
    poss = past + np.arange(dec, dtype=np.float64)
    angs = poss[:, None] * inv[None, :]
    cs, sn = np.cos(angs), np.sin(angs)
    cosks = np.tile(cs, (nb, 1)); sinks = np.tile(sn, (nb, 1))
    c4s = np.tile(np.concatenate([cosks.T, cosks.T], 0), (4, 1))
    s4s = np.tile(np.concatenate([-sinks.T, sinks.T], 0), (4, 1))
    f = lambda a: np.ascontiguousarray(a, dtype=np.float32)
    return dict(cosk=f(cosk), sink=f(sink), cos4=f(c4), sin4=f(s4), cosks=f(cosks), sinks=f(sinks), cos4s=f(c4s), sin4s=f(s4s))


def _layout_weights(inp):
    f = lambda a: np.ascontiguousarray(a, dtype=np.float32)
    w_uq = np.asarray(inp["w_uq"])[0]
    wq_n = np.zeros((QL, NH, 128), np.float32)
    for h in range(NH):
        o = 0 if h % 2 == 0 else 64
        wq_n[:, h, o:o + 64] = w_uq[:, h, 0:64]
    wq_r = np.zeros((QL, 2, NH, 32), np.float32)
    wq_r[:, 0] = w_uq[:, :, 64:96]
    wq_r[:, 1, :, 0:16] = w_uq[:, :, 80:96]
    wq_r[:, 1, :, 16:32] = w_uq[:, :, 64:80]
    conv_w = np.asarray(inp["conv_w"])[0]
    cdiag = np.zeros((4, 128, CW, 128), np.float32)
    idx = np.arange(128)
    for cc in range(4):
        for j in range(CW):
            cdiag[cc, idx, j, idx] = conv_w[j, cc * 128:(cc + 1) * 128]
    vecs = np.zeros((128, 24), np.float32)
    vecs[:, 0:6] = np.asarray(inp["q_norm_g"])[0].reshape(6, 128).T
    vecs[:, 6:10] = np.asarray(inp["conv_b"])[0].reshape(4, 128).T
    vecs[:, 10:14] = np.asarray(inp["conv_ln_g"])[0].reshape(4, 128).T
    vecs[:, 14:18] = np.asarray(inp["conv_ln_b"])[0].reshape(4, 128).T
    for j in range(4):
        vecs[32 * j:32 * j + 32, 18 + j] = 1.0
    return dict(
        w_in=f(np.asarray(inp["w_in"])[0]), wq_n=f(wq_n.reshape(QL, 1024)), wq_r=f(wq_r.reshape(QL, 512)),
        w_uk=f(np.asarray(inp["w_uk"])[0].reshape(KVL, 512)), w_uv=f(np.asarray(inp["w_uv"])[0].reshape(KVL, 512)),
        w_ao=f(np.asarray(inp["w_attn_out"])[0]), cdiag=f(cdiag.reshape(4 * 128, CW * 128)), w_co=f(np.asarray(inp["w_conv_out"])[0]),
        w_out=f(np.asarray(inp["w_out"])[0]), w_up=f(np.asarray(inp["w_ff_up"])[0]), w_dn=f(np.asarray(inp["w_ff_down"])[0]),
        w_pg=f(np.asarray(inp["w_ple_gate"])[0]), w_pp=f(np.asarray(inp["w_ple_proj"])[0]),
        g_mix=f(np.asarray(inp["norm_mix_g"])[0]), g_ffn=f(np.asarray(inp["norm_ffn_g"])[0]), g_ple=f(np.asarray(inp["ple_norm_g"])[0]),
        g_fin=f(np.asarray(inp["final_norm_g"])), g_kv=f(np.asarray(inp["kv_norm_g"])[0]), vecs=vecs)


_PROG_CACHE = {}


def run_cores(inp, n_cores):
    x_prompt = np.asarray(inp["x_prompt"]); x_sample = np.asarray(inp["x_sample"])
    BATCH, T, _ = x_prompt.shape
    DECB, DEC, _ = x_sample.shape
    PAST = np.asarray(inp["cache_kv_latent"]).shape[2]
    NSEQ = BATCH // n_cores
    NB = DECB // n_cores
    key = (T, NSEQ, NB, PAST, DEC)
    if key not in _PROG_CACHE:
        _PROG_CACHE[key] = build_program(*key)
    nc = _PROG_CACHE[key]
    shared = _layout_weights(inp)
    shared.update(_rope_tables(T, PAST, DEC, NB))
    f = lambda a: np.ascontiguousarray(a, dtype=np.float32)
    p_prompt = np.asarray(inp["p_prompt"])[0]; p_sample = np.asarray(inp["p_sample"])[0]
    clat = np.asarray(inp["cache_kv_latent"])[0]; ckr = np.asarray(inp["cache_k_rope"])[0]; ccv = np.asarray(inp["cache_conv"])[0]
    in_maps = []
    for c in range(n_cores):
        sq = slice(c * NSEQ, (c + 1) * NSEQ)
        bq = slice(c * NB, (c + 1) * NB)
        m = dict(shared)
        m.update(xp=f(x_prompt[sq].reshape(NSEQ * T, D)), xs=f(x_sample[bq].reshape(NB * DEC, D)),
                 pp=f(p_prompt[sq].reshape(NSEQ * T, PLE)), psd=f(p_sample[bq].reshape(NB * DEC, PLE)),
                 clat=f(clat[bq].reshape(NB * PAST, KVL)), ckr=f(ckr[bq].reshape(NB * PAST, 32)), ccv=f(ccv[bq]))
        in_maps.append(m)
    res = run_bass_kernel_spmd(nc, in_maps, core_ids=list(range(n_cores)))
    R = res.results
    cat = lambda k, shp: np.concatenate([np.asarray(r[k], dtype=np.float32).reshape(shp) for r in R], axis=0)
    y_p = cat("y_p", (NSEQ, T, D)); y_s = cat("y_s", (NB, DEC, D))
    lat_p = cat("lat_p", (NSEQ, T, KVL))[None]; kr_p = cat("kr_p", (NSEQ, T, 32))[None]; cv_p = cat("cv_p", (NSEQ, HIST, CC))[None]
    lat_s = cat("lat_s", (NB, DEC, KVL))[None]; kr_s = cat("kr_s", (NB, DEC, 32))[None]; cv_s = cat("cv_s", (NB, HIST, CC))[None]
    return (y_p, y_s, lat_p, kr_p, cv_p, lat_s, kr_s, cv_s)


def kernel(**inputs):
    return run_cores(inputs, NCORES)
```

```python
import numpy as np
import concourse.bass as bass
import concourse.mybir as mybir
from concourse.bass_utils import run_bass_kernel_spmd

F32 = mybir.dt.float32
BF16 = mybir.dt.bfloat16
AF = mybir.ActivationFunctionType
ALU = mybir.AluOpType

D = 1024
NH = 8
KVL = 256
QL = 768
CC = 512
CW = 31
HIST = CW - 1
DFF = 4096
PLE = 256
OFF_KV = 768
OFF_CONV = 1056
OFF_GATE = 2080
IN_W = 4128
SM_SCALE = 96.0 ** -0.5
NCORES = 8


class Tok:
    __slots__ = ("last_w", "readers", "sem")

    def __init__(self):
        self.last_w = None
        self.readers = []
        self.sem = None


class Prog:
    ENGS = ("pe", "act", "dve", "pool", "sp")

    def __init__(self, nc):
        self.nc = nc
        self.ops = {e: [] for e in self.ENGS}
        self.dma_cnt = {}
        self.nsem = 0

    def _tok_sem(self, t):
        if t.sem is None:
            t.sem = self.nc.alloc_semaphore(f"d{self.nsem}")
            self.nsem += 1
            self.dma_cnt[t.sem] = 0
        return t.sem

    def op(self, eng, fn, r=(), w=(), dma=None):
        idx = len(self.ops[eng])
        waits = {}

        def need(dep):
            if dep is not None and waits.get(dep[0], 0) < dep[1]:
                waits[dep[0]] = dep[1]

        for t in r:
            need(t.last_w)
        for t in w:
            need(t.last_w)
            for d in t.readers:
                need(d)
        if dma is not None:
            s = self._tok_sem(dma)
            self.dma_cnt[s] += 16
            mydep = (s, self.dma_cnt[s])
        else:
            mydep = (eng, idx + 1)
        for t in r:
            if len(t.readers) > 64:
                m = {}
                for k, v in t.readers:
                    if m.get(k, 0) < v:
                        m[k] = v
                t.readers = list(m.items())
            t.readers.append(mydep)
        for t in w:
            t.last_w = mydep
            t.readers = []
        self.ops[eng].append((fn, waits, self._tok_sem(dma) if dma is not None else None))

    def emit(self):
        nc = self.nc
        signaled = {e: set() for e in self.ENGS}
        plan = {}
        for e in self.ENGS:
            known = {}
            lst = []
            for (fn, waits, dsem) in self.ops[e]:
                ws = []
                for k, v in waits.items():
                    if k == e and e in ("pe", "sp"):
                        continue
                    if known.get(k, 0) >= v:
                        continue
                    known[k] = v
                    ws.append((k, v))
                    if isinstance(k, str):
                        signaled[k].add(v)
                lst.append(ws)
            plan[e] = lst
        semval = {}
        for e in self.ENGS:
            m = {}
            for c, v in enumerate(sorted(signaled[e])):
                m[v] = c + 1
            semval[e] = m
        esem = {e: nc.alloc_semaphore(f"e_{e}") for e in self.ENGS}
        final_waits = list(self.dma_cnt.items())
        engmap = {"pe": "tensor", "act": "scalar", "dve": "vector", "pool": "gpsimd", "sp": "sync"}
        with nc.Block() as block:
            for e in self.ENGS:
                def body(eng, e=e):
                    for i, (fn, waits, dsem) in enumerate(self.ops[e]):
                        for k, v in plan[e][i]:
                            if isinstance(k, str):
                                eng.wait_ge(esem[k], semval[k][v])
                            else:
                                eng.wait_ge(k, v)
                        ins = fn(eng)
                        if dsem is not None:
                            ins.then_inc(dsem, 16)
                        elif (i + 1) in semval[e]:
                            ins.then_inc(esem[e], 1)
                    if e == "sp":
                        for s, c in final_waits:
                            eng.wait_ge(s, c)
                getattr(block, engmap[e])(body)


class B:
    def __init__(self, t):
        self.t = t
        self.k = Tok()


def build_program(T, NSEQ, NB, PAST, DEC):
    W = 256
    NT = T // W
    NSAMP = NB * DEC
    KMAX = max(T, PAST + DEC)
    NBLK = (KMAX + 127) // 128
    nc = bass.Bass("TRN2", target_bir_lowering=False)
    P = Prog(nc)

    def din(name, shape, dt=F32):
        return nc.dram_tensor(name, list(shape), dt, kind="ExternalInput").ap()

    def dout(name, shape):
        return nc.dram_tensor(name, list(shape), F32, kind="ExternalOutput").ap()

    def dscr(name, shape):
        return nc.dram_tensor(name, list(shape), BF16, kind="Internal").ap()

    xp = din("xp", [NSEQ * T, D])
    xs = din("xs", [NSAMP, D])
    pp = din("pp", [NSEQ * T, PLE])
    psd = din("psd", [NSAMP, PLE])
    clat = din("clat", [NB * PAST, KVL])
    ckr = din("ckr", [NB * PAST, 32])
    ccv = din("ccv", [NB, HIST, CC])
    wdefs = dict(w_in=[D, IN_W], wq_n=[QL, 1024], wq_r=[QL, 512], w_uk=[KVL, 512], w_uv=[KVL, 512],
                 w_ao=[512, D], cdiag=[4 * 128, CW * 128], w_co=[CC, D], w_out=[D, D], w_up=[D, DFF],
                 w_dn=[DFF, D], w_pg=[D, D], w_pp=[PLE, D])
    wf = {k: din(k, v) for k, v in wdefs.items()}
    wb = {k: dscr(k + "_b", v) for k, v in wdefs.items()}
    wtok = {k: Tok() for k in wdefs}
    clat_b = dscr("clat_b", [max(1, NB * PAST), KVL]); ckr_b = dscr("ckr_b", [max(1, NB * PAST), 32])
    ctok = Tok()
    g_mix = din("g_mix", [D]); g_ffn = din("g_ffn", [D]); g_ple = din("g_ple", [D]); g_fin = din("g_fin", [D])
    g_kv = din("g_kv", [KVL])
    vecs = din("vecs", [128, 24])
    cosk_d = din("cosk", [128, (T // 128) * 16]); sink_d = din("sink", [128, (T // 128) * 16])
    cosks_d = din("cosks", [NSAMP, 16]); sinks_d = din("sinks", [NSAMP, 16])
    cos4_d = din("cos4", [128, T]); sin4_d = din("sin4", [128, T])
    cos4s_d = din("cos4s", [128, NSAMP]); sin4s_d = din("sin4s", [128, NSAMP])

    y_p = dout("y_p", [NSEQ * T, D]); y_s = dout("y_s", [NSAMP, D])
    lat_p = dout("lat_p", [NSEQ * T, KVL]); kr_p = dout("kr_p", [NSEQ * T, 32]); cv_p = dout("cv_p", [NSEQ, HIST, CC])
    lat_s = dout("lat_s", [NSAMP, KVL]); kr_s = dout("kr_s", [NSAMP, 32]); cv_s = dout("cv_s", [NB, HIST, CC])

    def sb(name, shape, dt):
        return B(nc.alloc_sbuf_tensor("s_" + name, list(shape), dt))

    Kc = sb("Kc", [128, 4, NBLK * 128], BF16)
    krT = sb("krT", [128, NBLK * 128], BF16)
    Vc = sb("Vc", [128, NBLK, NH, 65], BF16)
    ident = sb("ident", [128, 128], BF16)
    identf = sb("identf", [128, 128], F32)
    onesb = sb("onesb", [128, 128], BF16)
    ones512 = sb("ones512", [128, 128], BF16)
    onesf = sb("onesf", [128, 64], F32)
    eps6 = sb("eps6", [128, 1], F32)
    eps5 = sb("eps5", [128, 1], F32)
    vec = sb("vec", [128, 24], F32)
    cosk = sb("cosk", [128, (T // 128) * 16], F32); sink = sb("sink", [128, (T // 128) * 16], F32)
    cosks = sb("cosks", [128, 16], F32); sinks = sb("sinks", [128, 16], F32)
    gkv = sb("gkv", [128, KVL], F32)
    wuk = sb("wuk", [128, 2, 512], BF16); wuv = sb("wuv", [128, 2, 512], BF16)
    CONST = Tok()
    xsets = [[sb(f"xres{a}_{s}", [128, D], F32) for s in range(2)] for a in range(2)]
    grep = sb("grep", [128, D], F32)
    hbfs = [sb(f"hbf{i}", [128, D], BF16) for i in range(2)]
    HTs = [sb(f"HT{i}", [128, 8, W], BF16) for i in range(2)]

    class _Cur:
        b = None
        t = property(lambda self: self.b.t)
        k = property(lambda self: self.b.k)
    HT = _Cur()
    cqT = sb("cqT", [128, 6, W], BF16)
    sqT = [sb(f"sqT{i}", [128, W], BF16) for i in range(2)]
    rq = sb("rq", [128, W], F32)
    cos4 = sb("cos4t", [128, W], F32); sin4 = sb("sin4t", [128, W], F32)
    latst = [sb(f"latst{i}", [128, 288], F32) for i in range(2)]
    latbf = [sb(f"latbf{i}", [128, 416], BF16) for i in range(2)]
    ropet = sb("ropet", [128, 64], F32)
    latT = sb("latT", [128, 2, W], BF16)
    krTs = sb("krTs", [128, 64], BF16)
    QT = sb("QT", [128, NH, W], BF16)
    QrT = sb("QrT", [128, NH, W], BF16)
    PT = [sb(f"PT{i}", [128, 2 * W], BF16) for i in range(3)]
    oT = sb("oT", [128, NH, W], BF16)
    rdn = sb("rdn", [128, W], F32)
    rrep = sb("rrep", [128, W], F32)
    USEG = HIST + 16
    ubuf = sb("ubuf", [128, 4, HIST + W], BF16)
    dwbf = sb("dwbf", [128, 4, W], BF16)
    dsq = sb("dsq", [128, 4, W], BF16)
    mean = sb("mean", [128, W], F32); var = sb("var", [128, W], F32); lrs = sb("lrs", [128, W], F32)
    tmp = [sb(f"tmp{i}", [128, 512], F32) for i in range(4)]
    hidT = [sb(f"hidT{i}", [128, 8, W], BF16) for i in range(2)]
    sT = cqT
    mrgT = hidT[0]
    pbf = sb("pbf", [128, 2, PLE], BF16)
    ppT = sb("ppT", [128, 2, W], BF16)
    ss = sb("ss", [128, 32], F32)
    rs = sb("rs", [128, 32], F32)
    NSLOT = 4
    wslot = [sb(f"wslot{i}", [128, 4096], BF16) for i in range(NSLOT)]
    pTb = [B(nc.alloc_psum_tensor(f"pT{i}", [128, 8, 128], BF16)) for i in range(2)]
    pS = [B(nc.alloc_psum_tensor(f"pS{i}", [128, 512], F32)) for i in range(3)]
    pO = [B(nc.alloc_psum_tensor(f"pO{i}", [128, 512], F32)) for i in range(2)]
    pG = [B(nc.alloc_psum_tensor(f"pG{i}", [128, 512], F32)) for i in range(1)]
    gen_pool = [pG[0], pS[0], pS[1], pS[2], pO[0], pO[1]]
    st = dict(g=0, t=0, tmp=0, ssc=0, pt=0, lat=0, hb=0, ht=0, pending_HT=None, finish_prev=None, stores=[])
    held = []

    def gbank():
        while True:
            b = gen_pool[st["g"] % len(gen_pool)]
            st["g"] += 1
            if b not in held:
                return b

    def tbank():
        b = pTb[st["t"] % 2]
        st["t"] += 1
        return b

    held_tmp = []

    def gtmp():
        while True:
            b = tmp[st["tmp"] % 4]
            st["tmp"] += 1
            if b not in held_tmp:
                return b

    def sscol():
        c = 2 * (st["ssc"] % 16)
        st["ssc"] += 1
        return c

    def mm(out, lhsT, rhs, start, stop, r, w, **kw):
        P.op("pe", lambda e: e.matmul(out, lhsT=lhsT, rhs=rhs, start=start, stop=stop, **kw), r=r, w=w)

    def tr(out, in_, r, w):
        P.op("pe", lambda e: e.transpose(out=out, in_=in_, identity=ident.t[:in_.shape[0], :in_.shape[0]]), r=list(r) + [CONST], w=w)

    def act(out, in_, func, r, w, **kw):
        P.op("act", lambda e: e.activation(out=out, in_=in_, func=func, **kw), r=r, w=w)

    def tt(eng, out, in0, in1, op, r, w):
        P.op(eng, lambda e: e.tensor_tensor(out=out, in0=in0, in1=in1, op=op), r=r, w=w)

    def stt(eng, out, in0, scalar, in1, op0, op1, r, w):
        P.op(eng, lambda e: e.scalar_tensor_tensor(out=out, in0=in0, scalar=scalar, in1=in1, op0=op0, op1=op1), r=r, w=w)

    def ts(eng, out, in0, s1, op0, r, w):
        P.op(eng, lambda e: e.tensor_scalar(out=out, in0=in0, scalar1=s1, scalar2=None, op0=op0), r=r, w=w)

    def cp(eng, out, in_, r, w):
        if eng == "act":
            P.op("act", lambda e: e.activation(out=out, in_=in_, func=AF.Copy), r=r, w=w)
        else:
            P.op(eng, lambda e: e.tensor_copy(out=out, in_=in_), r=r, w=w)

    def recip(out, in_, r, w):
        P.op("dve", lambda e: e.reciprocal(out=out, in_=in_), r=r, w=w)

    def memset(eng, ap, val, w):
        P.op(eng, lambda e: e.memset(ap, val), w=w)

    def dma(q, out, in_, r, w, tok):
        P.op(q, lambda e: e.dma_start(out=out, in_=in_), r=r, w=w, dma=tok)

    for name in ("w_in", "wq_n", "wq_r", "w_uk", "w_uv", "cdiag", "w_ao", "w_co", "w_out", "w_up", "w_dn", "w_pg", "w_pp"):
        rows = wdefs[name][0]
        for r0 in range(0, rows, 128):
            r1 = min(rows, r0 + 128)
            dma("pool", wb[name][r0:r1, :], wf[name][r0:r1, :], [], [wtok[name]], wtok[name])
    for r0 in range(0, NB * PAST, 1024):
        r1 = min(NB * PAST, r0 + 1024)
        dma("pool", clat_b[r0:r1, :], clat[r0:r1, :], [], [ctok], ctok)
    for r0 in range(0, NB * PAST, 4096):
        r1 = min(NB * PAST, r0 + 4096)
        dma("pool", ckr_b[r0:r1, :], ckr[r0:r1, :], [], [ctok], ctok)
    dma("sp", vec.t[:, :], vecs[:, :], [], [CONST], CONST)
    dma("sp", cosk.t[:, :], cosk_d[:, :], [], [CONST], CONST)
    dma("sp", sink.t[:, :], sink_d[:, :], [], [CONST], CONST)
    if NSAMP:
        dma("sp", cosks.t[:NSAMP, :], cosks_d[:, :], [], [CONST], CONST)
        dma("sp", sinks.t[:NSAMP, :], sinks_d[:, :], [], [CONST], CONST)
    dma("sp", gkv.t[:, :], g_kv.partition_broadcast(128), [], [CONST], CONST)
    dma("sp", wuk.t[:, :, :], wb["w_uk"].rearrange("(kc p) n -> p kc n", p=128), [wtok["w_uk"]], [CONST], CONST)
    dma("sp", wuv.t[:, :, :], wb["w_uv"].rearrange("(kc p) n -> p kc n", p=128), [wtok["w_uv"]], [CONST], CONST)
    memset("pool", identf.t[:, :], 0.0, [CONST])
    P.op("pool", lambda e: e.affine_select(out=identf.t[:, :], in_=identf.t[:, :], pattern=[[-1, 128]], compare_op=ALU.not_equal,
                                           fill=1.0, base=0, channel_multiplier=1), w=[CONST])
    cp("pool", ident.t[:, :], identf.t[:, :], [], [CONST])
    memset("pool", onesb.t[:, :], 1.0, [CONST])
    memset("pool", ones512.t[:, :], 1.0 / 512.0, [CONST])
    memset("pool", onesf.t[:, :], 1.0, [CONST])
    memset("pool", eps6.t[:, :], 1e-6, [CONST])
    memset("pool", eps5.t[:, :], 1e-5, [CONST])
    memset("pool", Vc.t[:, :, :, 64:65], 1.0, [Vc.k])
    memset("pool", ss.t[:, :], 0.0, [ss.k])
    GQ, CB, LG, LB, MK = 0, 6, 10, 14, 18

    wq = []
    wstate = dict(issued=0)
    released = set()

    def slot_view(j):
        name, src, shp = wq[j]
        slot = wslot[j % NSLOT]
        n = 1
        for d in shp[1:]:
            n *= d
        v = slot.t[:shp[0], 0:n]
        if len(shp) == 3:
            v = v.rearrange("p (a b) -> p a b", a=shp[1])
        elif len(shp) == 4:
            v = v.rearrange("p (a b c) -> p a b c", a=shp[1], b=shp[2])
        return v, slot

    def pump():
        while wstate["issued"] < len(wq) and (wstate["issued"] < NSLOT or (wstate["issued"] - NSLOT) in released):
            j = wstate["issued"]
            v, slot = slot_view(j)
            if len(wq[j][2]) == 4:
                for g_ in range(wq[j][2][2]):
                    dma("sp", v[:, :, g_, :], wq[j][1][:, :, g_, :], [wtok[wq[j][0]]], [slot.k], slot.k)
            else:
                dma("sp", v, wq[j][1], [wtok[wq[j][0]]], [slot.k], slot.k)
            wstate["issued"] += 1

    def wget(name, src, shp):
        wq.append((name, src, shp))
        return len(wq) - 1

    def wuse(j):
        pump()
        assert wstate["issued"] > j, "weight slot pipeline stuck"
        v, slot = slot_view(j)
        return v, slot.k

    def wdone(*js):
        for j in js:
            released.add(j)
        pump()

    def kmaj(name, k0, k1, c0, c1):
        return wb[name][k0:k1, c0:c1].rearrange("(kc p) n -> p kc n", p=128), [128, (k1 - k0) // 128, c1 - c0]

    def rstd_from(ssap, sw, epsb):
        c = sscol()
        o = rs.t[:sw, c:c + 1]
        act(o, ssap, AF.Sqrt, [ss.k, CONST], [rs.k], bias=epsb.t[:sw, 0:1], scale=1.0)
        recip(o, o, [rs.k], [rs.k])
        return o

    def norm_to_HT(xres, next_gd, NS, SW, batched=True, phase=None, carry=None):
        if phase == 2:
            HTn, hb_ = carry
        else:
            HTn = HTs[st["ht"] % 2]
            st["ht"] += 1
            c = 2 * (st["ssc"] % 16)
            st["ssc"] += 1
            memset("pool", ss.t[:SW, c:c + NS], 0.0, [ss.k])
            hb_ = []
        for s in range(NS if phase != 2 else 0):
            hbf = hbfs[st["hb"] % 2]
            st["hb"] += 1
            hb_.append(hbf)
            act(hbf.t[:SW, :], xres[s].t[:SW, :], AF.Square, [xres[s].k, ss.k], [hbf.k, ss.k], scale=1.0 / 32.0, accum_out=ss.t[:SW, c + s:c + s + 1])
            if not batched:
                act(rs.t[:SW, c + s:c + s + 1], ss.t[:SW, c + s:c + s + 1], AF.Sqrt, [ss.k, CONST], [rs.k], bias=eps6.t[:SW, 0:1], scale=1.0)
                recip(rs.t[:SW, c + s:c + s + 1], rs.t[:SW, c + s:c + s + 1], [], [rs.k])
                stt("dve", hbf.t[:SW, :], xres[s].t[:SW, :], rs.t[:SW, c + s:c + s + 1], grep.t[:SW, :], ALU.mult, ALU.mult, [xres[s].k, rs.k, grep.k], [hbf.k])
        if batched and phase != 2:
            act(rs.t[:SW, c:c + NS], ss.t[:SW, c:c + NS], AF.Sqrt, [ss.k, CONST], [rs.k], bias=eps6.t[:SW, 0:1], scale=1.0)
            recip(rs.t[:SW, c:c + NS], rs.t[:SW, c:c + NS], [], [rs.k])
            for s in range(NS):
                hbf = hb_[s]
                stt("dve", hbf.t[:SW, :], xres[s].t[:SW, :], rs.t[:SW, c + s:c + s + 1], grep.t[:SW, :], ALU.mult, ALU.mult, [xres[s].k, rs.k, grep.k], [hbf.k])
        if phase == 1:
            return (HTn, hb_)
        for s in range(NS):
            hbf = hb_[s]
            tb = tbank()
            for kc in range(8):
                tr(tb.t[:, kc, 0:SW], hbf.t[:SW, kc * 128:(kc + 1) * 128], [hbf.k], [tb.k])
            cp("dve" if s % 2 == 0 else "act", HTn.t[:, :, s * SW:(s + 1) * SW], tb.t[:, :, 0:SW], [], [tb.k, HTn.k])
        if next_gd is not None:
            dma("sp", grep.t[:, :], next_gd.partition_broadcast(128), [], [grep.k], grep.k)
        return HTn

    def kv_build(lb, SW, latT_dst, lat_tok, kr_dst, kr_tok):
        tb = tbank()
        tr(tb.t[:, 0, 0:SW], lb.t[:SW, 0:128], [lb.k], [tb.k])
        tr(tb.t[:, 1, 0:SW], lb.t[:SW, 128:256], [lb.k], [tb.k])
        tr(tb.t[:, 2, 0:SW], lb.t[:SW, 288:416], [lb.k], [tb.k])
        cp("dve", latT_dst, tb.t[:, 0:2, 0:SW], [], [tb.k, lat_tok])
        cp("act", kr_dst, tb.t[:, 2, 0:SW], [], [tb.k, kr_tok])

    def kv_project(wcols, kcol0, vblk0, vrows_list):
        for pr in range(4):
            g = gbank()
            for kc in range(2):
                mm(g.t[:, 0:wcols], wuk.t[:, kc, pr * 128:(pr + 1) * 128], latT.t[:, kc, 0:wcols], kc == 0, kc == 1, [CONST, latT.k], [g.k])
            cp("act" if pr % 2 else "dve", Kc.t[:, pr, kcol0:kcol0 + wcols], g.t[:, 0:wcols], [], [g.k, Kc.k])
        for i, (c0, n) in enumerate(vrows_list):
            g = gbank()
            for kc in range(2):
                mm(g.t[:n, :], latT.t[:, kc, c0:c0 + n], wuv.t[:, kc, :], kc == 0, kc == 1, [CONST, latT.k], [g.k])
            cp("dve" if i % 2 else "act", Vc.t[:n, vblk0 + i, :, 0:64], g.t[:n, :].rearrange("p (h d) -> p h d", h=NH), [], [g.k, Vc.k])

    def attention(h, qc0, qn, blocks, hook=None):
        pr = h // 2
        ob = pO[h % 2]
        nb = len(blocks)
        groups = [blocks[i:i + 2] for i in range(0, nb, 2)]
        sbank = {}

        def S(gi):
            sbk = pS[gi % 3]
            sbank[gi] = sbk
            for j, (kcol, nk, c0, dg) in enumerate(groups[gi]):
                o = sbk.t[:nk, j * W + c0:j * W + qn]
                mm(o, Kc.t[:, pr, kcol:kcol + nk], QT.t[:, h, qc0 + c0:qc0 + qn], True, False, [Kc.k, QT.k], [sbk.k])
                mm(o, krT.t[:, kcol:kcol + nk], QrT.t[:, h, qc0 + c0:qc0 + qn], False, True, [krT.k, QrT.k], [sbk.k])

        S(0)
        if len(groups) > 1:
            S(1)
        for gi in range(len(groups)):
            if gi + 2 < len(groups):
                S(gi + 2)
            if gi == min(2, len(groups) - 1) and hook is not None:
                hook()
            sbk = sbank[gi]
            pt = PT[st["pt"] % 3]
            st["pt"] += 1
            grp = groups[gi]
            full = all(c0 == 0 and nk == 128 and not dg for (_, nk, c0, dg) in grp) and len(grp) == 2 and qn == W
            if full:
                act(pt.t[:, 0:2 * W], sbk.t[:, 0:2 * W], AF.Exp, [], [sbk.k, pt.k], scale=SM_SCALE)
            else:
                for j, (kcol, nk, c0, dg) in enumerate(grp):
                    act(pt.t[:nk, j * W + c0:j * W + qn], sbk.t[:nk, j * W + c0:j * W + qn], AF.Exp, [], [sbk.k, pt.k], scale=SM_SCALE)
                    if dg:
                        memset("pool", pt.t[64:128, j * W + c0:j * W + c0 + 64], 0.0, [pt.k])
            for j, (kcol, nk, c0, dg) in enumerate(grp):
                bi = gi * 2 + j
                mm(ob.t[0:65, c0:qn], Vc.t[:nk, kcol // 128, h, 0:65], pt.t[:nk, j * W + c0:j * W + qn], bi == 0, bi == nb - 1, [Vc.k, pt.k], [ob.k])
        recip(rdn.t[64:65, 0:qn], ob.t[64:65, 0:qn], [], [ob.k, rdn.k])

        def norm():
            g = pG[0]
            mm(g.t[0:64, 0:qn], onesf.t[64:65, 0:64], rdn.t[64:65, 0:qn], True, True, [CONST, rdn.k], [g.k])
            cp("act", rrep.t[0:64, 0:qn], g.t[0:64, 0:qn], [], [g.k, rrep.k])
            tt("dve", oT.t[0:64, h, qc0:qc0 + qn], ob.t[0:64, 0:qn], rrep.t[0:64, 0:qn], ALU.mult, [rrep.k], [ob.k, oT.k])
        return norm

    FFN_ORDER = [("u", 0), ("u", 1), ("d", 0), ("u", 2), ("d", 1), ("u", 3), ("d", 2), ("d", 3)]

    def decl_chunks():
        ch = {}
        ch["cq"] = [wget("w_in", *kmaj("w_in", 0, D, 256 * c, 256 * c + 256)) for c in range(3)]
        ch["ckv"] = wget("w_in", *kmaj("w_in", 0, D, OFF_KV, OFF_KV + 288))
        ch["u"] = [wget("w_in", *kmaj("w_in", 0, D, OFF_CONV + 256 * c, OFF_CONV + 256 * c + 256)) for c in (0, 2, 1, 3)]
        ch["cd"] = [wget("cdiag", wb["cdiag"][128 * c:128 * c + 128, :], [128, CW * 128]) for c in range(4)]
        ch["qn"] = [wget("wq_n", *kmaj("wq_n", 0, QL, 512 * c, 512 * c + 512)) for c in range(2)]
        ch["qr"] = wget("wq_r", *kmaj("wq_r", 0, QL, 0, 512))
        ch["mrg"] = []
        for mp in range(4):
            c0 = 256 * mp
            a = wget("w_ao", wb["w_ao"][:, c0:c0 + 256].rearrange("(h d) n -> d h n", d=64), [64, NH, 256])
            b_ = wget("w_co", *kmaj("w_co", 0, CC, c0, c0 + 256))
            gg = wget("w_in", wb["w_in"][:, OFF_GATE:OFF_GATE + 2 * D].rearrange("(kc p) (g n) -> p kc g n", p=128, g=2)[:, :, :, c0:c0 + 256],
                      [128, 8, 2, 256])
            ch["mrg"].append((a, b_, gg))
        ch["wo"] = [wget("w_out", *kmaj("w_out", 0, D, 512 * n, 512 * n + 512)) for n in range(2)]
        ch["up"], ch["dn"] = {}, {}
        for kind_, q in FFN_ORDER:
            if kind_ == "u":
                ch["up"][q] = [wget("w_up", *kmaj("w_up", 0, D, q * 1024 + 256 * c, q * 1024 + 256 * c + 256)) for c in range(4)]
            else:
                ch["dn"][q] = [wget("w_dn", *kmaj("w_dn", q * 1024, (q + 1) * 1024, 512 * n, 512 * n + 512)) for n in range(2)]
        ch["ple"] = [(wget("w_pg", *kmaj("w_pg", 0, D, 512 * n, 512 * n + 512)),
                      wget("w_pp", *kmaj("w_pp", 0, PLE, 512 * n, 512 * n + 512))) for n in range(2)]
        return ch

    tiles = [("p", seq, ti) for seq in range(NSEQ) for ti in range(NT)] + ([("s", 0, 0)] if NSAMP else [])

    def tile_geom(idx):
        kind, seq, ti = tiles[idx]
        if kind == "p":
            return dict(kind=kind, seq=seq, ti=ti, NS=2, SW=128, Wt=W, t0=ti * W, row0=seq * T + ti * W, xin=xp, pin=pp)
        return dict(kind=kind, seq=0, ti=0, NS=1, SW=NSAMP, Wt=NSAMP, t0=0, row0=0, xin=xs, pin=psd)

    def load_x(idx):
        gm = tile_geom(idx)
        xr = xsets[idx % 2]
        for s in range(gm["NS"]):
            r0 = gm["row0"] + s * gm["SW"]
            dma("sp", xr[s].t[:gm["SW"], :], gm["xin"][r0:r0 + gm["SW"], :], [], [xr[s].k], xr[s].k)

    def flush_stores():
        for dst, src, tok in st["stores"]:
            dma("sp", dst, src, [tok], [], tok)
        st["stores"] = []

    def run_tile(idx):
        gm = tile_geom(idx)
        kind, seq, ti, NS, SW, Wt, t0, row0 = (gm[k] for k in ("kind", "seq", "ti", "NS", "SW", "Wt", "t0", "row0"))
        pin = gm["pin"]
        if kind == "p":
            yout, latout, krout = y_p, lat_p, kr_p
            last = (ti == NT - 1)
        else:
            yout, latout, krout = y_s, lat_s, kr_s
            last = True
        xres = xsets[idx % 2]
        ch = chs.pop(0)

        for s in range(NS):
            dma("pool", pbf.t[:SW, s, :], pin[row0 + s * SW:row0 + (s + 1) * SW, :], [], [pbf.k], pbf.k)
        if kind == "p":
            dma("sp", cos4.t[:, 0:Wt], cos4_d[:, t0:t0 + Wt], [], [cos4.k], cos4.k)
            dma("sp", sin4.t[:, 0:Wt], sin4_d[:, t0:t0 + Wt], [], [sin4.k], sin4.k)
        else:
            dma("sp", cos4.t[:, 0:Wt], cos4s_d[:, :], [], [cos4.k], cos4.k)
            dma("sp", sin4.t[:, 0:Wt], sin4s_d[:, :], [], [sin4.k], sin4.k)

        if st["pending_HT"] is not None:
            HT.b = st["pending_HT"]
            st["pending_HT"] = None
        else:
            HT.b = norm_to_HT(xres, None, NS, SW)

        sbk = gbank()
        held.append(sbk)
        prev = None
        for m in range(6):
            wv, wk = wuse(ch["cq"][m // 2])
            g = gbank()
            for kc in range(8):
                mm(g.t[:, 0:Wt], wv[:, kc, (m % 2) * 128:(m % 2) * 128 + 128], HT.t[:, kc, 0:Wt], kc == 0, kc == 7, [wk, HT.k], [g.k])
            sq = sqT[m % 2]
            act(sq.t[:, 0:Wt], g.t[:, 0:Wt], AF.Square, [], [g.k, sq.k])
            ts("dve", cqT.t[:, m, 0:Wt], g.t[:, 0:Wt], vec.t[:, GQ + m:GQ + m + 1], ALU.mult, [CONST], [g.k, cqT.k])
            if prev is not None:
                mm(sbk.t[:, 0:Wt], onesb.t[:, :], prev[0].t[:, 0:Wt], prev[1] == 0, False, [CONST, prev[0].k], [sbk.k])
            prev = (sq, m)
            if m % 2 == 1:
                wdone(ch["cq"][m // 2])

        dma("sp", grep.t[:, :], g_fin.partition_broadcast(128), [], [grep.k], grep.k)

        wv, wk = wuse(ch["ckv"])
        lbs = []
        for s in range(NS):
            g = gbank()
            for kc in range(8):
                mm(g.t[:SW, 0:288], HT.t[:, kc, s * SW:(s + 1) * SW], wv[:, kc, :], kc == 0, kc == 7, [wk, HT.k], [g.k])
            if s == 0:
                mm(sbk.t[:, 0:Wt], onesb.t[:, :], prev[0].t[:, 0:Wt], False, True, [CONST, prev[0].k], [sbk.k])
                act(rq.t[:, 0:Wt], sbk.t[:, 0:Wt], AF.Sqrt, [CONST], [sbk.k, rq.k], bias=eps6.t[:, 0:1], scale=1.0 / QL)
                held.remove(sbk)
                recip(rq.t[:, 0:Wt], rq.t[:, 0:Wt], [], [rq.k])
                tt("dve", cos4.t[:, 0:Wt], cos4.t[:, 0:Wt], rq.t[:, 0:Wt], ALU.mult, [rq.k], [cos4.k])
                tt("pool", sin4.t[:, 0:Wt], sin4.t[:, 0:Wt], rq.t[:, 0:Wt], ALU.mult, [rq.k], [sin4.k])
            lst = latst[st["lat"] % 2]
            lb = latbf[st["lat"] % 2]
            st["lat"] += 1
            cp("act", lst.t[:SW, 0:288], g.t[:SW, 0:288], [], [g.k, lst.k])
            lbs.append((s, lb, lst))
        wdone(ch["ckv"])

        if kind == "p" and ti == 0:
            memset("pool", ubuf.t[:, :, 0:HIST], 0.0, [ubuf.k])
        if last:
            ba, bb = gbank(), gbank()
            held.extend([ba, bb])
        for half in range(2):
            wa, ka = wuse(ch["u"][2 * half])
            wb_, kb = wuse(ch["u"][2 * half + 1])
            for blk in range(2):
                cc = 2 * half + blk
                g1, g2 = gbank(), gbank()
                for kc in range(8):
                    mm(g1.t[:, 0:Wt], wa[:, kc, blk * 128:blk * 128 + 128], HT.t[:, kc, 0:Wt], kc == 0, kc == 7, [ka, HT.k], [g1.k])
                for kc in range(8):
                    mm(g2.t[:, 0:Wt], wb_[:, kc, blk * 128:blk * 128 + 128], HT.t[:, kc, 0:Wt], kc == 0, kc == 7, [kb, HT.k], [g2.k])
                tm = gtmp()
                act(tm.t[:, 0:Wt], g2.t[:, 0:Wt], AF.Sigmoid, [], [g2.k, tm.k])
                if kind == "p":
                    tt("dve", ubuf.t[:, cc, HIST:HIST + Wt], g1.t[:, 0:Wt], tm.t[:, 0:Wt], ALU.mult, [tm.k], [g1.k, ubuf.k])
                else:
                    for b in range(NB):
                        tt("dve", ubuf.t[:, cc, b * USEG + HIST:b * USEG + HIST + DEC], g1.t[:, b * DEC:(b + 1) * DEC],
                           tm.t[:, b * DEC:(b + 1) * DEC], ALU.mult, [tm.k], [g1.k, ubuf.k])
            if last:
                sl = slice((NS - 1) * SW, NS * SW)
                for kc in range(8):
                    mm(ba.t[:SW, half * 256:half * 256 + 256], HT.t[:, kc, sl], wa[:, kc, :], kc == 0, kc == 7, [ka, HT.k], [ba.k])
                for kc in range(8):
                    mm(bb.t[:SW, half * 256:half * 256 + 256], HT.t[:, kc, sl], wb_[:, kc, :], kc == 0, kc == 7, [kb, HT.k], [bb.k])
            wdone(ch["u"][2 * half], ch["u"][2 * half + 1])
        if last:
            tm, tm2 = gtmp(), gtmp()
            act(tm.t[:SW, :], bb.t[:SW, :], AF.Sigmoid, [], [bb.k, tm.k])
            tt("dve", tm2.t[:SW, :], ba.t[:SW, :], tm.t[:SW, :], ALU.mult, [tm.k], [ba.k, tm2.k])
            held.remove(ba)
            held.remove(bb)
            if kind == "p":
                dma("sp", cv_p[seq, :, :], tm2.t[SW - HIST:SW, :], [tm2.k], [], tm2.k)
            else:
                for b in range(NB):
                    dma("sp", cv_s[b, HIST - DEC:HIST, :], tm2.t[b * DEC:(b + 1) * DEC, :], [tm2.k], [], tm2.k)
                    dma("sp", cv_s[b, 0:HIST - DEC, :], ccv[b, DEC:HIST, :], [], [], tm2.k)

        ckc = 2 * (st["ssc"] % 16)
        st["ssc"] += 1

        def ckv_a():
            memset("pool", ss.t[:SW, ckc:ckc + NS], 0.0, [ss.k])
            for s, lb, lst in lbs:
                act(lb.t[:SW, 0:256], lst.t[:SW, 0:256], AF.Square, [ss.k, lst.k], [lb.k, ss.k], scale=1.0 / 16.0, accum_out=ss.t[:SW, ckc + s:ckc + s + 1])
            act(rs.t[:SW, ckc:ckc + NS], ss.t[:SW, ckc:ckc + NS], AF.Sqrt, [ss.k, CONST], [rs.k], bias=eps6.t[:SW, 0:1], scale=1.0)

        def ckv_b():
            recip(rs.t[:SW, ckc:ckc + NS], rs.t[:SW, ckc:ckc + NS], [], [rs.k])
            for s, lb, lst in lbs:
                stt("dve", lst.t[:SW, 0:256], lst.t[:SW, 0:256], rs.t[:SW, ckc + s:ckc + s + 1], gkv.t[:SW, :], ALU.mult, ALU.mult, [rs.k, CONST], [lst.k])
                if kind == "p":
                    blk = (t0 + s * SW) // 128
                    cs_, sn_ = cosk.t[:SW, blk * 16:blk * 16 + 16], sink.t[:SW, blk * 16:blk * 16 + 16]
                else:
                    cs_, sn_ = cosks.t[:SW, :], sinks.t[:SW, :]
                x1, x2 = lst.t[:SW, 256:272], lst.t[:SW, 272:288]
                rt = ropet
                tt("dve", rt.t[:SW, 0:16], x1, cs_, ALU.mult, [CONST, lst.k], [rt.k])
                tt("dve", rt.t[:SW, 16:32], x2, sn_, ALU.mult, [CONST, lst.k], [rt.k])
                tt("dve", rt.t[:SW, 32:48], x2, cs_, ALU.mult, [CONST, lst.k], [rt.k])
                tt("dve", rt.t[:SW, 48:64], x1, sn_, ALU.mult, [CONST, lst.k], [rt.k])
                tt("dve", lst.t[:SW, 256:272], rt.t[:SW, 0:16], rt.t[:SW, 16:32], ALU.subtract, [rt.k], [lst.k])
                tt("dve", lst.t[:SW, 272:288], rt.t[:SW, 32:48], rt.t[:SW, 48:64], ALU.add, [rt.k], [lst.k])

        def ckv_c():
            for s, lb, lst in lbs:
                r0 = row0 + s * SW
                st["stores"].append((latout[r0:r0 + SW, :], lst.t[:SW, 0:256], lst.k))
                st["stores"].append((krout[r0:r0 + SW, :], lst.t[:SW, 256:288], lst.k))
                cp("pool", lb.t[:SW, 0:288], lst.t[:SW, 0:288], [lst.k], [lb.k])
                for rr in range(4):
                    cp("pool", lb.t[:SW, 288 + 32 * rr:320 + 32 * rr], lst.t[:SW, 256:288], [lst.k], [lb.k])

        fin_prev = st["finish_prev"]
        st["finish_prev"] = None
        ckv_a()
        if fin_prev is not None:
            fin_prev[0]()
        ckv_b()
        if fin_prev is not None:
            fin_prev[1]()
        ckv_c()

        if kind == "s":
            ccs = hidT[0]
            for b in range(NB):
                dma("pool", ccs.t[:HIST, 2 * b:2 * b + 2, :], ccv[b, :, :].rearrange("r (a c) -> r a c", a=2), [], [ccs.k], ccs.k)
            for b in range(NB):
                tb = tbank()
                for cc in range(4):
                    tr(tb.t[:, cc, 0:HIST], ccs.t[:HIST, 2 * b + cc // 2, (cc % 2) * 128:(cc % 2) * 128 + 128], [ccs.k], [tb.k])
                cp("dve", ubuf.t[:, :, b * USEG:b * USEG + HIST], tb.t[:, 0:4, 0:HIST], [], [tb.k, ubuf.k])
        dwf = [gtmp(), gtmp()]
        held_tmp.extend(dwf)
        for cc in range(4):
            wv, wk = wuse(ch["cd"][cc])
            db = gbank()
            o0 = 0
            df = dwf[cc // 2].t[:, (cc % 2) * W:(cc % 2) * W + Wt]
            if kind == "p":
                for j in range(CW):
                    mm(db.t[:, o0:o0 + Wt], wv[:, j * 128:(j + 1) * 128], ubuf.t[:, cc, j:j + Wt], j == 0, j == CW - 1, [wk, ubuf.k], [db.k])
            else:
                for b in range(NB):
                    for j in range(CW):
                        mm(db.t[:, o0 + b * DEC:o0 + (b + 1) * DEC], wv[:, j * 128:(j + 1) * 128], ubuf.t[:, cc, b * USEG + j:b * USEG + j + DEC],
                           j == 0, j == CW - 1, [wk, ubuf.k], [db.k])
            wdone(ch["cd"][cc])
            act(df, db.t[:, o0:o0 + Wt], AF.Identity, [CONST], [db.k, dwf[cc // 2].k], bias=vec.t[:, CB + cc:CB + cc + 1], scale=1.0)
            cp("pool", dwbf.t[:, cc, 0:Wt], df, [dwf[cc // 2].k], [dwbf.k])
            tt("pool", dsq.t[:, cc, 0:Wt], df, df, ALU.mult, [dwf[cc // 2].k], [dsq.k])
        if kind == "p":
            cp("pool", ubuf.t[:, :, 0:HIST], ubuf.t[:, :, Wt:Wt + HIST], [], [ubuf.k])

        for h in range(NH):
            wv, wk = wuse(ch["qn"][h // 4])
            g = gbank()
            for kc in range(6):
                mm(g.t[:, 0:Wt], wv[:, kc, (h % 4) * 128:(h % 4) * 128 + 128], cqT.t[:, kc, 0:Wt], kc == 0, kc == 5, [wk, cqT.k], [g.k])
            tt("dve", QT.t[:, h, 0:Wt], g.t[:, 0:Wt], rq.t[:, 0:Wt], ALU.mult, [rq.k], [g.k, QT.k])
            if h % 4 == 3:
                wdone(ch["qn"][h // 4])

        mb = gbank()
        held.append(mb)
        for cc in range(4):
            mm(mb.t[:, 0:Wt], ones512.t[:, :], dwbf.t[:, cc, 0:Wt], cc == 0, cc == 3, [CONST, dwbf.k], [mb.k])
        for cc in range(4):
            mm(mb.t[:, W:W + Wt], ones512.t[:, :], dsq.t[:, cc, 0:Wt], cc == 0, cc == 3, [CONST, dsq.k], [mb.k])

        wv, wk = wuse(ch["qr"])
        for g4 in range(2):
            g1, g2 = gbank(), gbank()
            for kc in range(6):
                mm(g1.t[:, 0:Wt], wv[:, kc, g4 * 128:g4 * 128 + 128], cqT.t[:, kc, 0:Wt], kc == 0, kc == 5, [wk, cqT.k], [g1.k])
            for kc in range(6):
                mm(g2.t[:, 0:Wt], wv[:, kc, 256 + g4 * 128:256 + g4 * 128 + 128], cqT.t[:, kc, 0:Wt], kc == 0, kc == 5, [wk, cqT.k], [g2.k])
            t1, t2 = gtmp(), gtmp()
            tt("dve", t1.t[:, 0:Wt], g1.t[:, 0:Wt], cos4.t[:, 0:Wt], ALU.mult, [cos4.k], [g1.k, t1.k])
            tt("dve", t2.t[:, 0:Wt], g2.t[:, 0:Wt], sin4.t[:, 0:Wt], ALU.mult, [sin4.k], [g2.k, t2.k])
            tt("pool", t1.t[:, 0:Wt], t1.t[:, 0:Wt], t2.t[:, 0:Wt], ALU.add, [t2.k], [t1.k])
            for j in range(4):
                if j % 2:
                    ts("dve", QrT.t[:, 4 * g4 + j, 0:Wt], t1.t[:, 0:Wt], vec.t[:, MK + j:MK + j + 1], ALU.mult, [CONST, t1.k], [QrT.k])
                else:
                    act(QrT.t[:, 4 * g4 + j, 0:Wt], t1.t[:, 0:Wt], AF.Copy, [CONST, t1.k], [QrT.k], scale=vec.t[:, MK + j:MK + j + 1])
        wdone(ch["qr"])

        for s_, lb, _ in lbs:
            if kind == "p":
                kv_build(lb, SW, latT.t[:, 0:2, s_ * SW:(s_ + 1) * SW], latT.k, krT.t[:, t0 + s_ * SW:t0 + (s_ + 1) * SW], krT.k)
            else:
                kv_build(lb, SW, latTs.t[:, 0:2, 0:SW], latTs.k, krTs.t[:, 0:SW], krTs.k)
        if kind == "p":
            kv_project(Wt, t0, t0 // 128, [(0, 128), (128, 128)])

        cp("act", mean.t[:, 0:Wt], mb.t[:, 0:Wt], [], [mb.k, mean.k])
        tt("pool", var.t[:, 0:Wt], mean.t[:, 0:Wt], mean.t[:, 0:Wt], ALU.mult, [mean.k], [var.k])
        tt("dve", var.t[:, 0:Wt], mb.t[:, W:W + Wt], var.t[:, 0:Wt], ALU.subtract, [], [mb.k, var.k])
        held.remove(mb)
        act(lrs.t[:, 0:Wt], var.t[:, 0:Wt], AF.Sqrt, [var.k, CONST], [lrs.k], bias=eps5.t[:, 0:1], scale=1.0)
        recip(lrs.t[:, 0:Wt], lrs.t[:, 0:Wt], [], [lrs.k])

        def ln_apply_a():
            for cc in range(4):
                df = dwf[cc // 2].t[:, (cc % 2) * W:(cc % 2) * W + Wt]
                tt("dve", df, df, mean.t[:, 0:Wt], ALU.subtract, [mean.k], [dwf[cc // 2].k])
                tt("pool", df, df, lrs.t[:, 0:Wt], ALU.mult, [lrs.k], [dwf[cc // 2].k])

        def ln_apply_b():
            for cc in range(4):
                df = dwf[cc // 2].t[:, (cc % 2) * W:(cc % 2) * W + Wt]
                act(sT.t[:, cc, 0:Wt], df, AF.Silu, [dwf[cc // 2].k, CONST], [sT.k], scale=vec.t[:, LG + cc:LG + cc + 1], bias=vec.t[:, LB + cc:LB + cc + 1])
            held_tmp.remove(dwf[0])
            held_tmp.remove(dwf[1])

        if kind == "p":
            nkb = (t0 + Wt) // 128
            blocks = []
            for kb in range(nkb):
                c0 = max(0, kb * 128 - t0)
                blocks.append((kb * 128, 128, c0, kb * 128 >= t0))
            pend = None
            for h in range(NH):
                if h == 2:
                    pend = (lambda p: (lambda: (p(), ln_apply_a())))(pend)
                pend = attention(h, 0, Wt, blocks, hook=pend)
            pend()
            ln_apply_b()
        else:
            ln_apply_a()
            ln_apply_b()
            for b in range(NB):
                for blk in range(PAST // 128):
                    lb = latbf[st["lat"] % 2]
                    st["lat"] += 1
                    r0 = b * PAST + blk * 128
                    dma("sp", lb.t[:, 0:256], clat_b[r0:r0 + 128, :], [ctok], [lb.k], lb.k)
                    dma("sp", lb.t[:, 256:288], ckr_b[r0:r0 + 128, :], [ctok], [lb.k], lb.k)
                    for rr in range(4):
                        cp("pool", lb.t[:, 288 + 32 * rr:320 + 32 * rr], lb.t[:, 256:288], [], [lb.k])
                    half = blk % 2
                    kv_build(lb, 128, latT.t[:, 0:2, half * 128:(half + 1) * 128], latT.k, krT.t[:, blk * 128:(blk + 1) * 128], krT.k)
                    if half == 1:
                        kv_project(256, (blk - 1) * 128, blk - 1, [(0, 128), (128, 128)])
                cp("pool", latT.t[:, 0:2, 0:DEC], latTs.t[:, 0:2, b * DEC:(b + 1) * DEC], [latTs.k], [latT.k])
                cp("pool", krT.t[:, PAST:PAST + DEC], krTs.t[:, b * DEC:(b + 1) * DEC], [krTs.k], [krT.k])
                kv_project(DEC, PAST, PAST // 128, [(0, DEC)])
                blocks = [(kb * 128, 128, 0, False) for kb in range(PAST // 128)] + [(PAST, DEC, 0, False)]
                pend = None
                for h in range(NH):
                    pend = attention(h, b * DEC, DEC, blocks, hook=pend)
                pend()

        dma("sp", grep.t[:, :], g_ffn.partition_broadcast(128), [], [grep.k], grep.k)

        for mp in range(4):
            gA = [gbank(), gbank()]
            held.extend(gA)
            gG = [gbank(), gbank()]
            held.extend(gG)
            wa, ka = wuse(ch["mrg"][mp][0])
            for blk in range(2):
                bs = slice(blk * 128, blk * 128 + 128)
                for h in range(NH):
                    mm(gA[blk].t[:, 0:Wt], wa[:, h, bs], oT.t[0:64, h, 0:Wt], h == 0, h == NH - 1, [ka, oT.k], [gA[blk].k])
            wdone(ch["mrg"][mp][0])
            wc, kc_ = wuse(ch["mrg"][mp][1])
            for blk in range(2):
                bs = slice(blk * 128, blk * 128 + 128)
                for cc in range(4):
                    mm(gA[blk].t[:, W:W + Wt], wc[:, cc, bs], sT.t[:, cc, 0:Wt], cc == 0, cc == 3, [kc_, sT.k], [gA[blk].k])
            wdone(ch["mrg"][mp][1])
            wgg, kgg = wuse(ch["mrg"][mp][2])
            for blk in range(2):
                bs = slice(blk * 128, blk * 128 + 128)
                for kc in range(8):
                    mm(gG[blk].t[:, 0:Wt], wgg[:, kc, 0, bs], HT.t[:, kc, 0:Wt], kc == 0, kc == 7, [kgg, HT.k], [gG[blk].k])
                for kc in range(8):
                    mm(gG[blk].t[:, W:W + Wt], wgg[:, kc, 1, bs], HT.t[:, kc, 0:Wt], kc == 0, kc == 7, [kgg, HT.k], [gG[blk].k])
            wdone(ch["mrg"][mp][2])
            for blk in range(2):
                m = 2 * mp + blk
                t1, t2 = gtmp(), gtmp()
                for o0 in (0, W):
                    act(t1.t[:, o0:o0 + Wt], gG[blk].t[:, o0:o0 + Wt], AF.Sigmoid, [], [gG[blk].k, t1.k])
                    tt("dve", t2.t[:, o0:o0 + Wt], gA[blk].t[:, o0:o0 + Wt], t1.t[:, o0:o0 + Wt], ALU.mult, [t1.k], [gA[blk].k, t2.k])
                tt("pool", mrgT.t[:, m, 0:Wt], t2.t[:, 0:Wt], t2.t[:, W:W + Wt], ALU.add, [t2.k], [mrgT.k])
            for b_ in gA + gG:
                held.remove(b_)

        wo = [wuse(c_) for c_ in ch["wo"]]
        for s in range(NS):
            for n in range(2):
                wv, wk = wo[n]
                g = gbank()
                for kc in range(8):
                    mm(g.t[:SW, :], mrgT.t[:, kc, s * SW:(s + 1) * SW], wv[:, kc, :], kc == 0, kc == 7, [wk, mrgT.k], [g.k])
                xs_ = xres[s].t[:SW, n * 512:(n + 1) * 512]
                tt("dve", xs_, xs_, g.t[:SW, :], ALU.add, [], [g.k, xres[s].k])
        wdone(*ch["wo"])

        flush_stores()
        if idx + 1 < len(tiles):
            load_x(idx + 1)

        HT.b = norm_to_HT(xres, None, NS, SW, batched=False)

        def ffn_up(q):
            hb = hidT[(q + 1) % 2]
            for c in range(4):
                wv, wk = wuse(ch["up"][q][c])
                for blk in range(2):
                    hc = 2 * c + blk
                    g = gbank()
                    for kc in range(8):
                        mm(g.t[:, 0:Wt], wv[:, kc, blk * 128:blk * 128 + 128], HT.t[:, kc, 0:Wt], kc == 0, kc == 7, [wk, HT.k], [g.k])
                    t1 = gtmp()
                    act(t1.t[:, 0:Wt], g.t[:, 0:Wt], AF.Relu, [], [g.k, t1.k])
                    tt("pool", hb.t[:, hc, 0:Wt], t1.t[:, 0:Wt], t1.t[:, 0:Wt], ALU.mult, [t1.k], [hb.k])
                wdone(ch["up"][q][c])

        def ffn_dn(q):
            hb = hidT[(q + 1) % 2]
            dns = ch["dn"][q]
            if q < 3:
                for n in range(2):
                    wv, wk = wuse(dns[n])
                    for s in range(NS):
                        g = gbank()
                        for kc in range(8):
                            mm(g.t[:SW, :], hb.t[:, kc, s * SW:(s + 1) * SW], wv[:, kc, :], kc == 0, kc == 7, [wk, hb.k], [g.k])
                        xs_ = xres[s].t[:SW, n * 512:(n + 1) * 512]
                        tt("dve", xs_, xs_, g.t[:SW, :], ALU.add, [], [g.k, xres[s].k])
                    wdone(dns[n])
            else:
                dn = [wuse(c_) for c_ in dns]
                for s in range(NS):
                    for n in range(2):
                        wv, wk = dn[n]
                        g = gbank()
                        for kc in range(8):
                            mm(g.t[:SW, :], hb.t[:, kc, s * SW:(s + 1) * SW], wv[:, kc, :], kc == 0, kc == 7, [wk, hb.k], [g.k])
                        xs_ = xres[s].t[:SW, n * 512:(n + 1) * 512]
                        tt("dve", xs_, xs_, g.t[:SW, :], ALU.add, [], [g.k, xres[s].k])
                wdone(*dns)

        nxt = idx + 1 < len(tiles)
        for i_, (kind_, q) in enumerate(FFN_ORDER):
            (ffn_up if kind_ == "u" else ffn_dn)(q)
            if i_ == 1:
                dma("sp", grep.t[:, :], (g_mix if nxt else g_ple).partition_broadcast(128), [], [grep.k], grep.k)
            if i_ == 3 and nxt:
                gn = tile_geom(idx + 1)
                carry_ = norm_to_HT(xsets[(idx + 1) % 2], None, gn["NS"], gn["SW"], phase=1)
            if i_ == 5 and nxt:
                st["pending_HT"] = norm_to_HT(xsets[(idx + 1) % 2], g_ple, gn["NS"], gn["SW"], phase=2, carry=carry_)

        for s in range(NS):
            tb = tbank()
            for kc in range(2):
                tr(tb.t[:, kc, 0:SW], pbf.t[:SW, s, kc * 128:(kc + 1) * 128], [pbf.k], [tb.k])
            cp("dve", ppT.t[:, 0:2, s * SW:(s + 1) * SW], tb.t[:, 0:2, 0:SW], [], [tb.k, ppT.k])
        HT.b = norm_to_HT(xres, None if nxt else g_fin, NS, SW, batched=False)
        for n in range(2):
            wg_, kg = wuse(ch["ple"][n][0])
            wp_, kp = wuse(ch["ple"][n][1])
            for s in range(NS):
                g1, g2 = gbank(), gbank()
                for kc in range(8):
                    mm(g1.t[:SW, :], HT.t[:, kc, s * SW:(s + 1) * SW], wg_[:, kc, :], kc == 0, kc == 7, [kg, HT.k], [g1.k])
                for kc in range(2):
                    mm(g2.t[:SW, :], ppT.t[:, kc, s * SW:(s + 1) * SW], wp_[:, kc, :], kc == 0, kc == 1, [kp, ppT.k], [g2.k])
                t1 = gtmp()
                act(t1.t[:SW, :], g1.t[:SW, :], AF.Sigmoid, [], [g1.k, t1.k])
                tt("dve", t1.t[:SW, :], t1.t[:SW, :], g2.t[:SW, :], ALU.mult, [], [g2.k, t1.k])
                xs_ = xres[s].t[:SW, n * 512:(n + 1) * 512]
                tt("pool", xs_, xs_, t1.t[:SW, :], ALU.add, [t1.k], [xres[s].k])
            wdone(*ch["ple"][n])

        fc = 2 * (st["ssc"] % 16)
        st["ssc"] += 1

        def finish_a():
            memset("pool", ss.t[:SW, fc:fc + NS], 0.0, [ss.k])
            for s in range(NS):
                hbf = hbfs[st["hb"] % 2]
                st["hb"] += 1
                act(hbf.t[:SW, :], xres[s].t[:SW, :], AF.Square, [xres[s].k, ss.k], [hbf.k, ss.k], scale=1.0 / 32.0, accum_out=ss.t[:SW, fc + s:fc + s + 1])
            act(rs.t[:SW, fc:fc + NS], ss.t[:SW, fc:fc + NS], AF.Sqrt, [ss.k, CONST], [rs.k], bias=eps6.t[:SW, 0:1], scale=1.0)

        def finish_b():
            recip(rs.t[:SW, fc:fc + NS], rs.t[:SW, fc:fc + NS], [], [rs.k])
            for s in range(NS):
                stt("dve", xres[s].t[:SW, :], xres[s].t[:SW, :], rs.t[:SW, fc + s:fc + s + 1], grep.t[:SW, :], ALU.mult, ALU.mult, [rs.k, grep.k], [xres[s].k])
                r0 = row0 + s * SW
                st["stores"].append((yout[r0:r0 + SW, :], xres[s].t[:SW, :], xres[s].k))
        st["finish_prev"] = (finish_a, finish_b)

    latTs = sb("latTs", [128, 2, 64], BF16)

    chs = [decl_chunks() for _ in tiles]
    dma("sp", grep.t[:, :], g_mix.partition_broadcast(128), [], [grep.k], grep.k)
    load_x(0)
    for idx in range(len(tiles)):
        run_tile(idx)
    st["finish_prev"][0]()
    st["finish_prev"][1]()
    flush_stores()
    P.emit()
    return nc


def _rope_tables(T, past, dec, nb):
    half = 16
    inv = 10000.0 ** (-np.arange(half, dtype=np.float64) / half)
    pos = np.arange(T, dtype=np.float64)
    ang = pos[:, None] * inv[None, :]
    cos_t, sin_t = np.cos(ang), np.sin(ang)
    nblk = T // 128
    cosk = cos_t.reshape(nblk, 128, 16).transpose(1, 0, 2).reshape(128, nblk * 16)
    sink = sin_t.reshape(nblk, 128, 16).transpose(1, 0, 2).reshape(128, nblk * 16)
    c4 = np.tile(np.concatenate([cos_t.T, cos_t.T], 0), (4, 1))
    s4 = np.tile(np.concatenate([-sin_t.T, sin_t.T], 0), (4, 1))
    poss = past + np.arange(dec, dtype=np.float64)
    angs = poss[:, None] * inv[None, :]
    cs, sn = np.cos(angs), np.sin(angs)
    cosks = np.tile(cs, (nb, 1)); sinks = np.tile(sn, (nb, 1))
    c4s = np.tile(np.concatenate([cosks.T, cosks.T], 0), (4, 1))
    s4s = np.tile(np.concatenate([-sinks.T, sinks.T], 0), (4, 1))
    f = lambda a: np.ascontiguousarray(a, dtype=np.float32)
    return dict(cosk=f(cosk), sink=f(sink), cos4=f(c4), sin4=f(s4), cosks=f(cosks), sinks=f(sinks), cos4s=f(c4s), sin4s=f(s4s))


def _layout_weights(inp):
    f = lambda a: np.ascontiguousarray(a, dtype=np.float32)
    w_uq = np.asarray(inp["w_uq"])[0]
    wq_n = np.zeros((QL, NH, 128), np.float32)
    for h in range(NH):
        o = 0 if h % 2 == 0 else 64
        wq_n[:, h, o:o + 64] = w_uq[:, h, 0:64]
    wq_r = np.zeros((QL, 2, NH, 32), np.float32)
    wq_r[:, 0] = w_uq[:, :, 64:96]
    wq_r[:, 1, :, 0:16] = w_uq[:, :, 80:96]
    wq_r[:, 1, :, 16:32] = w_uq[:, :, 64:80]
    conv_w = np.asarray(inp["conv_w"])[0]
    cdiag = np.zeros((4, 128, CW, 128), np.float32)
    idx = np.arange(128)
    for cc in range(4):
        for j in range(CW):
            cdiag[cc, idx, j, idx] = conv_w[j, cc * 128:(cc + 1) * 128]
    vecs = np.zeros((128, 24), np.float32)
    vecs[:, 0:6] = np.asarray(inp["q_norm_g"])[0].reshape(6, 128).T
    vecs[:, 6:10] = np.asarray(inp["conv_b"])[0].reshape(4, 128).T
    vecs[:, 10:14] = np.asarray(inp["conv_ln_g"])[0].reshape(4, 128).T
    vecs[:, 14:18] = np.asarray(inp["conv_ln_b"])[0].reshape(4, 128).T
    for j in range(4):
        vecs[32 * j:32 * j + 32, 18 + j] = 1.0
    return dict(
        w_in=f(np.asarray(inp["w_in"])[0]), wq_n=f(wq_n.reshape(QL, 1024)), wq_r=f(wq_r.reshape(QL, 512)),
        w_uk=f(np.asarray(inp["w_uk"])[0].reshape(KVL, 512)), w_uv=f(np.asarray(inp["w_uv"])[0].reshape(KVL, 512)),
        w_ao=f(np.asarray(inp["w_attn_out"])[0]), cdiag=f(cdiag.reshape(4 * 128, CW * 128)), w_co=f(np.asarray(inp["w_conv_out"])[0]),
        w_out=f(np.asarray(inp["w_out"])[0]), w_up=f(np.asarray(inp["w_ff_up"])[0]), w_dn=f(np.asarray(inp["w_ff_down"])[0]),
        w_pg=f(np.asarray(inp["w_ple_gate"])[0]), w_pp=f(np.asarray(inp["w_ple_proj"])[0]),
        g_mix=f(np.asarray(inp["norm_mix_g"])[0]), g_ffn=f(np.asarray(inp["norm_ffn_g"])[0]), g_ple=f(np.asarray(inp["ple_norm_g"])[0]),
        g_fin=f(np.asarray(inp["final_norm_g"])), g_kv=f(np.asarray(inp["kv_norm_g"])[0]), vecs=vecs)


_PROG_CACHE = {}


def run_cores(inp, n_cores):
    x_prompt = np.asarray(inp["x_prompt"]); x_sample = np.asarray(inp["x_sample"])
    BATCH, T, _ = x_prompt.shape
    DECB, DEC, _ = x_sample.shape
    PAST = np.asarray(inp["cache_kv_latent"]).shape[2]
    NSEQ = BATCH // n_cores
    NB = DECB // n_cores
    key = (T, NSEQ, NB, PAST, DEC)
    if key not in _PROG_CACHE:
        _PROG_CACHE[key] = build_program(*key)
    nc = _PROG_CACHE[key]
    shared = _layout_weights(inp)
    shared.update(_rope_tables(T, PAST, DEC, NB))
    f = lambda a: np.ascontiguousarray(a, dtype=np.float32)
    p_prompt = np.asarray(inp["p_prompt"])[0]; p_sample = np.asarray(inp["p_sample"])[0]
    clat = np.asarray(inp["cache_kv_latent"])[0]; ckr = np.asarray(inp["cache_k_rope"])[0]; ccv = np.asarray(inp["cache_conv"])[0]
    in_maps = []
    for c in range(n_cores):
        sq = slice(c * NSEQ, (c + 1) * NSEQ)
        bq = slice(c * NB, (c + 1) * NB)
        m = dict(shared)
        m.update(xp=f(x_prompt[sq].reshape(NSEQ * T, D)), xs=f(x_sample[bq].reshape(NB * DEC, D)),
                 pp=f(p_prompt[sq].reshape(NSEQ * T, PLE)), psd=f(p_sample[bq].reshape(NB * DEC, PLE)),
                 clat=f(clat[bq].reshape(NB * PAST, KVL)), ckr=f(ckr[bq].reshape(NB * PAST, 32)), ccv=f(ccv[bq]))
        in_maps.append(m)
    res = run_bass_kernel_spmd(nc, in_maps, core_ids=list(range(n_cores)))
    R = res.results
    cat = lambda k, shp: np.concatenate([np.asarray(r[k], dtype=np.float32).reshape(shp) for r in R], axis=0)
    y_p = cat("y_p", (NSEQ, T, D)); y_s = cat("y_s", (NB, DEC, D))
    lat_p = cat("lat_p", (NSEQ, T, KVL))[None]; kr_p = cat("kr_p", (NSEQ, T, 32))[None]; cv_p = cat("cv_p", (NSEQ, HIST, CC))[None]
    lat_s = cat("lat_s", (NB, DEC, KVL))[None]; kr_s = cat("kr_s", (NB, DEC, 32))[None]; cv_s = cat("cv_s", (NB, HIST, CC))[None]
    return (y_p, y_s, lat_p, kr_p, cv_p, lat_s, kr_s, cv_s)


def kernel(**inputs):
    return run_cores(inputs, NCORES)
```
